# Optimizing a Trainium2 kernel written in Bass

```python
import jax, jax.numpy as jnp
from jax import lax
import numpy as np

D_MODEL = 2048
BATCH = 2
SEQ = 8192
DEPTH = 1

D_MIX = D_MODEL
D_LRU = D_MIX // 2
D_CONV = D_MIX - D_LRU
LRU_HEADS = 16
LRU_HEAD_DIM = D_LRU // LRU_HEADS
LRU_CONV_WIDTH = 4
LRU_C = 8.0
CONF_CONV_WIDTH = 31
D_FF = 4 * D_MODEL
D_IN = 2 * D_LRU + 2 * D_CONV
EPS = 1e-6

kernel_name = "hymba_style_rglru_conformer_conv_hybrid"


def rms_norm(x, g):
    xf = x.astype(jnp.float32)
    y = xf * lax.rsqrt(jnp.mean(xf * xf, axis=-1, keepdims=True) + EPS)
    return (y * g.astype(jnp.float32)).astype(x.dtype)


def layer_norm(x, g, b):
    xf = x.astype(jnp.float32)
    mu = jnp.mean(xf, axis=-1, keepdims=True)
    xc = xf - mu
    var = jnp.mean(xc * xc, axis=-1, keepdims=True)
    y = xc * lax.rsqrt(var + EPS)
    return (y * g.astype(jnp.float32) + b.astype(jnp.float32)).astype(x.dtype)


def causal_depthwise_conv(x, w, b):
    k = w.shape[0]
    y = lax.conv_general_dilated(
        x, w[:, None, :].astype(x.dtype), window_strides=(1,), padding=[(k - 1, 0)],
        dimension_numbers=("NWC", "WIO", "NWC"), feature_group_count=x.shape[-1])
    return y + b.astype(x.dtype)


def _linear_recurrence_combine(left, right):
    a_l, b_l = left
    a_r, b_r = right
    return a_l * a_r, a_r * b_l + b_r


def rg_lru(xc, w_a, b_a, w_x, b_x, lam):
    bsz, s, c = xc.shape
    xh = xc.reshape(bsz, s, LRU_HEADS, LRU_HEAD_DIM)
    r = jax.nn.sigmoid(jnp.einsum("bshi,hij->bshj", xh, w_a).reshape(bsz, s, c) + b_a)
    i = jax.nn.sigmoid(jnp.einsum("bshi,hij->bshj", xh, w_x).reshape(bsz, s, c) + b_x)
    log_a = -LRU_C * r.astype(jnp.float32) * jax.nn.softplus(-lam.astype(jnp.float32))
    a = jnp.exp(log_a)
    mult = jnp.sqrt(-jnp.expm1(2.0 * log_a))
    u = mult * (i * xc).astype(jnp.float32)
    _, h = lax.associative_scan(_linear_recurrence_combine, (a, u), axis=1)
    return h.astype(xc.dtype)


def setup_inputs(seed: int = 0) -> dict:
    key = jax.random.key(seed)
    ks = jax.random.split(key, 24)
    f32 = jnp.float32

    def nrm(k, shape, scale):
        return jax.random.normal(k, shape, f32) * scale

    a_c = jax.random.uniform(ks[9], (DEPTH, D_LRU), f32, 0.9, 0.999)
    s = a_c ** (1.0 / LRU_C)
    lru_lambda = jnp.log(s) - jnp.log1p(-s)

    return {
        "x": nrm(ks[0], (BATCH, SEQ, D_MODEL), 1.0),
        "mix_norm_g": 1.0 + nrm(ks[1], (DEPTH, D_MODEL), 0.02),
        "w_in": nrm(ks[2], (DEPTH, D_MODEL, D_IN), D_MODEL ** -0.5),
        "lru_conv_w": nrm(ks[3], (DEPTH, LRU_CONV_WIDTH, D_LRU), LRU_CONV_WIDTH ** -0.5),
        "lru_conv_b": nrm(ks[4], (DEPTH, D_LRU), 0.01),
        "lru_gate_a_w": nrm(ks[5], (DEPTH, LRU_HEADS, LRU_HEAD_DIM, LRU_HEAD_DIM), LRU_HEAD_DIM ** -0.5),
        "lru_gate_a_b": nrm(ks[6], (DEPTH, D_LRU), 0.01),
        "lru_gate_x_w": nrm(ks[7], (DEPTH, LRU_HEADS, LRU_HEAD_DIM, LRU_HEAD_DIM), LRU_HEAD_DIM ** -0.5),
        "lru_gate_x_b": nrm(ks[8], (DEPTH, D_LRU), 0.01),
        "lru_lambda": lru_lambda,
        "conf_dw_w": nrm(ks[10], (DEPTH, CONF_CONV_WIDTH, D_CONV), CONF_CONV_WIDTH ** -0.5),
        "conf_dw_b": nrm(ks[11], (DEPTH, D_CONV), 0.01),
        "conf_ln_g": 1.0 + nrm(ks[12], (DEPTH, D_CONV), 0.02),
        "conf_ln_b": nrm(ks[13], (DEPTH, D_CONV), 0.01),
        "w_out": nrm(ks[14], (DEPTH, D_MIX, D_MODEL), D_MIX ** -0.5),
        "mlp_norm_g": 1.0 + nrm(ks[15], (DEPTH, D_MODEL), 0.02),
        "mlp_w1": nrm(ks[16], (DEPTH, D_MODEL, D_FF), D_MODEL ** -0.5),
        "mlp_w2": nrm(ks[17], (DEPTH, D_FF, D_MODEL), D_FF ** -0.5),
        "final_norm_g": 1.0 + nrm(ks[18], (D_MODEL,), 0.02),
    }


def reference(x, mix_norm_g, w_in, lru_conv_w, lru_conv_b, lru_gate_a_w, lru_gate_a_b,
              lru_gate_x_w, lru_gate_x_b, lru_lambda, conf_dw_w, conf_dw_b, conf_ln_g,
              conf_ln_b, w_out, mlp_norm_g, mlp_w1, mlp_w2, final_norm_g):
    h = x
    for l in range(DEPTH):
        xn = rms_norm(h, mix_norm_g[l])
        u = xn @ w_in[l]
        x_lru = u[..., :D_LRU]
        y_lru = u[..., D_LRU:2 * D_LRU]
        c_val = u[..., 2 * D_LRU:2 * D_LRU + D_CONV]
        c_gate = u[..., 2 * D_LRU + D_CONV:]

        xc = causal_depthwise_conv(x_lru, lru_conv_w[l], lru_conv_b[l])
        hr = rg_lru(xc, lru_gate_a_w[l], lru_gate_a_b[l], lru_gate_x_w[l], lru_gate_x_b[l],
                    lru_lambda[l])
        o_lru = hr * jax.nn.gelu(y_lru)

        g = c_val * jax.nn.sigmoid(c_gate)
        g = causal_depthwise_conv(g, conf_dw_w[l], conf_dw_b[l])
        g = layer_norm(g, conf_ln_g[l], conf_ln_b[l])
        o_conv = jax.nn.silu(g)

        o = jnp.concatenate([o_lru, o_conv], axis=-1) @ w_out[l]
        h = h + o

        z = rms_norm(h, mlp_norm_g[l]) @ mlp_w1[l]
        h = h + jnp.square(jax.nn.relu(z)) @ mlp_w2[l]
    return rms_norm(h, final_norm_g)
```

```python
import numpy as np
import concourse.bass as bass
import concourse.mybir as mybir
from concourse.bass_utils import run_bass_kernel_spmd

F32 = mybir.dt.float32
BF16 = mybir.dt.bfloat16
AF = mybir.ActivationFunctionType
ALU = mybir.AluOpType

D = 2048
DIN = 4096
DFF = 8192
TT = 512
HAL = 32
NCH = 8
EPS = 1e-6
CV_CB, CV_GAB, CV_GXB, CV_LAM, CV_DWB, CV_LNG, CV_LNB = 0, 8, 16, 24, 32, 40, 48
CV_C4W = 56
CV_FLAG = 88
CV_N = 104


class Sched:
    def __init__(self, nc):
        self.nc = nc
        self.ops = []
        self.last_w = {}
        self.readers = {}
        self.eng_objs = {"pe": nc.tensor, "act": nc.scalar, "dve": nc.vector, "pool": nc.gpsimd, "sp": nc.sync}

    def _deps(self, eng, reads, writes):
        deps = set()
        for k in reads:
            w = self.last_w.get(k)
            if w is not None:
                deps.add(w)
        for k in writes:
            w = self.last_w.get(k)
            if w is not None:
                deps.add(w)
            for r in self.readers.get(k, ()):
                deps.add(r)
        return deps

    def _add(self, eng, fn, reads, writes, kind, chan=None):
        oid = len(self.ops)
        deps = self._deps(eng, reads, writes)
        keep = set()
        raw = set(self.last_w.get(k) for k in reads if self.last_w.get(k) is not None)
        for d in deps:
            od = self.ops[d]
            if od["kind"] == "dma" or kind == "dma":
                keep.add(d)
            elif od["eng"] != eng:
                keep.add(d)
            elif d in raw and eng != "pe":
                keep.add(d)
        self.ops.append(dict(eng=eng, fn=fn, deps=keep, kind=kind, chan=chan, sig=False))
        for k in reads:
            self.readers.setdefault(k, []).append(oid)
        for k in writes:
            self.last_w[k] = oid
            self.readers[k] = []
        return oid

    def op(self, eng, fn, reads=(), writes=()):
        return self._add(eng, fn, list(reads), list(writes), "cmp")

    def dma(self, queue, fn, chan, reads=(), writes=()):
        return self._add(queue, fn, list(reads), list(writes), "dma", chan)

    def barrier(self):
        last = {}
        for i, o in enumerate(self.ops):
            if o["kind"] == "dma":
                last[("c", o["chan"])] = i
            else:
                last[("e", o["eng"])] = i
        deps = set(last.values())
        for eng in ["pe", "act", "dve", "pool", "sp"]:
            self.ops.append(dict(eng=eng, fn=lambda e: e.nop(), deps=set(deps), kind="cmp", chan=None, sig=False))

    def emit(self, block, sems, chan_sems):
        ops = self.ops
        for o in ops:
            for d in o["deps"]:
                ops[d]["sig"] = True
        cnt = {}
        for i, o in enumerate(ops):
            if o["kind"] == "dma":
                c = o["chan"]
                cnt[c] = cnt.get(c, 0) + 16
                o["ev"] = (("c", c), cnt[c])
            elif o["sig"]:
                e = o["eng"]
                cnt[e] = cnt.get(e, 0) + 1
                o["ev"] = (("e", e), cnt[e])
            else:
                o["ev"] = None
        const_total = {c: v for c, v in cnt.items() if isinstance(c, str) and c.startswith("const")}
        self.final_counts = cnt
        by_eng = {}
        for i, o in enumerate(ops):
            by_eng.setdefault(o["eng"], []).append(i)

        def semof(key):
            return chan_sems[key[1]] if key[0] == "c" else sems[key[1]]

        snap = {}
        nwaits = [0]

        def run_engine(ename, engobj):
            known = {}
            for i in by_eng.get(ename, []):
                o = ops[i]
                need = {}
                for d in o["deps"]:
                    key, val = ops[d]["ev"]
                    if key[0] == "c" and key[1] in const_total:
                        val = const_total[key[1]]
                    if need.get(key, 0) < val:
                        need[key] = val
                for key, val in sorted(need.items(), key=lambda kv: -kv[1]):
                    if known.get(key, 0) >= val:
                        continue
                    engobj.wait_ge(semof(key), val)
                    nwaits[0] += 1
                    known[key] = val
                    sn = snap.get((key, val))
                    if sn:
                        for k2, v2 in sn.items():
                            if known.get(k2, 0) < v2:
                                known[k2] = v2
                inst = o["fn"](engobj)
                if o["ev"] is not None:
                    key, val = o["ev"]
                    if o["kind"] == "dma":
                        inst.then_inc(semof(key), 16)
                    else:
                        inst.then_inc(semof(key), 1)
                        snap[(key, val)] = dict(known)
                        known[key] = max(known.get(key, 0), 0)

        @block.sync
        def _(e):
            run_engine("sp", e)

        @block.gpsimd
        def _(e):
            run_engine("pool", e)

        @block.scalar
        def _(e):
            run_engine("act", e)

        @block.vector
        def _(e):
            run_engine("dve", e)

        @block.tensor
        def _(e):
            run_engine("pe", e)


def build_program(NPRE, NFULL):
    nc = bass.Bass("TRN2", target_bir_lowering=False)
    NTILE = NPRE + NFULL
    xs_d = nc.dram_tensor("xs", [NTILE * TT, D], F32, kind="ExternalInput").ap()
    cv_d = nc.dram_tensor("cv", [128, CV_N], F32, kind="ExternalInput").ap()
    gng_d = nc.dram_tensor("gng", [128, 3, D], F32, kind="ExternalInput").ap()
    gw_d = nc.dram_tensor("gw", [128, 16, 128], F32, kind="ExternalInput").ap()
    dg_d = nc.dram_tensor("dg", [NCH, 128, 31 * 128], F32, kind="ExternalInput").ap()
    win_d = nc.dram_tensor("w_in", [D, DIN], F32, kind="ExternalInput").ap()
    wout_d = nc.dram_tensor("w_out", [D, D], F32, kind="ExternalInput").ap()
    w1_d = nc.dram_tensor("w1", [D, DFF], F32, kind="ExternalInput").ap()
    w2_d = nc.dram_tensor("w2", [DFF, D], F32, kind="ExternalInput").ap()
    out_d = nc.dram_tensor("out", [NFULL * TT, D], F32, kind="ExternalOutput").ap()

    S = Sched(nc)
    import contextlib
    es = contextlib.ExitStack()
    off = [16384]

    def sb(name, shape, dt, at=None):
        nbytes = int(np.prod(shape[1:])) * (4 if dt == F32 else 2)
        nbytes = (nbytes + 63) // 64 * 64
        if at is None:
            at = off[0]
            off[0] += nbytes
        return nc.alloc_sbuf_tensor_at(name, shape, dt, offset=at), at + nbytes

    def sbp(name, shape, dt):
        return sb(name, shape, dt)[0]

    def ps(name, shape, dt):
        return es.enter_context(nc.psum_tensor(name, shape, dt))

    with es:
        cv = sbp("cv", [128, CV_N], F32)
        cc = sbp("cc", [128, 16], F32)
        gng = sbp("gng", [128, 3, D], F32)
        gw = sbp("gw", [128, 16, 128], BF16)
        ident = sbp("ident", [128, 128], BF16)
        identf = sbp("identf", [128, 128], F32)
        ones = sbp("ones", [128, 128], BF16)
        hst = sbp("hst", [128, NCH], F32)
        hal = sbp("hal", [128, NCH, 4], F32)
        xnT = sbp("xnT", [128, 16, HAL + TT], BF16)
        xsb = [sbp(f"xsb{i}", [128, D], BF16) for i in range(2)]
        ssq = sbp("ssq", [128, 8], F32)
        rstd = sbp("rstd", [128, 8], F32)
        NT = 6
        tmp = [sbp(f"tmp{i}", [128, HAL + TT], F32) for i in range(NT)]
        xcb = sbp("xcb", [128, TT], BF16)
        xst0 = sbp("xst0", [128, D], F32)
        base = off[0]
        wlru = sbp("wlru", [128, 16, 1024], BF16)
        xst1 = sbp("xst1", [128, D], F32)
        xst = [xst0, xst1]
        off[0] = base
        WS = 2
        wring = [sbp(f"wr{i}", [128, 16 * 512], BF16) for i in range(WS)]
        R = [sbp(f"R{i}", [128, D], F32) for i in range(4)]
        oT = sbp("oT", [128, 16, HAL + TT], BF16)
        gc = sbp("gc", [128, NCH, TT], F32)
        gstat = [sbp(f"gstat{i}", [128, TT], BF16) for i in range(2)]
        dgb = [sbp(f"dgb{i}", [128, 31, 128], BF16) for i in range(2)]
        lnt = [sbp(f"lnt{i}", [128, TT], F32) for i in range(4)]
        rl = [sbp(f"rl{i}", [128, TT], BF16) for i in range(2)]
        assert off[0] < 229000, off[0]
        NB = 4
        pb = [ps(f"pb{i}", [128, 512], F32) for i in range(NB)]
        pstat = [ps(f"pst{i}", [128, 512], F32) for i in range(2)]
        ptr = [ps(f"ptr{i}", [128, 8, 128], BF16) for i in range(2)]
        bank_ctr = [0]

        def nextbank():
            b = bank_ctr[0] % NB
            bank_ctr[0] += 1
            return b

        S.dma("sp", lambda e: e.dma_start(out=cv[:], in_=cv_d), "const_sp", writes=["cv"])
        S.dma("sp", lambda e: e.dma_start(out=gng[:], in_=gng_d), "const_sp", writes=["gng"])
        S.dma("pool", lambda e: e.dma_start(out=gw[:], in_=gw_d), "const_pool", writes=["gw"])
        win_v = win_d.rearrange("(k p) n -> p k n", p=128)
        wout_v = wout_d.rearrange("(k p) n -> p k n", p=128)
        w1_v = w1_d.rearrange("(k p) n -> p k n", p=128)
        w2_v = w2_d.rearrange("(k p) n -> p k n", p=128)
        if NPRE > 0:
            for h in range(2):
                S.dma("pool", lambda e, h=h: e.dma_start(out=wlru[:, :, h * 512:(h + 1) * 512], in_=win_v[:, :, h * 512:(h + 1) * 512]),
                      "const_pool", writes=[("wlru", h)])
        S.op("pool", lambda e: e.memset(identf[:], 0.0), writes=["identf"])
        S.op("pool", lambda e: e.affine_select(out=identf[:], in_=identf[:], pattern=[[-1, 128]], compare_op=ALU.not_equal,
                                               fill=1.0, base=0, channel_multiplier=1), reads=["identf"], writes=["identf"])
        S.op("dve", lambda e: e.tensor_copy(out=ident[:], in_=identf[:]), reads=["identf"], writes=["ident"])
        S.op("dve", lambda e: e.memset(ones[:], 1.0), writes=["ones"])
        S.op("dve", lambda e: e.memset(hst[:], 0.0), writes=[("hst", c) for c in range(NCH)])
        S.op("dve", lambda e: e.memset(hal[:], 0.0), writes=[("hal", c) for c in range(NCH)])
        S.op("act", lambda e: e.activation(out=cc[:, 0:8], in_=cv[:, CV_LAM:CV_LAM + 8], func=AF.Exp, scale=-1.0),
             reads=["cv"], writes=["cc"])
        S.op("act", lambda e: e.activation(out=cc[:, 0:8], in_=cc[:, 0:8], func=AF.Ln, bias=1.0, scale=1.0),
             reads=["cc"], writes=["cc"])
        S.op("dve", lambda e: e.tensor_scalar(out=cc[:, 8:16], in0=cc[:, 0:8], scalar1=-16.0, scalar2=None, op0=ALU.mult),
             reads=["cc"], writes=["cc2"])
        S.op("dve", lambda e: e.tensor_scalar(out=cc[:, 0:8], in0=cc[:, 0:8], scalar1=-8.0, scalar2=None, op0=ALU.mult),
             reads=["cc", "cc2"], writes=["cc"])

        nrm_ctr = [0]

        def norm_T(src, srck, np_, c0, gidx):
            n = nrm_ctr[0]
            nrm_ctr[0] += 1
            slot = n % 2
            col = n % 8
            S.op("act", lambda e: e.activation(out=xsb[slot][0:np_, :], in_=src[0:np_, :], func=AF.Square, accum_out=ssq[0:np_, col:col + 1]),
                 reads=srck, writes=[("xsb", slot), ("ssq", col)])
            S.op("act", lambda e: e.activation(out=rstd[0:np_, col:col + 1], in_=ssq[0:np_, col:col + 1], func=AF.Sqrt, scale=1.0 / D, bias=EPS),
                 reads=[("ssq", col)], writes=[("rstd", col)])
            S.op("dve", lambda e: e.reciprocal(out=rstd[0:np_, col:col + 1], in_=rstd[0:np_, col:col + 1]),
                 reads=[("rstd", col)], writes=[("rstd", col)])
            S.op("dve", lambda e: e.scalar_tensor_tensor(out=xsb[slot][0:np_, :], in0=src[0:np_, :], scalar=rstd[0:np_, col:col + 1],
                                                         in1=gng[0:np_, gidx, :], op0=ALU.mult, op1=ALU.mult),
                 reads=list(srck) + [("rstd", col), "gng"], writes=[("xsb", slot)])
            for half in range(2):
                def tr(e, half=half):
                    for j in range(8):
                        k = half * 8 + j
                        mm = e.transpose(out=ptr[half][:, j, 0:np_], in_=xsb[slot][0:np_, k * 128:(k + 1) * 128], identity=ident[0:np_, 0:np_])
                    return mm
                S.op("pe", tr, reads=[("xsb", slot), "ident"], writes=[("ptr", half)])
                wk = [("xnT", half * 8 + j, c0) for j in range(8)]
                if half == 0:
                    S.op("act", lambda e: e.activation(out=xnT[:, 0:8, c0:c0 + np_], in_=ptr[0][:, :, 0:np_], func=AF.Copy),
                         reads=[("ptr", 0)], writes=wk)
                else:
                    S.op("dve", lambda e: e.tensor_copy(out=xnT[:, 8:16, c0:c0 + np_], in_=ptr[1][:, :, 0:np_]),
                         reads=[("ptr", 1)], writes=wk)
            return col

        allx = [("xnT", k, HAL + s * 128) for k in range(16) for s in range(4)]
        allxh = allx + [("xnT", k, 0) for k in range(16)]

        wr_ctr = [0]

        def load_piece(src_ap, nk, ncol):
            slot = wr_ctr[0] % WS
            wr_ctr[0] += 1
            view = wring[slot][:, 0:nk * ncol].rearrange("p (k n) -> p k n", k=nk)
            S.dma("pool", lambda e: e.dma_start(out=view, in_=src_ap), f"wr{slot}", writes=[("wr", slot)])
            return view, ("wr", slot)

        tmp_ctr = [0]

        def nexttmp():
            i = tmp_ctr[0] % NT
            tmp_ctr[0] += 1
            return i

        def lru_chunk(c, wview, wkey, wcol0):
            b = nextbank()

            def mm(e):
                for k in range(16):
                    m = e.matmul(pb[b][:], lhsT=wview[:, k, wcol0:wcol0 + 128], rhs=xnT[:, k, HAL:HAL + TT], start=(k == 0), stop=(k == 15))
                return m
            S.op("pe", mm, reads=[wkey] + allx, writes=[("pb", b)])
            ixl = nexttmp()
            xl = tmp[ixl]
            S.op("act", lambda e: e.activation(out=xl[:, HAL:HAL + TT], in_=pb[b][:], func=AF.Copy), reads=[("pb", b)], writes=[("tmp", ixl)])
            S.op("dve", lambda e: e.tensor_copy(out=xl[:, HAL - 3:HAL], in_=hal[:, c, 0:3]),
                 reads=[("hal", c), ("tmp", ixl)], writes=[("tmp", ixl)])
            ixc = nexttmp()
            xc = tmp[ixc]
            w0 = CV_C4W + c * 4
            S.op("dve", lambda e: e.tensor_scalar(out=xc[:, 0:TT], in0=xl[:, HAL - 3:HAL - 3 + TT], scalar1=cv[:, w0:w0 + 1],
                                                  scalar2=cv[:, CV_CB + c:CV_CB + c + 1], op0=ALU.mult, op1=ALU.add),
                 reads=[("tmp", ixl), "cv"], writes=[("tmp", ixc)])
            for k in range(1, 4):
                S.op("dve", lambda e, k=k: e.scalar_tensor_tensor(out=xc[:, 0:TT], in0=xl[:, HAL - 3 + k:HAL - 3 + k + TT],
                                                                  scalar=cv[:, w0 + k:w0 + k + 1], in1=xc[:, 0:TT], op0=ALU.mult, op1=ALU.add),
                     reads=[("tmp", ixl), ("tmp", ixc), "cv"], writes=[("tmp", ixc)])
            S.op("dve", lambda e: e.tensor_copy(out=hal[:, c, 0:3], in_=xl[:, HAL + TT - 3:HAL + TT]), reads=[("tmp", ixl)], writes=[("hal", c)])
            S.op("act", lambda e: e.activation(out=xcb[:], in_=xc[:, 0:TT], func=AF.Copy), reads=[("tmp", ixc)], writes=["xcb"])
            ba = nextbank()
            S.op("pe", lambda e: e.matmul(pb[ba][:], lhsT=gw[:, c, :], rhs=xcb[:], start=True, stop=True), reads=["gw", "xcb"], writes=[("pb", ba)])
            bx = nextbank()
            S.op("pe", lambda e: e.matmul(pb[bx][:], lhsT=gw[:, 8 + c, :], rhs=xcb[:], start=True, stop=True), reads=["gw", "xcb"], writes=[("pb", bx)])
            ir, ii, ia = nexttmp(), nexttmp(), nexttmp()
            r_, i_, a_ = tmp[ir], tmp[ii], tmp[ia]
            S.op("act", lambda e: e.activation(out=r_[:, 0:TT], in_=pb[ba][:], func=AF.Sigmoid, bias=cv[:, CV_GAB + c:CV_GAB + c + 1]),
                 reads=[("pb", ba), "cv"], writes=[("tmp", ir)])
            S.op("act", lambda e: e.activation(out=i_[:, 0:TT], in_=pb[bx][:], func=AF.Sigmoid, bias=cv[:, CV_GXB + c:CV_GXB + c + 1]),
                 reads=[("pb", bx), "cv"], writes=[("tmp", ii)])
            S.op("dve", lambda e: e.tensor_tensor(out=i_[:, 0:TT], in0=i_[:, 0:TT], in1=xc[:, 0:TT], op=ALU.mult),
                 reads=[("tmp", ii), ("tmp", ixc)], writes=[("tmp", ii)])
            S.op("act", lambda e: e.activation(out=a_[:, 0:TT], in_=r_[:, 0:TT], func=AF.Exp, scale=cc[:, c:c + 1]),
                 reads=[("tmp", ir), "cc"], writes=[("tmp", ia)])
            S.op("act", lambda e: e.activation(out=r_[:, 0:TT], in_=r_[:, 0:TT], func=AF.Exp, scale=cc[:, 8 + c:9 + c]),
                 reads=[("tmp", ir), "cc2"], writes=[("tmp", ir)])
            S.op("act", lambda e: e.activation(out=r_[:, 0:TT], in_=r_[:, 0:TT], func=AF.Sqrt, scale=-1.0, bias=1.0),
                 reads=[("tmp", ir)], writes=[("tmp", ir)])
            S.op("dve", lambda e: e.tensor_tensor(out=i_[:, 0:TT], in0=i_[:, 0:TT], in1=r_[:, 0:TT], op=ALU.mult),
                 reads=[("tmp", ii), ("tmp", ir)], writes=[("tmp", ii)])
            S.op("dve", lambda e: e.tensor_tensor_scan(out=xc[:, 0:TT], data0=a_[:, 0:TT], data1=i_[:, 0:TT], initial=hst[:, c:c + 1],
                                                       op0=ALU.mult, op1=ALU.add),
                 reads=[("tmp", ia), ("tmp", ii), ("hst", c)], writes=[("tmp", ixc)])
            return ixc

        xst_ctr = [0]

        def prefix_tile(pt):
            for s in range(4):
                slot = xst_ctr[0] % 2
                xst_ctr[0] += 1
                r0 = pt * TT + s * 128
                S.dma("sp", lambda e, slot=slot, r0=r0: e.dma_start(out=xst[slot][:], in_=xs_d[r0:r0 + 128, :]), f"xst{slot}", writes=[("xst", slot)])
                norm_T(xst[slot], [("xst", slot)], 128, HAL + s * 128, 0)
            for c in range(NCH):
                ih = lru_chunk(c, wlru, ("wlru", c // 4), c * 128)
                h = tmp[ih]
                S.op("dve", lambda e, h=h, c=c: e.tensor_scalar(out=hst[:, c:c + 1], in0=h[:, TT - 1:TT], scalar1=cv[:, CV_FLAG + pt:CV_FLAG + pt + 1],
                                                                scalar2=None, op0=ALU.mult),
                     reads=[("tmp", ih), "cv"], writes=[("hst", c)])

        dg_ctr = [0]

        def rkeys(s):
            return [("R", s, j) for j in range(4)]

        import os
        STOP = int(os.environ.get("STOP_PHASE", "99"))

        def store_R(ft):
            for s in range(4):
                S.dma("sp", lambda e, s=s: e.dma_start(out=out_d[ft * TT + s * 128: ft * TT + (s + 1) * 128, :], in_=R[s][:]),
                      f"R{s}", reads=rkeys(s), writes=[("out", ft, s)])

        def full_tile(ft):
            t0 = (NPRE + ft) * TT
            S.dma("sp", lambda e: e.dma_start(out=xst0[0:HAL, :], in_=xs_d[t0 - HAL:t0, :]), "xst0", writes=[("xst", 0)])
            for s in range(4):
                S.dma("sp", lambda e, s=s: e.dma_start(out=R[s][:], in_=xs_d[t0 + s * 128:t0 + (s + 1) * 128, :]), f"R{s}", writes=rkeys(s))
            norm_T(xst0, [("xst", 0)], HAL, 0, 0)
            for s in range(4):
                norm_T(R[s], rkeys(s), 128, HAL + s * 128, 0)
            if STOP <= 1:
                return store_R(ft)
            for g in range(2):
                wx, wxk = load_piece(win_v[:, :, g * 512:(g + 1) * 512], 16, 512)
                wy, wyk = load_piece(win_v[:, :, 1024 + g * 512:1024 + (g + 1) * 512], 16, 512)
                for cl in range(4):
                    c = g * 4 + cl
                    ih = lru_chunk(c, wx, wxk, cl * 128)
                    h = tmp[ih]
                    S.op("dve", lambda e, h=h, c=c: e.tensor_copy(out=hst[:, c:c + 1], in_=h[:, TT - 1:TT]), reads=[("tmp", ih)], writes=[("hst", c)])
                    by = nextbank()

                    def mmy(e, by=by, wy=wy, cl=cl):
                        for k in range(16):
                            m = e.matmul(pb[by][:], lhsT=wy[:, k, cl * 128:(cl + 1) * 128], rhs=xnT[:, k, HAL:HAL + TT], start=(k == 0), stop=(k == 15))
                        return m
                    S.op("pe", mmy, reads=[wyk] + allx, writes=[("pb", by)])
                    igy = nexttmp()
                    gy = tmp[igy]
                    S.op("act", lambda e, by=by, gy=gy: e.activation(out=gy[:, 0:TT], in_=pb[by][:], func=AF.Gelu_apprx_tanh),
                         reads=[("pb", by)], writes=[("tmp", igy)])
                    S.op("dve", lambda e, h=h, gy=gy, c=c: e.tensor_tensor(out=oT[:, c, HAL:HAL + TT], in0=h[:, 0:TT], in1=gy[:, 0:TT], op=ALU.mult),
                         reads=[("tmp", ih), ("tmp", igy)], writes=[("oT", c)])
            if STOP <= 2:
                return store_R(ft)
            pending = None
            for g in range(2):
                wv, wvk = load_piece(win_v[:, :, 2048 + g * 512:2048 + (g + 1) * 512], 16, 512)
                wg, wgk = load_piece(win_v[:, :, 3072 + g * 512:3072 + (g + 1) * 512], 16, 512)
                for cl in range(4):
                    c = g * 4 + cl
                    gk = ("oT", 8 + c)
                    ds = dg_ctr[0] % 2
                    dg_ctr[0] += 1
                    S.dma("pool", lambda e, ds=ds, c=c: e.dma_start(out=dgb[ds][:], in_=dg_d[c].rearrange("p (k n) -> p k n", k=31),
                                                                    max_dma_last_dim=4096),
                          f"dg{ds}", writes=[("dgb", ds)])
                    bgm = nextbank()

                    def mmg(e, b=bgm, wg=wg, cl=cl):
                        for k in range(16):
                            m = e.matmul(pb[b][:], lhsT=wg[:, k, cl * 128:(cl + 1) * 128], rhs=xnT[:, k, HAL:HAL + TT], start=(k == 0), stop=(k == 15))
                        return m
                    S.op("pe", mmg, reads=[wgk] + allx, writes=[("pb", bgm)])
                    bgh = nextbank()

                    def mmgh(e, b=bgh, wg=wg, wv=wv, cl=cl):
                        for k in range(16):
                            m = e.matmul(pb[b][:, 0:HAL], lhsT=wg[:, k, cl * 128:(cl + 1) * 128], rhs=xnT[:, k, 0:HAL], start=(k == 0), stop=(k == 15))
                        for k in range(16):
                            m = e.matmul(pb[b][:, 64:64 + HAL], lhsT=wv[:, k, cl * 128:(cl + 1) * 128], rhs=xnT[:, k, 0:HAL],
                                         start=(k == 0), stop=(k == 15), skip_group_check=True)
                        return m
                    S.op("pe", mmgh, reads=[wgk, wvk] + allxh, writes=[("pb", bgh)])
                    isg = nexttmp()
                    sg = tmp[isg]
                    S.op("act", lambda e, b=bgm, sg=sg: e.activation(out=sg[:, HAL:HAL + TT], in_=pb[b][:], func=AF.Sigmoid),
                         reads=[("pb", bgm)], writes=[("tmp", isg)])
                    S.op("act", lambda e, b=bgh, sg=sg: e.activation(out=sg[:, 0:HAL], in_=pb[b][:, 0:HAL], func=AF.Sigmoid),
                         reads=[("pb", bgh), ("tmp", isg)], writes=[("tmp", isg)])
                    bvm = nextbank()

                    def mmv(e, b=bvm, wv=wv, cl=cl):
                        for k in range(16):
                            m = e.matmul(pb[b][:], lhsT=wv[:, k, cl * 128:(cl + 1) * 128], rhs=xnT[:, k, HAL:HAL + TT], start=(k == 0), stop=(k == 15))
                        return m
                    S.op("pe", mmv, reads=[wvk] + allx, writes=[("pb", bvm)])
                    S.op("dve", lambda e, b=bvm, sg=sg, c=c: e.tensor_tensor(out=oT[:, 8 + c, HAL:HAL + TT], in0=pb[b][:], in1=sg[:, HAL:HAL + TT],
                                                                             op=ALU.mult),
                         reads=[("pb", bvm), ("tmp", isg)], writes=[gk])
                    S.op("dve", lambda e, b=bgh, sg=sg, c=c: e.tensor_tensor(out=oT[:, 8 + c, 0:HAL], in0=pb[b][:, 64:64 + HAL], in1=sg[:, 0:HAL],
                                                                             op=ALU.mult),
                         reads=[("pb", bgh), ("tmp", isg), gk], writes=[gk])
                    bc = nextbank()

                    def mmc(e, b=bc, ds=ds, c=c):
                        for k in range(31):
                            m = e.matmul(pb[b][:], lhsT=dgb[ds][:, k, :], rhs=oT[:, 8 + c, HAL - 30 + k:HAL - 30 + k + TT], start=(k == 0), stop=(k == 30))
                        return m
                    S.op("pe", mmc, reads=[("dgb", ds), gk], writes=[("pb", bc)])
                    if pending is not None:
                        pending()
                    bcol = cv[:, CV_DWB + c:CV_DWB + c + 1]
                    S.op("act", lambda e, b=bc, c=c, bcol=bcol: e.activation(out=gc[:, c, :], in_=pb[b][:], func=AF.Identity, bias=bcol),
                         reads=[("pb", bc), "cv"], writes=[("gc", c)])
                    S.op("act", lambda e, b=bc, bcol=bcol: e.activation(out=gstat[0][:], in_=pb[b][:], func=AF.Identity, bias=bcol),
                         reads=[("pb", bc), "cv"], writes=[("gstat", 0)])
                    S.op("act", lambda e, b=bc, bcol=bcol: e.activation(out=gstat[1][:], in_=pb[b][:], func=AF.Square, bias=bcol),
                         reads=[("pb", bc), "cv"], writes=[("gstat", 1)])

                    def stat_mm(c=c):
                        S.op("pe", lambda e, c=c: e.matmul(pstat[0][:], lhsT=ones[:], rhs=gstat[0][:], start=(c == 0), stop=(c == NCH - 1)),
                             reads=["ones", ("gstat", 0)], writes=[("pstat", 0)])
                        S.op("pe", lambda e, c=c: e.matmul(pstat[1][:], lhsT=ones[:], rhs=gstat[1][:], start=(c == 0), stop=(c == NCH - 1)),
                             reads=["ones", ("gstat", 1)], writes=[("pstat", 1)])
                    pending = stat_mm
            pending()
            if STOP <= 3:
                return store_R(ft)
            mean, msq, rs, nmr = lnt
            S.op("act", lambda e: e.activation(out=mean[:], in_=pstat[0][:], func=AF.Copy, scale=1.0 / 1024), reads=[("pstat", 0)], writes=[("lnt", 0)])
            S.op("act", lambda e: e.activation(out=msq[:], in_=pstat[0][:], func=AF.Square, scale=1.0 / 1024), reads=[("pstat", 0)], writes=[("lnt", 1)])
            S.op("dve", lambda e: e.scalar_tensor_tensor(out=rs[:], in0=pstat[1][:], scalar=1.0 / 1024, in1=msq[:], op0=ALU.mult, op1=ALU.subtract),
                 reads=[("pstat", 1), ("lnt", 1)], writes=[("lnt", 2)])
            S.op("act", lambda e: e.activation(out=rs[:], in_=rs[:], func=AF.Sqrt, bias=EPS, scale=1.0), reads=[("lnt", 2)], writes=[("lnt", 2)])
            S.op("dve", lambda e: e.reciprocal(out=rs[:], in_=rs[:]), reads=[("lnt", 2)], writes=[("lnt", 2)])
            S.op("dve", lambda e: e.scalar_tensor_tensor(out=nmr[:], in0=mean[:], scalar=-1.0, in1=rs[:], op0=ALU.mult, op1=ALU.mult),
                 reads=[("lnt", 0), ("lnt", 2)], writes=[("lnt", 3)])
            for c in range(NCH):
                it = nexttmp()
                t_ = tmp[it]
                S.op("dve", lambda e, c=c, t_=t_: e.tensor_tensor(out=t_[:, 0:TT], in0=gc[:, c, :], in1=rs[:], op=ALU.mult),
                     reads=[("gc", c), ("lnt", 2)], writes=[("tmp", it)])
                S.op("dve", lambda e, t_=t_: e.tensor_tensor(out=t_[:, 0:TT], in0=t_[:, 0:TT], in1=nmr[:], op=ALU.add),
                     reads=[("tmp", it), ("lnt", 3)], writes=[("tmp", it)])
                S.op("act", lambda e, c=c, t_=t_: e.activation(out=oT[:, 8 + c, HAL:HAL + TT], in_=t_[:, 0:TT], func=AF.Silu,
                                                               scale=cv[:, CV_LNG + c:CV_LNG + c + 1], bias=cv[:, CV_LNB + c:CV_LNB + c + 1]),
                     reads=[("tmp", it), "cv"], writes=[("oT", 8 + c)])
            if STOP <= 4:
                return store_R(ft)
            for j in range(4):
                wo, wok = load_piece(wout_v[:, :, j * 512:(j + 1) * 512], 16, 512)
                for s in range(4):
                    b = nextbank()

                    def mmo(e, b=b, s=s, wo=wo):
                        for k in range(16):
                            m = e.matmul(pb[b][:], lhsT=oT[:, k, HAL + s * 128:HAL + (s + 1) * 128], rhs=wo[:, k, :], start=(k == 0), stop=(k == 15))
                        return m
                    S.op("pe", mmo, reads=[wok] + [("oT", k) for k in range(16)], writes=[("pb", b)])
                    S.op("dve", lambda e, b=b, s=s, j=j: e.tensor_tensor(out=R[s][:, j * 512:(j + 1) * 512], in0=pb[b][:],
                                                                         in1=R[s][:, j * 512:(j + 1) * 512], op=ALU.add),
                         reads=[("pb", b), ("R", s, j)], writes=[("R", s, j)])
            if STOP <= 5:
                return store_R(ft)
            for s in range(4):
                norm_T(R[s], rkeys(s), 128, HAL + s * 128, 1)
            for grp in range(8):
                a0 = (grp % 2) * 8
                for half in range(2):
                    w1p, w1k = load_piece(w1_v[:, :, grp * 1024 + half * 512: grp * 1024 + (half + 1) * 512], 16, 512)
                    for cl in range(4):
                        fl = half * 4 + cl
                        b = nextbank()

                        def mm1(e, b=b, w1p=w1p, cl=cl):
                            for k in range(16):
                                m = e.matmul(pb[b][:], lhsT=w1p[:, k, cl * 128:(cl + 1) * 128], rhs=xnT[:, k, HAL:HAL + TT], start=(k == 0), stop=(k == 15))
                            return m
                        S.op("pe", mm1, reads=[w1k] + allx, writes=[("pb", b)])
                        rs_ = fl % 2
                        S.op("act", lambda e, b=b, rs_=rs_: e.activation(out=rl[rs_][:], in_=pb[b][:], func=AF.Relu), reads=[("pb", b)], writes=[("rl", rs_)])
                        S.op("dve", lambda e, b=b, rs_=rs_, a0=a0, fl=fl: e.scalar_tensor_tensor(
                            out=oT[:, a0 + fl, HAL:HAL + TT], in0=pb[b][:], scalar=0.0, in1=rl[rs_][:], op0=ALU.max, op1=ALU.mult),
                            reads=[("pb", b), ("rl", rs_)], writes=[("oT", a0 + fl)])
                w2ps = []
                for half in range(2):
                    r0 = grp * 8 + half * 4
                    w2ps.append(load_piece(w2_v[:, r0:r0 + 4, :], 4, D))
                for s in range(4):
                    for j in range(4):
                        b = nextbank()

                        def mm2(e, b=b, s=s, j=j, a0=a0, w2ps=w2ps):
                            for fl in range(8):
                                wv_ = w2ps[fl // 4][0]
                                m = e.matmul(pb[b][:], lhsT=oT[:, a0 + fl, HAL + s * 128:HAL + (s + 1) * 128], rhs=wv_[:, fl % 4, j * 512:(j + 1) * 512],
                                             start=(fl == 0), stop=(fl == 7))
                            return m
                        S.op("pe", mm2, reads=[w2ps[0][1], w2ps[1][1]] + [("oT", a0 + fl) for fl in range(8)], writes=[("pb", b)])
                        S.op("dve", lambda e, b=b, s=s, j=j: e.tensor_tensor(out=R[s][:, j * 512:(j + 1) * 512], in0=pb[b][:],
                                                                             in1=R[s][:, j * 512:(j + 1) * 512], op=ALU.add),
                             reads=[("pb", b), ("R", s, j)], writes=[("R", s, j)])
            if STOP <= 6:
                return store_R(ft)
            for s in range(4):
                n = nrm_ctr[0]
                nrm_ctr[0] += 1
                slot = n % 2
                col = n % 8
                S.op("act", lambda e, s=s, slot=slot, col=col: e.activation(out=xsb[slot][:], in_=R[s][:], func=AF.Square, accum_out=ssq[:, col:col + 1]),
                     reads=rkeys(s), writes=[("xsb", slot), ("ssq", col)])
                S.op("act", lambda e, col=col: e.activation(out=rstd[:, col:col + 1], in_=ssq[:, col:col + 1], func=AF.Sqrt, scale=1.0 / D, bias=EPS),
                     reads=[("ssq", col)], writes=[("rstd", col)])
                S.op("dve", lambda e, col=col: e.reciprocal(out=rstd[:, col:col + 1], in_=rstd[:, col:col + 1]), reads=[("rstd", col)], writes=[("rstd", col)])
                S.op("dve", lambda e, s=s, col=col: e.scalar_tensor_tensor(out=R[s][:], in0=R[s][:], scalar=rstd[:, col:col + 1], in1=gng[:, 2, :],
                                                                           op0=ALU.mult, op1=ALU.mult),
                     reads=rkeys(s) + [("rstd", col), "gng"], writes=rkeys(s))
                S.dma("sp", lambda e, s=s: e.dma_start(out=out_d[ft * TT + s * 128: ft * TT + (s + 1) * 128, :], in_=R[s][:]),
                      f"R{s}", reads=rkeys(s), writes=[("out", ft, s)])

        for pt in range(NPRE):
            prefix_tile(pt)
        if NPRE > 0:
            S.barrier()
        for ft in range(NFULL):
            full_tile(ft)
        outkeys = [("out", ft, s) for ft in range(NFULL) for s in range(4)]
        S.op("sp", lambda e: e.nop(), reads=outkeys)

        chans = sorted(set(o["chan"] for o in S.ops if o["kind"] == "dma"))
        sems = {e: es.enter_context(nc.semaphore(f"s_{e}")) for e in ["pe", "act", "dve", "pool", "sp"]}
        chan_sems = {c: es.enter_context(nc.semaphore(f"c_{c}")) for c in chans}
        block = es.enter_context(nc.Block())
        S.emit(block, sems, chan_sems)
        print("ops", len(S.ops), "sbuf bytes", off[0], "counts", {k: v for k, v in S.final_counts.items()})
    return nc


_CACHE = {}


def _host_inputs(inp, NPRE=12, NFULL=4, seq=8192, batch=2, chunks=4):
    f = np.float32
    x = np.asarray(inp["x"], f)
    cvs = []
    per = NFULL * TT

    def pc(v):
        return np.ascontiguousarray(np.asarray(v, f).reshape(NCH, 128).T)
    base = np.zeros((128, CV_N), f)
    base[:, CV_CB:CV_CB + 8] = pc(inp["lru_conv_b"][0])
    base[:, CV_GAB:CV_GAB + 8] = pc(inp["lru_gate_a_b"][0])
    base[:, CV_GXB:CV_GXB + 8] = pc(inp["lru_gate_x_b"][0])
    base[:, CV_LAM:CV_LAM + 8] = pc(inp["lru_lambda"][0])
    base[:, CV_DWB:CV_DWB + 8] = pc(inp["conf_dw_b"][0])
    base[:, CV_LNG:CV_LNG + 8] = pc(inp["conf_ln_g"][0])
    base[:, CV_LNB:CV_LNB + 8] = pc(inp["conf_ln_b"][0])
    c4 = np.asarray(inp["lru_conv_w"][0], f)
    base[:, CV_C4W:CV_C4W + 32] = c4.reshape(4, NCH, 128).transpose(2, 1, 0).reshape(128, 32)
    gng = np.stack([np.broadcast_to(np.asarray(inp["mix_norm_g"][0], f), (128, D)),
                    np.broadcast_to(np.asarray(inp["mlp_norm_g"][0], f), (128, D)),
                    np.broadcast_to(np.asarray(inp["final_norm_g"], f), (128, D))], axis=1)
    gng = np.ascontiguousarray(gng)
    gw = np.zeros((128, 16, 128), f)
    for gi, name in enumerate(["lru_gate_a_w", "lru_gate_x_w"]):
        w = np.asarray(inp[name][0], f)
        for c in range(NCH):
            for hh in range(2):
                gw[hh * 64:(hh + 1) * 64, gi * 8 + c, hh * 64:(hh + 1) * 64] = w[c * 2 + hh]
    dw = np.asarray(inp["conf_dw_w"][0], f)
    dg = np.zeros((NCH, 128, 31, 128), f)
    ar = np.arange(128)
    for c in range(NCH):
        dg[c, ar, :, ar] = dw[:, c * 128:(c + 1) * 128].T
    dg = dg.reshape(NCH, 128, 31 * 128)
    shared = dict(gng=gng, gw=gw, dg=dg,
                  w_in=np.ascontiguousarray(np.asarray(inp["w_in"][0], f)),
                  w_out=np.ascontiguousarray(np.asarray(inp["w_out"][0], f)),
                  w1=np.ascontiguousarray(np.asarray(inp["mlp_w1"][0], f)),
                  w2=np.ascontiguousarray(np.asarray(inp["mlp_w2"][0], f)))
    maps = []
    for core in range(batch * chunks):
        b, q = divmod(core, chunks)
        npad = (NPRE * TT) - q * per
        xs = np.zeros(((NPRE + NFULL) * TT, D), f)
        xs[npad:] = x[b, 0:(q + 1) * per]
        cvc = base.copy()
        for pt in range(NPRE):
            cvc[:, CV_FLAG + pt] = 1.0 if pt * TT >= npad else 0.0
        m = dict(shared)
        m["xs"] = xs
        m["cv"] = cvc
        maps.append(m)
    return maps


def kernel(**inputs):
    NPRE, NFULL = 12, 4
    key = (NPRE, NFULL)
    if key not in _CACHE:
        _CACHE[key] = build_program(NPRE, NFULL)
    nc = _CACHE[key]
    maps = _host_inputs(inputs, NPRE, NFULL)
    res = run_bass_kernel_spmd(nc, maps, core_ids=list(range(8)))
    out = np.zeros((2, 8192, D), np.float32)
    for core in range(8):
        b, q = divmod(core, 4)
        out[b, q * 2048:(q + 1) * 2048] = res.results[core]["out"]
    return out
```

```python
import numpy as np
import concourse.bass as bass
import concourse.mybir as mybir
from concourse.bass_utils import run_bass_kernel_spmd

F32 = mybir.dt.float32
BF16 = mybir.dt.bfloat16
AF = mybir.ActivationFunctionType
ALU = mybir.AluOpType

D = 2048
DIN = 4096
DFF = 8192
TT = 512
HAL = 32
NCH = 8
EPS = 1e-6
CV_CB, CV_GAB, CV_GXB, CV_LAM, CV_DWB, CV_LNG, CV_LNB = 0, 8, 16, 24, 32, 40, 48
CV_C4W = 56
CV_FLAG = 88
CV_N = 104


class Sched:
    def __init__(self, nc):
        self.nc = nc
        self.ops = []
        self.last_w = {}
        self.readers = {}
        self.eng_objs = {"pe": nc.tensor, "act": nc.scalar, "dve": nc.vector, "pool": nc.gpsimd, "sp": nc.sync}

    def _deps(self, eng, reads, writes):
        deps = set()
        for k in reads:
            w = self.last_w.get(k)
            if w is not None:
                deps.add(w)
        for k in writes:
            w = self.last_w.get(k)
            if w is not None:
                deps.add(w)
            for r in self.readers.get(k, ()):
                deps.add(r)
        return deps

    def _add(self, eng, fn, reads, writes, kind, chan=None):
        oid = len(self.ops)
        deps = self._deps(eng, reads, writes)
        keep = set()
        raw = set(self.last_w.get(k) for k in reads if self.last_w.get(k) is not None)
        for d in deps:
            od = self.ops[d]
            if od["kind"] == "dma" or kind == "dma":
                keep.add(d)
            elif od["eng"] != eng:
                keep.add(d)
            elif d in raw and eng != "pe":
                keep.add(d)
        self.ops.append(dict(eng=eng, fn=fn, deps=keep, kind=kind, chan=chan, sig=False))
        for k in reads:
            self.readers.setdefault(k, []).append(oid)
        for k in writes:
            self.last_w[k] = oid
            self.readers[k] = []
        return oid

    def op(self, eng, fn, reads=(), writes=()):
        return self._add(eng, fn, list(reads), list(writes), "cmp")

    def dma(self, queue, fn, chan, reads=(), writes=()):
        return self._add(queue, fn, list(reads), list(writes), "dma", chan)

    def barrier(self):
        last = {}
        for i, o in enumerate(self.ops):
            if o["kind"] == "dma":
                last[("c", o["chan"])] = i
            else:
                last[("e", o["eng"])] = i
        deps = set(last.values())
        for eng in ["pe", "act", "dve", "pool", "sp"]:
            self.ops.append(dict(eng=eng, fn=lambda e: e.nop(), deps=set(deps), kind="cmp", chan=None, sig=False))

    def emit(self, block, sems, chan_sems):
        ops = self.ops
        for o in ops:
            for d in o["deps"]:
                ops[d]["sig"] = True
        cnt = {}
        for i, o in enumerate(ops):
            if o["kind"] == "dma":
                c = o["chan"]
                cnt[c] = cnt.get(c, 0) + 16
                o["ev"] = (("c", c), cnt[c])
            elif o["sig"]:
                e = o["eng"]
                cnt[e] = cnt.get(e, 0) + 1
                o["ev"] = (("e", e), cnt[e])
            else:
                o["ev"] = None
        const_total = {c: v for c, v in cnt.items() if isinstance(c, str) and c.startswith("const")}
        self.final_counts = cnt
        by_eng = {}
        for i, o in enumerate(ops):
            by_eng.setdefault(o["eng"], []).append(i)

        def semof(key):
            return chan_sems[key[1]] if key[0] == "c" else sems[key[1]]

        snap = {}
        nwaits = [0]

        def run_engine(ename, engobj):
            known = {}
            for i in by_eng.get(ename, []):
                o = ops[i]
                need = {}
                for d in o["deps"]:
                    key, val = ops[d]["ev"]
                    if key[0] == "c" and key[1] in const_total:
                        val = const_total[key[1]]
                    if need.get(key, 0) < val:
                        need[key] = val
                for key, val in sorted(need.items(), key=lambda kv: -kv[1]):
                    if known.get(key, 0) >= val:
                        continue
                    engobj.wait_ge(semof(key), val)
                    nwaits[0] += 1
                    known[key] = val
                    sn = snap.get((key, val))
                    if sn:
                        for k2, v2 in sn.items():
                            if known.get(k2, 0) < v2:
                                known[k2] = v2
                inst = o["fn"](engobj)
                if o["ev"] is not None:
                    key, val = o["ev"]
                    if o["kind"] == "dma":
                        inst.then_inc(semof(key), 16)
                    else:
                        inst.then_inc(semof(key), 1)
                        snap[(key, val)] = dict(known)
                        known[key] = max(known.get(key, 0), 0)

        @block.sync
        def _(e):
            run_engine("sp", e)

        @block.gpsimd
        def _(e):
            run_engine("pool", e)

        @block.scalar
        def _(e):
            run_engine("act", e)

        @block.vector
        def _(e):
            run_engine("dve", e)

        @block.tensor
        def _(e):
            run_engine("pe", e)


def build_program(NPRE, NFULL):
    nc = bass.Bass("TRN2", target_bir_lowering=False)
    NTILE = NPRE + NFULL
    xs_d = nc.dram_tensor("xs", [NTILE * TT, D], F32, kind="ExternalInput").ap()
    cv_d = nc.dram_tensor("cv", [128, CV_N], F32, kind="ExternalInput").ap()
    gng_d = nc.dram_tensor("gng", [128, 3, D], F32, kind="ExternalInput").ap()
    gw_d = nc.dram_tensor("gw", [128, 16, 128], F32, kind="ExternalInput").ap()
    dg_d = nc.dram_tensor("dg", [NCH, 128, 31 * 128], F32, kind="ExternalInput").ap()
    win_d = nc.dram_tensor("w_in", [D, DIN], F32, kind="ExternalInput").ap()
    wout_d = nc.dram_tensor("w_out", [D, D], F32, kind="ExternalInput").ap()
    w1_d = nc.dram_tensor("w1", [D, DFF], F32, kind="ExternalInput").ap()
    w2_d = nc.dram_tensor("w2", [DFF, D], F32, kind="ExternalInput").ap()
    out_d = nc.dram_tensor("out", [NFULL * TT, D], F32, kind="ExternalOutput").ap()

    S = Sched(nc)
    import contextlib
    es = contextlib.ExitStack()
    off = [16384]

    def sb(name, shape, dt, at=None):
        nbytes = int(np.prod(shape[1:])) * (4 if dt == F32 else 2)
        nbytes = (nbytes + 63) // 64 * 64
        if at is None:
            at = off[0]
            off[0] += nbytes
        return nc.alloc_sbuf_tensor_at(name, shape, dt, offset=at), at + nbytes

    def sbp(name, shape, dt):
        return sb(name, shape, dt)[0]

    def ps(name, shape, dt):
        return es.enter_context(nc.psum_tensor(name, shape, dt))

    with es:
        cv = sbp("cv", [128, CV_N], F32)
        cc = sbp("cc", [128, 16], F32)
        gng = sbp("gng", [128, 3, D], F32)
        gw = sbp("gw", [128, 16, 128], BF16)
        ident = sbp("ident", [128, 128], BF16)
        identf = sbp("identf", [128, 128], F32)
        ones = sbp("ones", [128, 128], BF16)
        hst = sbp("hst", [128, NCH], F32)
        hal = sbp("hal", [128, NCH, 4], F32)
        xnT = sbp("xnT", [128, 16, HAL + TT], BF16)
        xsb = [sbp(f"xsb{i}", [128, D], BF16) for i in range(2)]
        ssq = sbp("ssq", [128, 8], F32)
        rstd = sbp("rstd", [128, 8], F32)
        NT = 6
        tmp = [sbp(f"tmp{i}", [128, HAL + TT], F32) for i in range(NT)]
        xcbs = [sbp(f"xcb{i}", [128, TT], BF16) for i in range(2)]
        xh = sbp("xh", [128, 16, HAL], BF16)
        base = off[0]
        wlru = sbp("wlru", [128, 16, 1024], BF16)
        xst = [sbp(f"xst{i}", [128, D], F32) for i in range(2)]
        NPT = 35
        ptmp = [sbp(f"ptmp{i}", [128, HAL + TT], F32) for i in range(NPT)]
        pxcb = [sbp(f"pxcb{i}", [128, TT], BF16) for i in range(NCH)]
        assert off[0] < 229000, off[0]
        off[0] = base
        WS = 3
        wring = [sbp(f"wr{i}", [128, 16 * 512], BF16) for i in range(WS)]
        R = [sbp(f"R{i}", [128, D], F32) for i in range(4)]
        oT = sbp("oT", [128, 16, HAL + TT], BF16)
        gc = sbp("gc", [128, NCH, TT], F32)
        gstat = [sbp(f"gstat{i}", [128, TT], BF16) for i in range(2)]
        dgb = [sbp(f"dgb{i}", [128, 31, 128], BF16) for i in range(1)]
        lnt = [sbp(f"lnt{i}", [128, TT], F32) for i in range(4)]
        rl = [sbp(f"rl{i}", [128, TT], BF16) for i in range(2)]
        assert off[0] < 229000, off[0]
        NB = 4
        pb = [ps(f"pb{i}", [128, 512], F32) for i in range(NB)]
        pstat = [ps(f"pst{i}", [128, 512], F32) for i in range(2)]
        ptr = [ps(f"ptr{i}", [128, 8, 128], BF16) for i in range(2)]
        bank_ctr = [0]

        def nextbank():
            b = bank_ctr[0] % NB
            bank_ctr[0] += 1
            return b

        S.dma("sp", lambda e: e.dma_start(out=cv[:], in_=cv_d), "const_sp", writes=["cv"])
        S.dma("sp", lambda e: e.dma_start(out=gng[:], in_=gng_d), "const_sp", writes=["gng"])
        S.dma("pool", lambda e: e.dma_start(out=gw[:], in_=gw_d), "const_pool", writes=["gw"])
        win_v = win_d.rearrange("(k p) n -> p k n", p=128)
        wout_v = wout_d.rearrange("(k p) n -> p k n", p=128)
        w1_v = w1_d.rearrange("(k p) n -> p k n", p=128)
        w2_v = w2_d.rearrange("(k p) n -> p k n", p=128)
        if NPRE > 0:
            for h in range(2):
                S.dma("pool", lambda e, h=h: e.dma_start(out=wlru[:, :, h * 512:(h + 1) * 512], in_=win_v[:, :, h * 512:(h + 1) * 512]),
                      "const_pool", writes=[("wlru", h)])
        S.op("pool", lambda e: e.memset(identf[:], 0.0), writes=["identf"])
        S.op("pool", lambda e: e.affine_select(out=identf[:], in_=identf[:], pattern=[[-1, 128]], compare_op=ALU.not_equal,
                                               fill=1.0, base=0, channel_multiplier=1), reads=["identf"], writes=["identf"])
        S.op("dve", lambda e: e.tensor_copy(out=ident[:], in_=identf[:]), reads=["identf"], writes=["ident"])
        S.op("dve", lambda e: e.memset(ones[:], 1.0), writes=["ones"])
        S.op("dve", lambda e: e.memset(hst[:], 0.0), writes=[("hst", c) for c in range(NCH)])
        S.op("dve", lambda e: e.memset(hal[:], 0.0), writes=[("hal", c) for c in range(NCH)])
        S.op("act", lambda e: e.activation(out=cc[:, 0:8], in_=cv[:, CV_LAM:CV_LAM + 8], func=AF.Exp, scale=-1.0),
             reads=["cv"], writes=["cc"])
        S.op("act", lambda e: e.activation(out=cc[:, 0:8], in_=cc[:, 0:8], func=AF.Ln, bias=1.0, scale=1.0),
             reads=["cc"], writes=["cc"])
        S.op("dve", lambda e: e.tensor_scalar(out=cc[:, 8:16], in0=cc[:, 0:8], scalar1=-16.0, scalar2=None, op0=ALU.mult),
             reads=["cc"], writes=["cc2"])
        S.op("dve", lambda e: e.tensor_scalar(out=cc[:, 0:8], in0=cc[:, 0:8], scalar1=-8.0, scalar2=None, op0=ALU.mult),
             reads=["cc", "cc2"], writes=["cc"])

        nrm_ctr = [0]

        def norm_T(src, srck, np_, c0, gidx):
            n = nrm_ctr[0]
            nrm_ctr[0] += 1
            slot = n % 2
            col = n % 8
            S.op("act", lambda e: e.activation(out=xsb[slot][0:np_, :], in_=src[0:np_, :], func=AF.Square, accum_out=ssq[0:np_, col:col + 1]),
                 reads=srck, writes=[("xsb", slot), ("ssq", col)])
            S.op("act", lambda e: e.activation(out=rstd[0:np_, col:col + 1], in_=ssq[0:np_, col:col + 1], func=AF.Sqrt, scale=1.0 / D, bias=EPS),
                 reads=[("ssq", col)], writes=[("rstd", col)])
            S.op("dve", lambda e: e.reciprocal(out=rstd[0:np_, col:col + 1], in_=rstd[0:np_, col:col + 1]),
                 reads=[("rstd", col)], writes=[("rstd", col)])
            S.op("dve", lambda e: e.scalar_tensor_tensor(out=xsb[slot][0:np_, :], in0=src[0:np_, :], scalar=rstd[0:np_, col:col + 1],
                                                         in1=gng[0:np_, gidx, :], op0=ALU.mult, op1=ALU.mult),
                 reads=list(srck) + [("rstd", col), "gng"], writes=[("xsb", slot)])
            for half in range(2):
                def tr(e, half=half):
                    for j in range(8):
                        k = half * 8 + j
                        mm = e.transpose(out=ptr[half][:, j, 0:np_], in_=xsb[slot][0:np_, k * 128:(k + 1) * 128], identity=ident[0:np_, 0:np_])
                    return mm
                S.op("pe", tr, reads=[("xsb", slot), "ident"], writes=[("ptr", half)])
                wk = [("xnT", half * 8 + j, c0) for j in range(8)]
                if half == 0:
                    S.op("act", lambda e: e.activation(out=xnT[:, 0:8, c0:c0 + np_], in_=ptr[0][:, :, 0:np_], func=AF.Copy),
                         reads=[("ptr", 0)], writes=wk)
                else:
                    S.op("dve", lambda e: e.tensor_copy(out=xnT[:, 8:16, c0:c0 + np_], in_=ptr[1][:, :, 0:np_]),
                         reads=[("ptr", 1)], writes=wk)
            return col

        allx = [("xnT", k, HAL + s * 128) for k in range(16) for s in range(4)]
        allxh = allx + [("xnT", k, 0) for k in range(16)]

        wr_ctr = [0]

        def load_piece(src_ap, nk, ncol):
            slot = wr_ctr[0] % WS
            wr_ctr[0] += 1
            view = wring[slot][:, 0:nk * ncol].rearrange("p (k n) -> p k n", k=nk)
            S.dma("pool", lambda e: e.dma_start(out=view, in_=src_ap), f"wr{slot}", writes=[("wr", slot)])
            return view, ("wr", slot)

        tmp_ctr = [0]

        def nexttmp():
            i = tmp_ctr[0] % NT
            tmp_ctr[0] += 1
            return i

        class Pool_:
            def __init__(self, items):
                self.free = list(items)

            def get(self, wide=False):
                for i, it in enumerate(self.free):
                    if (it[2] >= HAL + TT) == wide:
                        return self.free.pop(i)
                for i, it in enumerate(self.free):
                    if it[2] >= HAL + TT:
                        return self.free.pop(i)
                raise RuntimeError("scratch pool exhausted")

            def put(self, it):
                self.free.append(it)

        def lru_batch(chunks, wsel, pool, xcb_of, aux_eng):
            st = {}
            for c in chunks:
                wview, wkey, wcol0 = wsel(c)
                b = nextbank()

                def mm(e, b=b, wview=wview, wcol0=wcol0):
                    for k in range(16):
                        m = e.matmul(pb[b][:], lhsT=wview[:, k, wcol0:wcol0 + 128], rhs=xnT[:, k, HAL:HAL + TT], start=(k == 0), stop=(k == 15))
                    return m
                S.op("pe", mm, reads=[wkey] + allx, writes=[("pb", b)])
                XL = pool.get(wide=True)
                XC = pool.get()
                xl, xlk = XL[0], XL[1]
                xc, xck = XC[0], XC[1]
                S.op("act", lambda e, b=b, xl=xl: e.activation(out=xl[:, HAL:HAL + TT], in_=pb[b][:], func=AF.Copy), reads=[("pb", b)], writes=[xlk])
                S.op("dve", lambda e, xl=xl, c=c: e.tensor_copy(out=xl[:, HAL - 3:HAL], in_=hal[:, c, 0:3]), reads=[("hal", c), xlk], writes=[xlk])
                w0 = CV_C4W + c * 4
                S.op("dve", lambda e, xl=xl, xc=xc, w0=w0, c=c: e.tensor_scalar(out=xc[:, 0:TT], in0=xl[:, HAL - 3:HAL - 3 + TT], scalar1=cv[:, w0:w0 + 1],
                                                                              scalar2=cv[:, CV_CB + c:CV_CB + c + 1], op0=ALU.mult, op1=ALU.add),
                     reads=[xlk, "cv"], writes=[xck])
                for k in range(1, 4):
                    S.op("dve", lambda e, xl=xl, xc=xc, w0=w0, k=k: e.scalar_tensor_tensor(out=xc[:, 0:TT], in0=xl[:, HAL - 3 + k:HAL - 3 + k + TT],
                                                                                         scalar=cv[:, w0 + k:w0 + k + 1], in1=xc[:, 0:TT], op0=ALU.mult, op1=ALU.add),
                         reads=[xlk, xck, "cv"], writes=[xck])
                S.op("dve", lambda e, xl=xl, c=c: e.tensor_copy(out=hal[:, c, 0:3], in_=xl[:, HAL + TT - 3:HAL + TT]), reads=[xlk], writes=[("hal", c)])
                xb, xbk = xcb_of(c)
                if aux_eng == "pool":
                    S.op("pool", lambda e, xc=xc, xb=xb: e.tensor_copy(out=xb[:], in_=xc[:, 0:TT]), reads=[xck], writes=[xbk])
                else:
                    S.op("act", lambda e, xc=xc, xb=xb: e.activation(out=xb[:], in_=xc[:, 0:TT], func=AF.Copy), reads=[xck], writes=[xbk])
                st[c] = dict(XL=XL, XC=XC, xb=xb, xbk=xbk)
            for c in chunks:
                d = st[c]
                xb, xbk = d["xb"], d["xbk"]
                ba = nextbank()
                S.op("pe", lambda e, ba=ba, c=c, xb=xb: e.matmul(pb[ba][:], lhsT=gw[:, c, :], rhs=xb[:], start=True, stop=True), reads=["gw", xbk], writes=[("pb", ba)])
                bx = nextbank()
                S.op("pe", lambda e, bx=bx, c=c, xb=xb: e.matmul(pb[bx][:], lhsT=gw[:, 8 + c, :], rhs=xb[:], start=True, stop=True), reads=["gw", xbk], writes=[("pb", bx)])
                RR = d["XL"]
                II = pool.get()
                r_, rk = RR[0], RR[1]
                i_, ik = II[0], II[1]
                xc, xck = d["XC"][0], d["XC"][1]
                S.op("act", lambda e, ba=ba, r_=r_, c=c: e.activation(out=r_[:, 0:TT], in_=pb[ba][:], func=AF.Sigmoid, bias=cv[:, CV_GAB + c:CV_GAB + c + 1]),
                     reads=[("pb", ba), "cv"], writes=[rk])
                S.op("act", lambda e, bx=bx, i_=i_, c=c: e.activation(out=i_[:, 0:TT], in_=pb[bx][:], func=AF.Sigmoid, bias=cv[:, CV_GXB + c:CV_GXB + c + 1]),
                     reads=[("pb", bx), "cv"], writes=[ik])
                S.op(aux_eng, lambda e, i_=i_, xc=xc: e.tensor_tensor(out=i_[:, 0:TT], in0=i_[:, 0:TT], in1=xc[:, 0:TT], op=ALU.mult),
                     reads=[ik, xck], writes=[ik])
                d["II"] = II
            for c in chunks:
                d = st[c]
                AA = pool.get()
                d["AA"] = AA
                r_, rk = d["XL"][0], d["XL"][1]
                a_, ak = AA[0], AA[1]
                S.op("act", lambda e, r_=r_, a_=a_, c=c: e.activation(out=a_[:, 0:TT], in_=r_[:, 0:TT], func=AF.Exp, scale=cc[:, c:c + 1]),
                     reads=[rk, "cc"], writes=[ak])
                S.op("act", lambda e, r_=r_, c=c: e.activation(out=r_[:, 0:TT], in_=r_[:, 0:TT], func=AF.Exp, scale=cc[:, 8 + c:9 + c]),
                     reads=[rk, "cc2"], writes=[rk])
            for c in chunks:
                d = st[c]
                r_, rk = d["XL"][0], d["XL"][1]
                i_, ik = d["II"][0], d["II"][1]
                a_, ak = d["AA"][0], d["AA"][1]
                xc, xck = d["XC"][0], d["XC"][1]
                S.op("act", lambda e, r_=r_: e.activation(out=r_[:, 0:TT], in_=r_[:, 0:TT], func=AF.Sqrt, scale=-1.0, bias=1.0), reads=[rk], writes=[rk])
                S.op(aux_eng, lambda e, i_=i_, r_=r_: e.tensor_tensor(out=i_[:, 0:TT], in0=i_[:, 0:TT], in1=r_[:, 0:TT], op=ALU.mult),
                     reads=[ik, rk], writes=[ik])
                S.op("dve", lambda e, a_=a_, i_=i_, xc=xc, c=c: e.tensor_tensor_scan(out=xc[:, 0:TT], data0=a_[:, 0:TT], data1=i_[:, 0:TT],
                                                                                    initial=hst[:, c:c + 1], op0=ALU.mult, op1=ALU.add),
                     reads=[ak, ik, ("hst", c)], writes=[xck])
                pool.put(d["XL"])
                pool.put(d["II"])
                pool.put(d["AA"])
            return {c: st[c]["XC"] for c in chunks}

        def save_halo():
            S.op("dve", lambda e: e.tensor_copy(out=xh[:], in_=xnT[:, :, HAL + TT - HAL:HAL + TT]),
                 reads=[("xnT", k, HAL + 384) for k in range(16)], writes=["xh"])

        xst_ctr = [0]
        ppool = Pool_([(ptmp[i], ("ptmp", i), HAL + TT) for i in range(NPT)])

        def prefix_tile(pt):
            for s in range(4):
                slot = xst_ctr[0] % 2
                xst_ctr[0] += 1
                r0 = pt * TT + s * 128
                S.dma("sp", lambda e, slot=slot, r0=r0: e.dma_start(out=xst[slot][:], in_=xs_d[r0:r0 + 128, :]), f"xst{slot}", writes=[("xst", slot)])
                norm_T(xst[slot], [("xst", slot)], 128, HAL + s * 128, 0)
            if pt == NPRE - 1:
                save_halo()
            hs = lru_batch(list(range(NCH)), lambda c: (wlru, ("wlru", c // 4), c * 128), ppool, lambda c: (pxcb[c], ("pxcb", c)), "pool")
            for c in range(NCH):
                H = hs[c]
                S.op("dve", lambda e, h=H[0], c=c: e.tensor_scalar(out=hst[:, c:c + 1], in0=h[:, TT - 1:TT], scalar1=cv[:, CV_FLAG + pt:CV_FLAG + pt + 1],
                                                                   scalar2=None, op0=ALU.mult),
                     reads=[H[1], "cv"], writes=[("hst", c)])
                ppool.put(H)

        dg_ctr = [0]

        def rkeys(s):
            return [("R", s, j) for j in range(4)]

        import os
        STOP = int(os.environ.get("STOP_PHASE", "99"))

        def store_R(ft):
            for s in range(4):
                S.dma("sp", lambda e, s=s: e.dma_start(out=out_d[ft * TT + s * 128: ft * TT + (s + 1) * 128, :], in_=R[s][:]),
                      f"R{s}", reads=rkeys(s), writes=[("out", ft, s)])

        def full_tile(ft):
            t0 = (NPRE + ft) * TT
            S.op("dve", lambda e: e.tensor_copy(out=xnT[:, :, 0:HAL], in_=xh[:]), reads=["xh"], writes=[("xnT", k, 0) for k in range(16)])
            for s in range(4):
                S.dma("sp", lambda e, s=s: e.dma_start(out=R[s][:], in_=xs_d[t0 + s * 128:t0 + (s + 1) * 128, :]), f"R{s}", writes=rkeys(s))
            for s in range(4):
                norm_T(R[s], rkeys(s), 128, HAL + s * 128, 0)
            save_halo()
            if STOP <= 1:
                return store_R(ft)
            fpool = Pool_([(tmp[i], ("tmp", i), HAL + TT) for i in range(NT)] + [(gc[:, c, :], ("gc", c), TT) for c in range(NCH)]
                          + [(lnt[i], ("lnt", i), TT) for i in range(4)])
            for g in range(2):
                wx, wxk = load_piece(win_v[:, :, g * 512:(g + 1) * 512], 16, 512)
                wy, wyk = load_piece(win_v[:, :, 1024 + g * 512:1024 + (g + 1) * 512], 16, 512)
                for pr in range(2):
                    chunks = [g * 4 + pr * 2, g * 4 + pr * 2 + 1]
                    hs = lru_batch(chunks, lambda c, wx=wx, wxk=wxk: (wx, wxk, (c % 4) * 128), fpool, lambda c: (xcbs[c % 2], ("xcb", c % 2)), "dve")
                    for c in chunks:
                        cl = c % 4
                        H = hs[c]
                        h = H[0]
                        S.op("dve", lambda e, h=h, c=c: e.tensor_copy(out=hst[:, c:c + 1], in_=h[:, TT - 1:TT]), reads=[H[1]], writes=[("hst", c)])
                        by = nextbank()

                        def mmy(e, by=by, wy=wy, cl=cl):
                            for k in range(16):
                                m = e.matmul(pb[by][:], lhsT=wy[:, k, cl * 128:(cl + 1) * 128], rhs=xnT[:, k, HAL:HAL + TT], start=(k == 0), stop=(k == 15))
                            return m
                        S.op("pe", mmy, reads=[wyk] + allx, writes=[("pb", by)])
                        GY = fpool.get()
                        gy = GY[0]
                        S.op("act", lambda e, by=by, gy=gy: e.activation(out=gy[:, 0:TT], in_=pb[by][:], func=AF.Gelu_apprx_tanh),
                             reads=[("pb", by)], writes=[GY[1]])
                        S.op("dve", lambda e, h=h, gy=gy, c=c: e.tensor_tensor(out=oT[:, c, HAL:HAL + TT], in0=h[:, 0:TT], in1=gy[:, 0:TT], op=ALU.mult),
                             reads=[H[1], GY[1]], writes=[("oT", c)])
                        fpool.put(H)
                        fpool.put(GY)
            if STOP <= 2:
                return store_R(ft)
            pending = None
            for g in range(2):
                wv, wvk = load_piece(win_v[:, :, 2048 + g * 512:2048 + (g + 1) * 512], 16, 512)
                wg, wgk = load_piece(win_v[:, :, 3072 + g * 512:3072 + (g + 1) * 512], 16, 512)
                for cl in range(4):
                    c = g * 4 + cl
                    gk = ("oT", 8 + c)
                    ds = 0
                    dg_ctr[0] += 1
                    S.dma("pool", lambda e, ds=ds, c=c: e.dma_start(out=dgb[ds][:], in_=dg_d[c].rearrange("p (k n) -> p k n", k=31),
                                                                    max_dma_last_dim=4096),
                          f"dg{ds}", writes=[("dgb", ds)])
                    bgm = nextbank()

                    def mmg(e, b=bgm, wg=wg, cl=cl):
                        for k in range(16):
                            m = e.matmul(pb[b][:], lhsT=wg[:, k, cl * 128:(cl + 1) * 128], rhs=xnT[:, k, HAL:HAL + TT], start=(k == 0), stop=(k == 15))
                        return m
                    S.op("pe", mmg, reads=[wgk] + allx, writes=[("pb", bgm)])
                    bgh = nextbank()

                    def mmgh(e, b=bgh, wg=wg, wv=wv, cl=cl):
                        for k in range(16):
                            m = e.matmul(pb[b][:, 0:HAL], lhsT=wg[:, k, cl * 128:(cl + 1) * 128], rhs=xnT[:, k, 0:HAL], start=(k == 0), stop=(k == 15))
                        for k in range(16):
                            m = e.matmul(pb[b][:, 64:64 + HAL], lhsT=wv[:, k, cl * 128:(cl + 1) * 128], rhs=xnT[:, k, 0:HAL],
                                         start=(k == 0), stop=(k == 15), skip_group_check=True)
                        return m
                    S.op("pe", mmgh, reads=[wgk, wvk] + allxh, writes=[("pb", bgh)])
                    isg = nexttmp()
                    sg = tmp[isg]
                    S.op("act", lambda e, b=bgm, sg=sg: e.activation(out=sg[:, HAL:HAL + TT], in_=pb[b][:], func=AF.Sigmoid),
                         reads=[("pb", bgm)], writes=[("tmp", isg)])
                    S.op("act", lambda e, b=bgh, sg=sg: e.activation(out=sg[:, 0:HAL], in_=pb[b][:, 0:HAL], func=AF.Sigmoid),
                         reads=[("pb", bgh), ("tmp", isg)], writes=[("tmp", isg)])
                    bvm = nextbank()

                    def mmv(e, b=bvm, wv=wv, cl=cl):
                        for k in range(16):
                            m = e.matmul(pb[b][:], lhsT=wv[:, k, cl * 128:(cl + 1) * 128], rhs=xnT[:, k, HAL:HAL + TT], start=(k == 0), stop=(k == 15))
                        return m
                    S.op("pe", mmv, reads=[wvk] + allx, writes=[("pb", bvm)])
                    S.op("dve", lambda e, b=bvm, sg=sg, c=c: e.tensor_tensor(out=oT[:, 8 + c, HAL:HAL + TT], in0=pb[b][:], in1=sg[:, HAL:HAL + TT],
                                                                             op=ALU.mult),
                         reads=[("pb", bvm), ("tmp", isg)], writes=[gk])
                    S.op("dve", lambda e, b=bgh, sg=sg, c=c: e.tensor_tensor(out=oT[:, 8 + c, 0:HAL], in0=pb[b][:, 64:64 + HAL], in1=sg[:, 0:HAL],
                                                                             op=ALU.mult),
                         reads=[("pb", bgh), ("tmp", isg), gk], writes=[gk])
                    bc = nextbank()

                    def mmc(e, b=bc, ds=ds, c=c):
                        for k in range(31):
                            m = e.matmul(pb[b][:], lhsT=dgb[ds][:, k, :], rhs=oT[:, 8 + c, HAL - 30 + k:HAL - 30 + k + TT], start=(k == 0), stop=(k == 30))
                        return m
                    S.op("pe", mmc, reads=[("dgb", ds), gk], writes=[("pb", bc)])
                    if pending is not None:
                        pending()
                    bcol = cv[:, CV_DWB + c:CV_DWB + c + 1]
                    S.op("act", lambda e, b=bc, c=c, bcol=bcol: e.activation(out=gc[:, c, :], in_=pb[b][:], func=AF.Identity, bias=bcol),
                         reads=[("pb", bc), "cv"], writes=[("gc", c)])
                    S.op("act", lambda e, b=bc, bcol=bcol: e.activation(out=gstat[0][:], in_=pb[b][:], func=AF.Identity, bias=bcol),
                         reads=[("pb", bc), "cv"], writes=[("gstat", 0)])
                    S.op("act", lambda e, b=bc, bcol=bcol: e.activation(out=gstat[1][:], in_=pb[b][:], func=AF.Square, bias=bcol),
                         reads=[("pb", bc), "cv"], writes=[("gstat", 1)])

                    def stat_mm(c=c):
                        S.op("pe", lambda e, c=c: e.matmul(pstat[0][:], lhsT=ones[:], rhs=gstat[0][:], start=(c == 0), stop=(c == NCH - 1)),
                             reads=["ones", ("gstat", 0)], writes=[("pstat", 0)])
                        S.op("pe", lambda e, c=c: e.matmul(pstat[1][:], lhsT=ones[:], rhs=gstat[1][:], start=(c == 0), stop=(c == NCH - 1)),
                             reads=["ones", ("gstat", 1)], writes=[("pstat", 1)])
                    pending = stat_mm
            pending()
            if STOP <= 3:
                return store_R(ft)
            mean, msq, rs, nmr = lnt
            S.op("act", lambda e: e.activation(out=mean[:], in_=pstat[0][:], func=AF.Copy, scale=1.0 / 1024), reads=[("pstat", 0)], writes=[("lnt", 0)])
            S.op("act", lambda e: e.activation(out=msq[:], in_=pstat[0][:], func=AF.Square, scale=1.0 / 1024), reads=[("pstat", 0)], writes=[("lnt", 1)])
            S.op("dve", lambda e: e.scalar_tensor_tensor(out=rs[:], in0=pstat[1][:], scalar=1.0 / 1024, in1=msq[:], op0=ALU.mult, op1=ALU.subtract),
                 reads=[("pstat", 1), ("lnt", 1)], writes=[("lnt", 2)])
            S.op("act", lambda e: e.activation(out=rs[:], in_=rs[:], func=AF.Sqrt, bias=EPS, scale=1.0), reads=[("lnt", 2)], writes=[("lnt", 2)])
            S.op("dve", lambda e: e.reciprocal(out=rs[:], in_=rs[:]), reads=[("lnt", 2)], writes=[("lnt", 2)])
            S.op("dve", lambda e: e.scalar_tensor_tensor(out=nmr[:], in0=mean[:], scalar=-1.0, in1=rs[:], op0=ALU.mult, op1=ALU.mult),
                 reads=[("lnt", 0), ("lnt", 2)], writes=[("lnt", 3)])
            for c in range(NCH):
                it = nexttmp()
                t_ = tmp[it]
                S.op("dve", lambda e, c=c, t_=t_: e.tensor_tensor(out=t_[:, 0:TT], in0=gc[:, c, :], in1=rs[:], op=ALU.mult),
                     reads=[("gc", c), ("lnt", 2)], writes=[("tmp", it)])
                S.op("dve", lambda e, t_=t_: e.tensor_tensor(out=t_[:, 0:TT], in0=t_[:, 0:TT], in1=nmr[:], op=ALU.add),
                     reads=[("tmp", it), ("lnt", 3)], writes=[("tmp", it)])
                S.op("act", lambda e, c=c, t_=t_: e.activation(out=oT[:, 8 + c, HAL:HAL + TT], in_=t_[:, 0:TT], func=AF.Silu,
                                                               scale=cv[:, CV_LNG + c:CV_LNG + c + 1], bias=cv[:, CV_LNB + c:CV_LNB + c + 1]),
                     reads=[("tmp", it), "cv"], writes=[("oT", 8 + c)])
            if STOP <= 4:
                return store_R(ft)
            for j in range(4):
                wo, wok = load_piece(wout_v[:, :, j * 512:(j + 1) * 512], 16, 512)
                for s in range(4):
                    b = nextbank()

                    def mmo(e, b=b, s=s, wo=wo):
                        for k in range(16):
                            m = e.matmul(pb[b][:], lhsT=oT[:, k, HAL + s * 128:HAL + (s + 1) * 128], rhs=wo[:, k, :], start=(k == 0), stop=(k == 15))
                        return m
                    S.op("pe", mmo, reads=[wok] + [("oT", k) for k in range(16)], writes=[("pb", b)])
                    S.op("dve", lambda e, b=b, s=s, j=j: e.tensor_tensor(out=R[s][:, j * 512:(j + 1) * 512], in0=pb[b][:],
                                                                         in1=R[s][:, j * 512:(j + 1) * 512], op=ALU.add),
                         reads=[("pb", b), ("R", s, j)], writes=[("R", s, j)])
            if STOP <= 5:
                return store_R(ft)
            for s in range(4):
                norm_T(R[s], rkeys(s), 128, HAL + s * 128, 1)
            def w1_phase(grp):
                a0 = (grp % 4) * 4
                w1p, w1k = load_piece(w1_v[:, :, grp * 512:(grp + 1) * 512], 16, 512)
                for cl in range(4):
                    b = nextbank()

                    def mm1(e, b=b, w1p=w1p, cl=cl):
                        for k in range(16):
                            m = e.matmul(pb[b][:], lhsT=w1p[:, k, cl * 128:(cl + 1) * 128], rhs=xnT[:, k, HAL:HAL + TT], start=(k == 0), stop=(k == 15))
                        return m
                    S.op("pe", mm1, reads=[w1k] + allx, writes=[("pb", b)])
                    rs_ = cl % 2
                    S.op("act", lambda e, b=b, rs_=rs_: e.activation(out=rl[rs_][:], in_=pb[b][:], func=AF.Relu), reads=[("pb", b)], writes=[("rl", rs_)])
                    S.op("dve", lambda e, b=b, rs_=rs_, a0=a0, cl=cl: e.scalar_tensor_tensor(
                        out=oT[:, a0 + cl, HAL:HAL + TT], in0=pb[b][:], scalar=0.0, in1=rl[rs_][:], op0=ALU.max, op1=ALU.mult),
                        reads=[("pb", b), ("rl", rs_)], writes=[("oT", a0 + cl)])

            def w2_phase(grp):
                a0 = (grp % 4) * 4
                w2p, w2k = load_piece(w2_v[:, grp * 4:(grp + 1) * 4, :], 4, D)
                for s in range(4):
                    for j in range(4):
                        b = nextbank()

                        def mm2(e, b=b, s=s, j=j, a0=a0, w2p=w2p):
                            for fl in range(4):
                                m = e.matmul(pb[b][:], lhsT=oT[:, a0 + fl, HAL + s * 128:HAL + (s + 1) * 128], rhs=w2p[:, fl, j * 512:(j + 1) * 512],
                                             start=(fl == 0), stop=(fl == 3))
                            return m
                        S.op("pe", mm2, reads=[w2k] + [("oT", a0 + fl) for fl in range(4)], writes=[("pb", b)])
                        S.op("dve", lambda e, b=b, s=s, j=j: e.tensor_tensor(out=R[s][:, j * 512:(j + 1) * 512], in0=pb[b][:],
                                                                             in1=R[s][:, j * 512:(j + 1) * 512], op=ALU.add),
                             reads=[("pb", b), ("R", s, j)], writes=[("R", s, j)])
            NG = 16
            w1_phase(0)
            for grp in range(NG):
                if grp + 1 < NG:
                    w1_phase(grp + 1)
                w2_phase(grp)
            if STOP <= 6:
                return store_R(ft)
            for s in range(4):
                n = nrm_ctr[0]
                nrm_ctr[0] += 1
                slot = n % 2
                col = n % 8
                S.op("act", lambda e, s=s, slot=slot, col=col: e.activation(out=xsb[slot][:], in_=R[s][:], func=AF.Square, accum_out=ssq[:, col:col + 1]),
                     reads=rkeys(s), writes=[("xsb", slot), ("ssq", col)])
                S.op("act", lambda e, col=col: e.activation(out=rstd[:, col:col + 1], in_=ssq[:, col:col + 1], func=AF.Sqrt, scale=1.0 / D, bias=EPS),
                     reads=[("ssq", col)], writes=[("rstd", col)])
                S.op("dve", lambda e, col=col: e.reciprocal(out=rstd[:, col:col + 1], in_=rstd[:, col:col + 1]), reads=[("rstd", col)], writes=[("rstd", col)])
                S.op("dve", lambda e, s=s, col=col: e.scalar_tensor_tensor(out=R[s][:], in0=R[s][:], scalar=rstd[:, col:col + 1], in1=gng[:, 2, :],
                                                                           op0=ALU.mult, op1=ALU.mult),
                     reads=rkeys(s) + [("rstd", col), "gng"], writes=rkeys(s))
                S.dma("sp", lambda e, s=s: e.dma_start(out=out_d[ft * TT + s * 128: ft * TT + (s + 1) * 128, :], in_=R[s][:]),
                      f"R{s}", reads=rkeys(s), writes=[("out", ft, s)])

        for pt in range(NPRE):
            prefix_tile(pt)
        if NPRE > 0:
            S.barrier()
        for ft in range(NFULL):
            full_tile(ft)
        outkeys = [("out", ft, s) for ft in range(NFULL) for s in range(4)]
        S.op("sp", lambda e: e.nop(), reads=outkeys)

        chans = sorted(set(o["chan"] for o in S.ops if o["kind"] == "dma"))
        sems = {e: es.enter_context(nc.semaphore(f"s_{e}")) for e in ["pe", "act", "dve", "pool", "sp"]}
        chan_sems = {c: es.enter_context(nc.semaphore(f"c_{c}")) for c in chans}
        block = es.enter_context(nc.Block())
        S.emit(block, sems, chan_sems)
        print("ops", len(S.ops), "sbuf bytes", off[0], "counts", {k: v for k, v in S.final_counts.items()})
    return nc


_CACHE = {}


def _host_inputs(inp, NPRE=12, NFULL=4, seq=8192, batch=2, chunks=4):
    f = np.float32
    x = np.asarray(inp["x"], f)
    cvs = []
    per = NFULL * TT

    def pc(v):
        return np.ascontiguousarray(np.asarray(v, f).reshape(NCH, 128).T)
    base = np.zeros((128, CV_N), f)
    base[:, CV_CB:CV_CB + 8] = pc(inp["lru_conv_b"][0])
    base[:, CV_GAB:CV_GAB + 8] = pc(inp["lru_gate_a_b"][0])
    base[:, CV_GXB:CV_GXB + 8] = pc(inp["lru_gate_x_b"][0])
    base[:, CV_LAM:CV_LAM + 8] = pc(inp["lru_lambda"][0])
    base[:, CV_DWB:CV_DWB + 8] = pc(inp["conf_dw_b"][0])
    base[:, CV_LNG:CV_LNG + 8] = pc(inp["conf_ln_g"][0])
    base[:, CV_LNB:CV_LNB + 8] = pc(inp["conf_ln_b"][0])
    c4 = np.asarray(inp["lru_conv_w"][0], f)
    base[:, CV_C4W:CV_C4W + 32] = c4.reshape(4, NCH, 128).transpose(2, 1, 0).reshape(128, 32)
    gng = np.stack([np.broadcast_to(np.asarray(inp["mix_norm_g"][0], f), (128, D)),
                    np.broadcast_to(np.asarray(inp["mlp_norm_g"][0], f), (128, D)),
                    np.broadcast_to(np.asarray(inp["final_norm_g"], f), (128, D))], axis=1)
    gng = np.ascontiguousarray(gng)
    gw = np.zeros((128, 16, 128), f)
    for gi, name in enumerate(["lru_gate_a_w", "lru_gate_x_w"]):
        w = np.asarray(inp[name][0], f)
        for c in range(NCH):
            for hh in range(2):
                gw[hh * 64:(hh + 1) * 64, gi * 8 + c, hh * 64:(hh + 1) * 64] = w[c * 2 + hh]
    dw = np.asarray(inp["conf_dw_w"][0], f)
    dg = np.zeros((NCH, 128, 31, 128), f)
    ar = np.arange(128)
    for c in range(NCH):
        dg[c, ar, :, ar] = dw[:, c * 128:(c + 1) * 128].T
    dg = dg.reshape(NCH, 128, 31 * 128)
    shared = dict(gng=gng, gw=gw, dg=dg,
                  w_in=np.ascontiguousarray(np.asarray(inp["w_in"][0], f)),
                  w_out=np.ascontiguousarray(np.asarray(inp["w_out"][0], f)),
                  w1=np.ascontiguousarray(np.asarray(inp["mlp_w1"][0], f)),
                  w2=np.ascontiguousarray(np.asarray(inp["mlp_w2"][0], f)))
    maps = []
    for core in range(batch * chunks):
        b, q = divmod(core, chunks)
        npad = (NPRE * TT) - q * per
        xs = np.zeros(((NPRE + NFULL) * TT, D), f)
        xs[npad:] = x[b, 0:(q + 1) * per]
        cvc = base.copy()
        for pt in range(NPRE):
            cvc[:, CV_FLAG + pt] = 1.0 if pt * TT >= npad else 0.0
        m = dict(shared)
        m["xs"] = xs
        m["cv"] = cvc
        maps.append(m)
    return maps


def kernel(**inputs):
    NPRE, NFULL = 12, 4
    key = (NPRE, NFULL)
    if key not in _CACHE:
        _CACHE[key] = build_program(NPRE, NFULL)
    nc = _CACHE[key]
    maps = _host_inputs(inputs, NPRE, NFULL)
    res = run_bass_kernel_spmd(nc, maps, core_ids=list(range(8)))
    out = np.zeros((2, 8192, D), np.float32)
    for core in range(8):
        b, q = divmod(core, 4)
        out[b, q * 2048:(q + 1) * 2048] = res.results[core]["out"]
    return out
```

```python
import numpy as np
import concourse.bass as bass
import concourse.mybir as mybir
from concourse.bass_utils import run_bass_kernel_spmd

F32 = mybir.dt.float32
BF16 = mybir.dt.bfloat16
AF = mybir.ActivationFunctionType
ALU = mybir.AluOpType

D = 2048
DIN = 4096
DFF = 8192
TT = 512
HAL = 32
NCH = 8
EPS = 1e-6
CV_CB, CV_GAB, CV_GXB, CV_LAM, CV_DWB, CV_LNG, CV_LNB = 0, 8, 16, 24, 32, 40, 48
CV_C4W = 56
CV_FLAG = 88
CV_N = 104


class Sched:
    def __init__(self, nc):
        self.nc = nc
        self.ops = []
        self.last_w = {}
        self.readers = {}
        self.eng_objs = {"pe": nc.tensor, "act": nc.scalar, "dve": nc.vector, "pool": nc.gpsimd, "sp": nc.sync}

    def _deps(self, eng, reads, writes):
        deps = set()
        for k in reads:
            w = self.last_w.get(k)
            if w is not None:
                deps.add(w)
        for k in writes:
            w = self.last_w.get(k)
            if w is not None:
                deps.add(w)
            for r in self.readers.get(k, ()):
                deps.add(r)
        return deps

    def _add(self, eng, fn, reads, writes, kind, chan=None):
        oid = len(self.ops)
        deps = self._deps(eng, reads, writes)
        keep = set()
        raw = set(self.last_w.get(k) for k in reads if self.last_w.get(k) is not None)
        for d in deps:
            od = self.ops[d]
            if od["kind"] == "dma" or kind == "dma":
                keep.add(d)
            elif od["eng"] != eng:
                keep.add(d)
            elif d in raw and eng != "pe":
                keep.add(d)
        self.ops.append(dict(eng=eng, fn=fn, deps=keep, kind=kind, chan=chan, sig=False))
        for k in reads:
            self.readers.setdefault(k, []).append(oid)
        for k in writes:
            self.last_w[k] = oid
            self.readers[k] = []
        return oid

    def op(self, eng, fn, reads=(), writes=()):
        return self._add(eng, fn, list(reads), list(writes), "cmp")

    def dma(self, queue, fn, chan, reads=(), writes=()):
        return self._add(queue, fn, list(reads), list(writes), "dma", chan)

    def barrier(self):
        last = {}
        for i, o in enumerate(self.ops):
            if o["kind"] == "dma":
                last[("c", o["chan"])] = i
            else:
                last[("e", o["eng"])] = i
        deps = set(last.values())
        for eng in ["pe", "act", "dve", "pool", "sp"]:
            self.ops.append(dict(eng=eng, fn=lambda e: e.nop(), deps=set(deps), kind="cmp", chan=None, sig=False))

    def emit(self, block, sems, chan_sems):
        ops = self.ops
        for o in ops:
            for d in o["deps"]:
                ops[d]["sig"] = True
        cnt = {}
        for i, o in enumerate(ops):
            if o["kind"] == "dma":
                c = o["chan"]
                cnt[c] = cnt.get(c, 0) + 16
                o["ev"] = (("c", c), cnt[c])
            elif o["sig"]:
                e = o["eng"]
                cnt[e] = cnt.get(e, 0) + 1
                o["ev"] = (("e", e), cnt[e])
            else:
                o["ev"] = None
        const_total = {c: v for c, v in cnt.items() if isinstance(c, str) and c.startswith("const")}
        self.final_counts = cnt
        by_eng = {}
        for i, o in enumerate(ops):
            by_eng.setdefault(o["eng"], []).append(i)

        def semof(key):
            return chan_sems[key[1]] if key[0] == "c" else sems[key[1]]

        snap = {}
        nwaits = [0]

        def run_engine(ename, engobj):
            known = {}
            for i in by_eng.get(ename, []):
                o = ops[i]
                need = {}
                for d in o["deps"]:
                    key, val = ops[d]["ev"]
                    if key[0] == "c" and key[1] in const_total:
                        val = const_total[key[1]]
                    if need.get(key, 0) < val:
                        need[key] = val
                for key, val in sorted(need.items(), key=lambda kv: -kv[1]):
                    if known.get(key, 0) >= val:
                        continue
                    engobj.wait_ge(semof(key), val)
                    nwaits[0] += 1
                    known[key] = val
                    sn = snap.get((key, val))
                    if sn:
                        for k2, v2 in sn.items():
                            if known.get(k2, 0) < v2:
                                known[k2] = v2
                inst = o["fn"](engobj)
                if o["ev"] is not None:
                    key, val = o["ev"]
                    if o["kind"] == "dma":
                        inst.then_inc(semof(key), 16)
                    else:
                        inst.then_inc(semof(key), 1)
                        snap[(key, val)] = dict(known)
                        known[key] = max(known.get(key, 0), 0)

        @block.sync
        def _(e):
            run_engine("sp", e)

        @block.gpsimd
        def _(e):
            run_engine("pool", e)

        @block.scalar
        def _(e):
            run_engine("act", e)

        @block.vector
        def _(e):
            run_engine("dve", e)

        @block.tensor
        def _(e):
            run_engine("pe", e)


def build_program(NPRE, NFULL):
    nc = bass.Bass("TRN2", target_bir_lowering=False)
    NTILE = NPRE + NFULL
    xs_d = nc.dram_tensor("xs", [NTILE * TT, D], F32, kind="ExternalInput").ap()
    cv_d = nc.dram_tensor("cv", [128, CV_N], F32, kind="ExternalInput").ap()
    gng_d = nc.dram_tensor("gng", [128, 3, D], F32, kind="ExternalInput").ap()
    gw_d = nc.dram_tensor("gw", [128, 16, 128], F32, kind="ExternalInput").ap()
    dg_d = nc.dram_tensor("dg", [NCH, 128, 31 * 128], F32, kind="ExternalInput").ap()
    win_d = nc.dram_tensor("w_in", [D, DIN], F32, kind="ExternalInput").ap()
    wout_d = nc.dram_tensor("w_out", [D, D], F32, kind="ExternalInput").ap()
    w1_d = nc.dram_tensor("w1", [D, DFF], F32, kind="ExternalInput").ap()
    w2_d = nc.dram_tensor("w2", [DFF, D], F32, kind="ExternalInput").ap()
    out_d = nc.dram_tensor("out", [NFULL * TT, D], F32, kind="ExternalOutput").ap()

    S = Sched(nc)
    import contextlib
    es = contextlib.ExitStack()
    off = [16384]

    def sb(name, shape, dt, at=None):
        nbytes = int(np.prod(shape[1:])) * (4 if dt == F32 else 2)
        nbytes = (nbytes + 63) // 64 * 64
        if at is None:
            at = off[0]
            off[0] += nbytes
        return nc.alloc_sbuf_tensor_at(name, shape, dt, offset=at), at + nbytes

    def sbp(name, shape, dt):
        return sb(name, shape, dt)[0]

    def ps(name, shape, dt):
        return es.enter_context(nc.psum_tensor(name, shape, dt))

    with es:
        cv = sbp("cv", [128, CV_N], F32)
        cc = sbp("cc", [128, 16], F32)
        gng = sbp("gng", [128, 3, D], F32)
        gw = sbp("gw", [128, 16, 128], BF16)
        ident = sbp("ident", [128, 128], BF16)
        identf = sbp("identf", [128, 128], F32)
        ones = sbp("ones", [128, 128], BF16)
        hst = sbp("hst", [128, NCH], F32)
        hal = sbp("hal", [128, NCH, 4], F32)
        xnT = sbp("xnT", [128, 16, HAL + TT], BF16)
        xsb = [sbp(f"xsb{i}", [128, D], BF16) for i in range(2)]
        ssq = sbp("ssq", [128, 8], F32)
        rstd = sbp("rstd", [128, 8], F32)
        NT = 5
        tmp = [sbp(f"tmp{i}", [128, HAL + TT], F32) for i in range(NT)]
        xcbs = [sbp(f"xcb{i}", [128, TT], BF16) for i in range(2)]
        xh = sbp("xh", [128, 16, HAL], BF16)
        base = off[0]
        wlru = sbp("wlru", [128, 16, 1024], BF16)
        xst = [sbp(f"xst{i}", [128, D], F32) for i in range(2)]
        NPT = 35
        ptmp = [sbp(f"ptmp{i}", [128, HAL + TT], F32) for i in range(NPT)]
        pxcb = [sbp(f"pxcb{i}", [128, TT], BF16) for i in range(NCH)]
        assert off[0] < 229000, off[0]
        off[0] = base
        WS = 3
        wring = [sbp(f"wr{i}", [128, 16 * 512], BF16) for i in range(WS)]
        R = [sbp(f"R{i}", [128, D], F32) for i in range(4)]
        oT = sbp("oT", [128, 16, HAL + TT], BF16)
        gc = sbp("gc", [128, NCH, TT], F32)
        gstat = [sbp(f"gstat{i}", [128, TT], BF16) for i in range(2)]
        dgb = [sbp(f"dgb{i}", [128, 31, 128], BF16) for i in range(2)]
        lnt = [sbp(f"lnt{i}", [128, TT], F32) for i in range(3)]
        rl = [sbp(f"rl{i}", [128, TT], BF16) for i in range(2)]
        assert off[0] < 229000, off[0]
        NB = 4
        pb = [ps(f"pb{i}", [128, 512], F32) for i in range(NB)]
        pstat = [ps(f"pst{i}", [128, 512], F32) for i in range(2)]
        ptr = [ps(f"ptr{i}", [128, 8, 128], BF16) for i in range(2)]
        bank_ctr = [0]

        def nextbank():
            b = bank_ctr[0] % NB
            bank_ctr[0] += 1
            return b

        S.dma("sp", lambda e: e.dma_start(out=cv[:], in_=cv_d), "const_sp", writes=["cv"])
        S.dma("sp", lambda e: e.dma_start(out=gng[:], in_=gng_d), "const_sp", writes=["gng"])
        S.dma("pool", lambda e: e.dma_start(out=gw[:], in_=gw_d), "const_pool", writes=["gw"])
        win_v = win_d.rearrange("(k p) n -> p k n", p=128)
        wout_v = wout_d.rearrange("(k p) n -> p k n", p=128)
        w1_v = w1_d.rearrange("(k p) n -> p k n", p=128)
        w2_v = w2_d.rearrange("(k p) n -> p k n", p=128)
        if NPRE > 0:
            for h in range(2):
                S.dma("pool", lambda e, h=h: e.dma_start(out=wlru[:, :, h * 512:(h + 1) * 512], in_=win_v[:, :, h * 512:(h + 1) * 512]),
                      "const_pool", writes=[("wlru", h)])
        S.op("pool", lambda e: e.memset(identf[:], 0.0), writes=["identf"])
        S.op("pool", lambda e: e.affine_select(out=identf[:], in_=identf[:], pattern=[[-1, 128]], compare_op=ALU.not_equal,
                                               fill=1.0, base=0, channel_multiplier=1), reads=["identf"], writes=["identf"])
        S.op("dve", lambda e: e.tensor_copy(out=ident[:], in_=identf[:]), reads=["identf"], writes=["ident"])
        S.op("dve", lambda e: e.memset(ones[:], 1.0), writes=["ones"])
        S.op("dve", lambda e: e.memset(hst[:], 0.0), writes=[("hst", c) for c in range(NCH)])
        S.op("dve", lambda e: e.memset(hal[:], 0.0), writes=[("hal", c) for c in range(NCH)])
        S.op("act", lambda e: e.activation(out=cc[:, 0:8], in_=cv[:, CV_LAM:CV_LAM + 8], func=AF.Exp, scale=-1.0),
             reads=["cv"], writes=["cc"])
        S.op("act", lambda e: e.activation(out=cc[:, 0:8], in_=cc[:, 0:8], func=AF.Ln, bias=1.0, scale=1.0),
             reads=["cc"], writes=["cc"])
        S.op("dve", lambda e: e.tensor_scalar(out=cc[:, 8:16], in0=cc[:, 0:8], scalar1=-16.0, scalar2=None, op0=ALU.mult),
             reads=["cc"], writes=["cc2"])
        S.op("dve", lambda e: e.tensor_scalar(out=cc[:, 0:8], in0=cc[:, 0:8], scalar1=-8.0, scalar2=None, op0=ALU.mult),
             reads=["cc", "cc2"], writes=["cc"])

        nrm_ctr = [0]

        def norm_front(src, srck, np_, gidx):
            n = nrm_ctr[0]
            nrm_ctr[0] += 1
            slot = n % 2
            col = n % 8
            S.op("act", lambda e: e.activation(out=xsb[slot][0:np_, :], in_=src[0:np_, :], func=AF.Square, accum_out=ssq[0:np_, col:col + 1]),
                 reads=srck, writes=[("xsb", slot), ("ssq", col)])
            S.op("act", lambda e: e.activation(out=rstd[0:np_, col:col + 1], in_=ssq[0:np_, col:col + 1], func=AF.Sqrt, scale=1.0 / D, bias=EPS),
                 reads=[("ssq", col)], writes=[("rstd", col)])
            S.op("dve", lambda e: e.reciprocal(out=rstd[0:np_, col:col + 1], in_=rstd[0:np_, col:col + 1]),
                 reads=[("rstd", col)], writes=[("rstd", col)])
            S.op("dve", lambda e: e.scalar_tensor_tensor(out=xsb[slot][0:np_, :], in0=src[0:np_, :], scalar=rstd[0:np_, col:col + 1],
                                                         in1=gng[0:np_, gidx, :], op0=ALU.mult, op1=ALU.mult),
                 reads=list(srck) + [("rstd", col), "gng"], writes=[("xsb", slot)])
            return slot

        def norm_back(slot, np_, c0):
            for half in range(2):
                def tr(e, half=half):
                    for j in range(8):
                        k = half * 8 + j
                        mm = e.transpose(out=ptr[half][:, j, 0:np_], in_=xsb[slot][0:np_, k * 128:(k + 1) * 128], identity=ident[0:np_, 0:np_])
                    return mm
                S.op("pe", tr, reads=[("xsb", slot), "ident"], writes=[("ptr", half)])
                wk = [("xnT", half * 8 + j, c0) for j in range(8)]
                if half == 0:
                    S.op("act", lambda e: e.activation(out=xnT[:, 0:8, c0:c0 + np_], in_=ptr[0][:, :, 0:np_], func=AF.Copy),
                         reads=[("ptr", 0)], writes=wk)
                else:
                    S.op("dve", lambda e: e.tensor_copy(out=xnT[:, 8:16, c0:c0 + np_], in_=ptr[1][:, :, 0:np_]),
                         reads=[("ptr", 1)], writes=wk)

        def norm_tile(srcs, gidx, pre=None):
            slots = {}
            for s in range(4):
                if pre is not None:
                    pre(s)
                slots[s] = norm_front(srcs[s][0], srcs[s][1], 128, gidx)
                if s >= 1:
                    norm_back(slots[s - 1], 128, HAL + (s - 1) * 128)
            norm_back(slots[3], 128, HAL + 3 * 128)

        allx = [("xnT", k, HAL + s * 128) for k in range(16) for s in range(4)]
        allxh = allx + [("xnT", k, 0) for k in range(16)]

        wr_ctr = [0]

        def load_piece(src_ap, nk, ncol):
            slot = wr_ctr[0] % WS
            wr_ctr[0] += 1
            view = wring[slot][:, 0:nk * ncol].rearrange("p (k n) -> p k n", k=nk)
            S.dma("pool", lambda e: e.dma_start(out=view, in_=src_ap), f"wr{slot}", writes=[("wr", slot)])
            return view, ("wr", slot)

        tmp_ctr = [0]

        def nexttmp():
            i = tmp_ctr[0] % NT
            tmp_ctr[0] += 1
            return i

        class Pool_:
            def __init__(self, items):
                self.free = list(items)

            def get(self, wide=False):
                for i, it in enumerate(self.free):
                    if (it[2] >= HAL + TT) == wide:
                        return self.free.pop(i)
                for i, it in enumerate(self.free):
                    if it[2] >= HAL + TT:
                        return self.free.pop(i)
                raise RuntimeError("scratch pool exhausted")

            def put(self, it):
                self.free.append(it)

        class Lru:
            def __init__(self, wsel, pool, xcb_of, aux_eng):
                self.wsel, self.pool, self.xcb_of, self.aux = wsel, pool, xcb_of, aux_eng
                self.st = {}

            def s1(self, chunks):
                pool = self.pool
                for c in chunks:
                    wview, wkey, wcol0 = self.wsel(c)
                    b = nextbank()

                    def mm(e, b=b, wview=wview, wcol0=wcol0):
                        for k in range(16):
                            m = e.matmul(pb[b][:], lhsT=wview[:, k, wcol0:wcol0 + 128], rhs=xnT[:, k, HAL:HAL + TT], start=(k == 0), stop=(k == 15))
                        return m
                    S.op("pe", mm, reads=[wkey] + allx, writes=[("pb", b)])
                    XL = pool.get(wide=True)
                    XC = pool.get()
                    xl, xlk = XL[0], XL[1]
                    xc, xck = XC[0], XC[1]
                    S.op("act", lambda e, b=b, xl=xl: e.activation(out=xl[:, HAL:HAL + TT], in_=pb[b][:], func=AF.Copy), reads=[("pb", b)], writes=[xlk])
                    S.op("dve", lambda e, xl=xl, c=c: e.tensor_copy(out=xl[:, HAL - 3:HAL], in_=hal[:, c, 0:3]), reads=[("hal", c), xlk], writes=[xlk])
                    w0 = CV_C4W + c * 4
                    S.op("dve", lambda e, xl=xl, xc=xc, w0=w0, c=c: e.tensor_scalar(out=xc[:, 0:TT], in0=xl[:, HAL - 3:HAL - 3 + TT], scalar1=cv[:, w0:w0 + 1],
                                                                                  scalar2=cv[:, CV_CB + c:CV_CB + c + 1], op0=ALU.mult, op1=ALU.add),
                         reads=[xlk, "cv"], writes=[xck])
                    for k in range(1, 4):
                        S.op("dve", lambda e, xl=xl, xc=xc, w0=w0, k=k: e.scalar_tensor_tensor(out=xc[:, 0:TT], in0=xl[:, HAL - 3 + k:HAL - 3 + k + TT],
                                                                                             scalar=cv[:, w0 + k:w0 + k + 1], in1=xc[:, 0:TT], op0=ALU.mult, op1=ALU.add),
                             reads=[xlk, xck, "cv"], writes=[xck])
                    S.op("dve", lambda e, xl=xl, c=c: e.tensor_copy(out=hal[:, c, 0:3], in_=xl[:, HAL + TT - 3:HAL + TT]), reads=[xlk], writes=[("hal", c)])
                    xb, xbk = self.xcb_of(c)
                    if self.aux == "pool":
                        S.op("pool", lambda e, xc=xc, xb=xb: e.tensor_copy(out=xb[:], in_=xc[:, 0:TT]), reads=[xck], writes=[xbk])
                    else:
                        S.op("act", lambda e, xc=xc, xb=xb: e.activation(out=xb[:], in_=xc[:, 0:TT], func=AF.Copy), reads=[xck], writes=[xbk])
                    self.st[c] = dict(XL=XL, XC=XC, xb=xb, xbk=xbk)

            def s2(self, chunks):
                for c in chunks:
                    d = self.st[c]
                    xb, xbk = d["xb"], d["xbk"]
                    ba = nextbank()
                    S.op("pe", lambda e, ba=ba, c=c, xb=xb: e.matmul(pb[ba][:], lhsT=gw[:, c, :], rhs=xb[:], start=True, stop=True), reads=["gw", xbk], writes=[("pb", ba)])
                    bx = nextbank()
                    S.op("pe", lambda e, bx=bx, c=c, xb=xb: e.matmul(pb[bx][:], lhsT=gw[:, 8 + c, :], rhs=xb[:], start=True, stop=True), reads=["gw", xbk], writes=[("pb", bx)])
                    II = self.pool.get()
                    d["II"] = II
                    r_, rk = d["XL"][0], d["XL"][1]
                    i_, ik = II[0], II[1]
                    xc, xck = d["XC"][0], d["XC"][1]
                    S.op("act", lambda e, ba=ba, r_=r_, c=c: e.activation(out=r_[:, 0:TT], in_=pb[ba][:], func=AF.Sigmoid, bias=cv[:, CV_GAB + c:CV_GAB + c + 1]),
                         reads=[("pb", ba), "cv"], writes=[rk])
                    S.op("act", lambda e, bx=bx, i_=i_, c=c: e.activation(out=i_[:, 0:TT], in_=pb[bx][:], func=AF.Sigmoid, bias=cv[:, CV_GXB + c:CV_GXB + c + 1]),
                         reads=[("pb", bx), "cv"], writes=[ik])
                    S.op(self.aux, lambda e, i_=i_, xc=xc: e.tensor_tensor(out=i_[:, 0:TT], in0=i_[:, 0:TT], in1=xc[:, 0:TT], op=ALU.mult),
                         reads=[ik, xck], writes=[ik])

            def s3(self, chunks):
                for c in chunks:
                    d = self.st[c]
                    r_, rk = d["XL"][0], d["XL"][1]
                    a_, ak = d["XC"][0], d["XC"][1]
                    S.op("act", lambda e, r_=r_, a_=a_, c=c: e.activation(out=a_[:, 0:TT], in_=r_[:, 0:TT], func=AF.Exp, scale=cc[:, c:c + 1]),
                         reads=[rk, "cc"], writes=[ak])
                    S.op("act", lambda e, r_=r_, c=c: e.activation(out=r_[:, 0:TT], in_=r_[:, 0:TT], func=AF.Exp, scale=cc[:, 8 + c:9 + c]),
                         reads=[rk, "cc2"], writes=[rk])
                for c in chunks:
                    d = self.st[c]
                    r_, rk = d["XL"][0], d["XL"][1]
                    i_, ik = d["II"][0], d["II"][1]
                    a_, ak = d["XC"][0], d["XC"][1]
                    S.op("act", lambda e, r_=r_: e.activation(out=r_[:, 0:TT], in_=r_[:, 0:TT], func=AF.Sqrt, scale=-1.0, bias=1.0), reads=[rk], writes=[rk])
                    S.op(self.aux, lambda e, i_=i_, r_=r_: e.tensor_tensor(out=i_[:, 0:TT], in0=i_[:, 0:TT], in1=r_[:, 0:TT], op=ALU.mult),
                         reads=[ik, rk], writes=[ik])
                    S.op("dve", lambda e, a_=a_, i_=i_, r_=r_, c=c: e.tensor_tensor_scan(out=r_[:, 0:TT], data0=a_[:, 0:TT], data1=i_[:, 0:TT],
                                                                                        initial=hst[:, c:c + 1], op0=ALU.mult, op1=ALU.add),
                         reads=[ak, ik, ("hst", c)], writes=[rk])
                    self.pool.put(d["XC"])
                    self.pool.put(d["II"])

            def h(self, c):
                return self.st[c]["XL"]

        def save_halo():
            S.op("dve", lambda e: e.tensor_copy(out=xh[:], in_=xnT[:, :, HAL + TT - HAL:HAL + TT]),
                 reads=[("xnT", k, HAL + 384) for k in range(16)], writes=["xh"])

        xst_ctr = [0]
        ppool = Pool_([(ptmp[i], ("ptmp", i), HAL + TT) for i in range(NPT)])

        def prefix_norm(pt):
            srcs = []
            for s in range(4):
                slot = (xst_ctr[0] + s) % 2
                srcs.append((xst[slot], [("xst", slot)]))

            def pre(s):
                slot = xst_ctr[0] % 2
                xst_ctr[0] += 1
                r0 = pt * TT + s * 128
                S.dma("sp", lambda e, slot=slot, r0=r0: e.dma_start(out=xst[slot][:], in_=xs_d[r0:r0 + 128, :]), f"xst{slot}", writes=[("xst", slot)])
            norm_tile(srcs, 0, pre)
            if pt == NPRE - 1:
                save_halo()

        def prefix_all():
            prefix_norm(0)
            for pt in range(NPRE):
                L = Lru(lambda c: (wlru, ("wlru", c // 4), c * 128), ppool, lambda c: (pxcb[c], ("pxcb", c)), "pool")
                A, B = [0, 1, 2, 3], [4, 5, 6, 7]
                L.s1(A)
                L.s1(B)
                L.s2(A)
                L.s3(A)
                if pt + 1 < NPRE:
                    prefix_norm(pt + 1)
                L.s2(B)
                L.s3(B)
                for c in range(NCH):
                    H = L.h(c)
                    S.op("dve", lambda e, h=H[0], c=c: e.tensor_scalar(out=hst[:, c:c + 1], in0=h[:, TT - 1:TT], scalar1=cv[:, CV_FLAG + pt:CV_FLAG + pt + 1],
                                                                       scalar2=None, op0=ALU.mult),
                         reads=[H[1], "cv"], writes=[("hst", c)])
                    ppool.put(H)

        dg_ctr = [0]

        def rkeys(s):
            return [("R", s, j) for j in range(4)]

        import os
        STOP = int(os.environ.get("STOP_PHASE", "99"))

        def store_R(ft):
            for s in range(4):
                S.dma("sp", lambda e, s=s: e.dma_start(out=out_d[ft * TT + s * 128: ft * TT + (s + 1) * 128, :], in_=R[s][:]),
                      f"R{s}", reads=rkeys(s), writes=[("out", ft, s)])

        def full_tile(ft):
            t0 = (NPRE + ft) * TT
            S.op("dve", lambda e: e.tensor_copy(out=xnT[:, :, 0:HAL], in_=xh[:]), reads=["xh"], writes=[("xnT", k, 0) for k in range(16)])
            for s in range(4):
                S.dma("sp", lambda e, s=s: e.dma_start(out=R[s][:], in_=xs_d[t0 + s * 128:t0 + (s + 1) * 128, :]), f"R{s}", writes=rkeys(s))
            norm_tile([(R[s], rkeys(s)) for s in range(4)], 0)
            save_halo()
            if STOP <= 1:
                return store_R(ft)
            fpool = Pool_([(tmp[i], ("tmp", i), HAL + TT) for i in range(NT)] + [(gc[:, c, :], ("gc", c), TT) for c in range(NCH)]
                          + [(lnt[i], ("lnt", i), TT) for i in range(3)])
            pieces = {}

            def get_pieces(g):
                if g not in pieces:
                    pieces[g] = (load_piece(win_v[:, :, g * 512:(g + 1) * 512], 16, 512),
                                 load_piece(win_v[:, :, 1024 + g * 512:1024 + (g + 1) * 512], 16, 512))
                return pieces[g]
            L = Lru(lambda c: (get_pieces(c // 4)[0][0], get_pieces(c // 4)[0][1], (c % 4) * 128), fpool, lambda c: xcb4[c % 4], "dve")
            gys = {}
            xcb4 = [(xcbs[0], ("xcb", 0)), (xcbs[1], ("xcb", 1)), (gstat[0], ("gstat", 0)), (gstat[1], ("gstat", 1))]

            def front(p):
                chunks = [2 * p, 2 * p + 1]
                (wx, wxk), (wy, wyk) = get_pieces(chunks[0] // 4)
                L.s1(chunks)
                for c in chunks:
                    cl = c % 4
                    by = nextbank()

                    def mmy(e, by=by, wy=wy, cl=cl):
                        for k in range(16):
                            m = e.matmul(pb[by][:], lhsT=wy[:, k, cl * 128:(cl + 1) * 128], rhs=xnT[:, k, HAL:HAL + TT], start=(k == 0), stop=(k == 15))
                        return m
                    S.op("pe", mmy, reads=[wyk] + allx, writes=[("pb", by)])
                    GY = fpool.get()
                    gys[c] = GY
                    S.op("act", lambda e, by=by, gy=GY[0]: e.activation(out=gy[:, 0:TT], in_=pb[by][:], func=AF.Gelu_apprx_tanh),
                         reads=[("pb", by)], writes=[GY[1]])

            def back(p):
                chunks = [2 * p, 2 * p + 1]
                L.s2(chunks)
                L.s3(chunks)
                for c in chunks:
                    H = L.h(c)
                    GY = gys[c]
                    S.op("dve", lambda e, h=H[0], c=c: e.tensor_copy(out=hst[:, c:c + 1], in_=h[:, TT - 1:TT]), reads=[H[1]], writes=[("hst", c)])
                    S.op("dve", lambda e, h=H[0], gy=GY[0], c=c: e.tensor_tensor(out=oT[:, c, HAL:HAL + TT], in0=h[:, 0:TT], in1=gy[:, 0:TT], op=ALU.mult),
                         reads=[H[1], GY[1]], writes=[("oT", c)])
                    fpool.put(H)
                    fpool.put(GY)
            front(0)
            for p in range(4):
                if p + 1 < 4:
                    front(p + 1)
                back(p)
            if STOP <= 2:
                return store_R(ft)
            cpieces = {}

            def get_cp(g):
                if g not in cpieces:
                    cpieces[g] = (load_piece(win_v[:, :, 2048 + g * 512:2048 + (g + 1) * 512], 16, 512),
                                  load_piece(win_v[:, :, 3072 + g * 512:3072 + (g + 1) * 512], 16, 512))
                return cpieces[g]

            def c_front(c):
                (wv, wvk), (wg, wgk) = get_cp(c // 4)
                cl = c % 4
                gk = ("oT", 8 + c)
                ds = c % 2
                S.dma("pool", lambda e: e.dma_start(out=dgb[ds][:], in_=dg_d[c].rearrange("p (k n) -> p k n", k=31), max_dma_last_dim=4096),
                      f"dg{ds}", writes=[("dgb", ds)])
                bgm = nextbank()

                def mmg(e):
                    for k in range(16):
                        m = e.matmul(pb[bgm][:], lhsT=wg[:, k, cl * 128:(cl + 1) * 128], rhs=xnT[:, k, HAL:HAL + TT], start=(k == 0), stop=(k == 15))
                    return m
                S.op("pe", mmg, reads=[wgk] + allx, writes=[("pb", bgm)])
                bgh = nextbank()

                def mmgh(e):
                    for k in range(16):
                        m = e.matmul(pb[bgh][:, 0:HAL], lhsT=wg[:, k, cl * 128:(cl + 1) * 128], rhs=xnT[:, k, 0:HAL], start=(k == 0), stop=(k == 15))
                    for k in range(16):
                        m = e.matmul(pb[bgh][:, 64:64 + HAL], lhsT=wv[:, k, cl * 128:(cl + 1) * 128], rhs=xnT[:, k, 0:HAL],
                                     start=(k == 0), stop=(k == 15), skip_group_check=True)
                    return m
                S.op("pe", mmgh, reads=[wgk, wvk] + allxh, writes=[("pb", bgh)])
                isg = nexttmp()
                sg = tmp[isg]
                S.op("act", lambda e: e.activation(out=sg[:, HAL:HAL + TT], in_=pb[bgm][:], func=AF.Sigmoid), reads=[("pb", bgm)], writes=[("tmp", isg)])
                S.op("act", lambda e: e.activation(out=sg[:, 0:HAL], in_=pb[bgh][:, 0:HAL], func=AF.Sigmoid), reads=[("pb", bgh), ("tmp", isg)], writes=[("tmp", isg)])
                bvm = nextbank()

                def mmv(e):
                    for k in range(16):
                        m = e.matmul(pb[bvm][:], lhsT=wv[:, k, cl * 128:(cl + 1) * 128], rhs=xnT[:, k, HAL:HAL + TT], start=(k == 0), stop=(k == 15))
                    return m
                S.op("pe", mmv, reads=[wvk] + allx, writes=[("pb", bvm)])
                S.op("dve", lambda e: e.tensor_tensor(out=oT[:, 8 + c, HAL:HAL + TT], in0=pb[bvm][:], in1=sg[:, HAL:HAL + TT], op=ALU.mult),
                     reads=[("pb", bvm), ("tmp", isg)], writes=[gk])
                S.op("dve", lambda e: e.tensor_tensor(out=oT[:, 8 + c, 0:HAL], in0=pb[bgh][:, 64:64 + HAL], in1=sg[:, 0:HAL], op=ALU.mult),
                     reads=[("pb", bgh), ("tmp", isg), gk], writes=[gk])

            def c_back(c):
                gk = ("oT", 8 + c)
                ds = c % 2
                bc = nextbank()

                def mmc(e):
                    for k in range(31):
                        m = e.matmul(pb[bc][:], lhsT=dgb[ds][:, k, :], rhs=oT[:, 8 + c, HAL - 30 + k:HAL - 30 + k + TT], start=(k == 0), stop=(k == 30))
                    return m
                S.op("pe", mmc, reads=[("dgb", ds), gk], writes=[("pb", bc)])
                if c > 0:
                    stat_mm(c - 1)
                bcol = cv[:, CV_DWB + c:CV_DWB + c + 1]
                gs = 0
                S.op("act", lambda e: e.activation(out=gc[:, c, :], in_=pb[bc][:], func=AF.Identity, bias=bcol), reads=[("pb", bc), "cv"], writes=[("gc", c)])
                S.op("act", lambda e: e.activation(out=gstat[gs][:], in_=pb[bc][:], func=AF.Identity, bias=bcol), reads=[("pb", bc), "cv"], writes=[("gstat", gs)])
                S.op("act", lambda e: e.activation(out=gstat[gs + 1][:], in_=pb[bc][:], func=AF.Square, bias=bcol), reads=[("pb", bc), "cv"], writes=[("gstat", gs + 1)])

            def stat_mm(c):
                gs = 0
                S.op("pe", lambda e: e.matmul(pstat[0][:], lhsT=ones[:], rhs=gstat[gs][:], start=(c == 0), stop=(c == NCH - 1)),
                     reads=["ones", ("gstat", gs)], writes=[("pstat", 0)])
                S.op("pe", lambda e: e.matmul(pstat[1][:], lhsT=ones[:], rhs=gstat[gs + 1][:], start=(c == 0), stop=(c == NCH - 1)),
                     reads=["ones", ("gstat", gs + 1)], writes=[("pstat", 1)])
            c_front(0)
            for c in range(NCH):
                if c + 1 < NCH:
                    c_front(c + 1)
                c_back(c)
            stat_mm(NCH - 1)
            if STOP <= 3:
                return store_R(ft)
            mean, msq, rs = lnt
            nmr = msq
            S.op("act", lambda e: e.activation(out=mean[:], in_=pstat[0][:], func=AF.Copy, scale=1.0 / 1024), reads=[("pstat", 0)], writes=[("lnt", 0)])
            S.op("act", lambda e: e.activation(out=msq[:], in_=pstat[0][:], func=AF.Square, scale=1.0 / 1024), reads=[("pstat", 0)], writes=[("lnt", 1)])
            S.op("dve", lambda e: e.scalar_tensor_tensor(out=rs[:], in0=pstat[1][:], scalar=1.0 / 1024, in1=msq[:], op0=ALU.mult, op1=ALU.subtract),
                 reads=[("pstat", 1), ("lnt", 1)], writes=[("lnt", 2)])
            S.op("act", lambda e: e.activation(out=rs[:], in_=rs[:], func=AF.Sqrt, bias=EPS, scale=1.0), reads=[("lnt", 2)], writes=[("lnt", 2)])
            S.op("dve", lambda e: e.reciprocal(out=rs[:], in_=rs[:]), reads=[("lnt", 2)], writes=[("lnt", 2)])
            S.op("dve", lambda e: e.scalar_tensor_tensor(out=nmr[:], in0=mean[:], scalar=-1.0, in1=rs[:], op0=ALU.mult, op1=ALU.mult),
                 reads=[("lnt", 0), ("lnt", 2), ("lnt", 1)], writes=[("lnt", 1)])
            for c in range(NCH):
                it = nexttmp()
                t_ = tmp[it]
                S.op("dve", lambda e, c=c, t_=t_: e.tensor_tensor(out=t_[:, 0:TT], in0=gc[:, c, :], in1=rs[:], op=ALU.mult),
                     reads=[("gc", c), ("lnt", 2)], writes=[("tmp", it)])
                S.op("dve", lambda e, t_=t_: e.tensor_tensor(out=t_[:, 0:TT], in0=t_[:, 0:TT], in1=nmr[:], op=ALU.add),
                     reads=[("tmp", it), ("lnt", 1)], writes=[("tmp", it)])
                S.op("act", lambda e, c=c, t_=t_: e.activation(out=oT[:, 8 + c, HAL:HAL + TT], in_=t_[:, 0:TT], func=AF.Silu,
                                                               scale=cv[:, CV_LNG + c:CV_LNG + c + 1], bias=cv[:, CV_LNB + c:CV_LNB + c + 1]),
                     reads=[("tmp", it), "cv"], writes=[("oT", 8 + c)])
            if STOP <= 4:
                return store_R(ft)
            for j in range(4):
                wo, wok = load_piece(wout_v[:, :, j * 512:(j + 1) * 512], 16, 512)
                for s in range(4):
                    b = nextbank()

                    def mmo(e, b=b, s=s, wo=wo):
                        for k in range(16):
                            m = e.matmul(pb[b][:], lhsT=oT[:, k, HAL + s * 128:HAL + (s + 1) * 128], rhs=wo[:, k, :], start=(k == 0), stop=(k == 15))
                        return m
                    S.op("pe", mmo, reads=[wok] + [("oT", k) for k in range(16)], writes=[("pb", b)])
                    S.op("dve", lambda e, b=b, s=s, j=j: e.tensor_tensor(out=R[s][:, j * 512:(j + 1) * 512], in0=pb[b][:],
                                                                         in1=R[s][:, j * 512:(j + 1) * 512], op=ALU.add),
                         reads=[("pb", b), ("R", s, j)], writes=[("R", s, j)])
            if STOP <= 5:
                return store_R(ft)
            norm_tile([(R[s], rkeys(s)) for s in range(4)], 1)
            def w1_phase(grp):
                a0 = (grp % 4) * 4
                w1p, w1k = load_piece(w1_v[:, :, grp * 512:(grp + 1) * 512], 16, 512)
                for cl in range(4):
                    b = nextbank()

                    def mm1(e, b=b, w1p=w1p, cl=cl):
                        for k in range(16):
                            m = e.matmul(pb[b][:], lhsT=w1p[:, k, cl * 128:(cl + 1) * 128], rhs=xnT[:, k, HAL:HAL + TT], start=(k == 0), stop=(k == 15))
                        return m
                    S.op("pe", mm1, reads=[w1k] + allx, writes=[("pb", b)])
                    rs_ = cl % 2
                    S.op("act", lambda e, b=b, rs_=rs_: e.activation(out=rl[rs_][:], in_=pb[b][:], func=AF.Relu), reads=[("pb", b)], writes=[("rl", rs_)])
                    S.op("dve", lambda e, b=b, rs_=rs_, a0=a0, cl=cl: e.scalar_tensor_tensor(
                        out=oT[:, a0 + cl, HAL:HAL + TT], in0=pb[b][:], scalar=0.0, in1=rl[rs_][:], op0=ALU.max, op1=ALU.mult),
                        reads=[("pb", b), ("rl", rs_)], writes=[("oT", a0 + cl)])

            def w2_phase(grp):
                a0 = (grp % 4) * 4
                w2p, w2k = load_piece(w2_v[:, grp * 4:(grp + 1) * 4, :], 4, D)
                for s in range(4):
                    for j in range(4):
                        b = nextbank()

                        def mm2(e, b=b, s=s, j=j, a0=a0, w2p=w2p):
                            for fl in range(4):
                                m = e.matmul(pb[b][:], lhsT=oT[:, a0 + fl, HAL + s * 128:HAL + (s + 1) * 128], rhs=w2p[:, fl, j * 512:(j + 1) * 512],
                                             start=(fl == 0), stop=(fl == 3))
                            return m
                        S.op("pe", mm2, reads=[w2k] + [("oT", a0 + fl) for fl in range(4)], writes=[("pb", b)])
                        S.op("dve", lambda e, b=b, s=s, j=j: e.tensor_tensor(out=R[s][:, j * 512:(j + 1) * 512], in0=pb[b][:],
                                                                             in1=R[s][:, j * 512:(j + 1) * 512], op=ALU.add),
                             reads=[("pb", b), ("R", s, j)], writes=[("R", s, j)])
            NG = 16
            w1_phase(0)
            for grp in range(NG):
                if grp + 1 < NG:
                    w1_phase(grp + 1)
                w2_phase(grp)
            if STOP <= 6:
                return store_R(ft)
            for s in range(4):
                n = nrm_ctr[0]
                nrm_ctr[0] += 1
                slot = n % 2
                col = n % 8
                S.op("act", lambda e, s=s, slot=slot, col=col: e.activation(out=xsb[slot][:], in_=R[s][:], func=AF.Square, accum_out=ssq[:, col:col + 1]),
                     reads=rkeys(s), writes=[("xsb", slot), ("ssq", col)])
                S.op("act", lambda e, col=col: e.activation(out=rstd[:, col:col + 1], in_=ssq[:, col:col + 1], func=AF.Sqrt, scale=1.0 / D, bias=EPS),
                     reads=[("ssq", col)], writes=[("rstd", col)])
                S.op("dve", lambda e, col=col: e.reciprocal(out=rstd[:, col:col + 1], in_=rstd[:, col:col + 1]), reads=[("rstd", col)], writes=[("rstd", col)])
                S.op("dve", lambda e, s=s, col=col: e.scalar_tensor_tensor(out=R[s][:], in0=R[s][:], scalar=rstd[:, col:col + 1], in1=gng[:, 2, :],
                                                                           op0=ALU.mult, op1=ALU.mult),
                     reads=rkeys(s) + [("rstd", col), "gng"], writes=rkeys(s))
                S.dma("sp", lambda e, s=s: e.dma_start(out=out_d[ft * TT + s * 128: ft * TT + (s + 1) * 128, :], in_=R[s][:]),
                      f"R{s}", reads=rkeys(s), writes=[("out", ft, s)])

        if NPRE > 0:
            prefix_all()
            S.barrier()
        for ft in range(NFULL):
            full_tile(ft)
        outkeys = [("out", ft, s) for ft in range(NFULL) for s in range(4)]
        S.op("sp", lambda e: e.nop(), reads=outkeys)

        chans = sorted(set(o["chan"] for o in S.ops if o["kind"] == "dma"))
        sems = {e: es.enter_context(nc.semaphore(f"s_{e}")) for e in ["pe", "act", "dve", "pool", "sp"]}
        chan_sems = {c: es.enter_context(nc.semaphore(f"c_{c}")) for c in chans}
        block = es.enter_context(nc.Block())
        S.emit(block, sems, chan_sems)
        print("ops", len(S.ops), "sbuf bytes", off[0], "counts", {k: v for k, v in S.final_counts.items()})
    return nc


_CACHE = {}


def _host_inputs(inp, NPRE=12, NFULL=4, seq=8192, batch=2, chunks=4):
    f = np.float32
    x = np.asarray(inp["x"], f)
    cvs = []
    per = NFULL * TT

    def pc(v):
        return np.ascontiguousarray(np.asarray(v, f).reshape(NCH, 128).T)
    base = np.zeros((128, CV_N), f)
    base[:, CV_CB:CV_CB + 8] = pc(inp["lru_conv_b"][0])
    base[:, CV_GAB:CV_GAB + 8] = pc(inp["lru_gate_a_b"][0])
    base[:, CV_GXB:CV_GXB + 8] = pc(inp["lru_gate_x_b"][0])
    base[:, CV_LAM:CV_LAM + 8] = pc(inp["lru_lambda"][0])
    base[:, CV_DWB:CV_DWB + 8] = pc(inp["conf_dw_b"][0])
    base[:, CV_LNG:CV_LNG + 8] = pc(inp["conf_ln_g"][0])
    base[:, CV_LNB:CV_LNB + 8] = pc(inp["conf_ln_b"][0])
    c4 = np.asarray(inp["lru_conv_w"][0], f)
    base[:, CV_C4W:CV_C4W + 32] = c4.reshape(4, NCH, 128).transpose(2, 1, 0).reshape(128, 32)
    gng = np.stack([np.broadcast_to(np.asarray(inp["mix_norm_g"][0], f), (128, D)),
                    np.broadcast_to(np.asarray(inp["mlp_norm_g"][0], f), (128, D)),
                    np.broadcast_to(np.asarray(inp["final_norm_g"], f), (128, D))], axis=1)
    gng = np.ascontiguousarray(gng)
    gw = np.zeros((128, 16, 128), f)
    for gi, name in enumerate(["lru_gate_a_w", "lru_gate_x_w"]):
        w = np.asarray(inp[name][0], f)
        for c in range(NCH):
            for hh in range(2):
                gw[hh * 64:(hh + 1) * 64, gi * 8 + c, hh * 64:(hh + 1) * 64] = w[c * 2 + hh]
    dw = np.asarray(inp["conf_dw_w"][0], f)
    dg = np.zeros((NCH, 128, 31, 128), f)
    ar = np.arange(128)
    for c in range(NCH):
        dg[c, ar, :, ar] = dw[:, c * 128:(c + 1) * 128].T
    dg = dg.reshape(NCH, 128, 31 * 128)
    shared = dict(gng=gng, gw=gw, dg=dg,
                  w_in=np.ascontiguousarray(np.asarray(inp["w_in"][0], f)),
                  w_out=np.ascontiguousarray(np.asarray(inp["w_out"][0], f)),
                  w1=np.ascontiguousarray(np.asarray(inp["mlp_w1"][0], f)),
                  w2=np.ascontiguousarray(np.asarray(inp["mlp_w2"][0], f)))
    maps = []
    for core in range(batch * chunks):
        b, q = divmod(core, chunks)
        npad = (NPRE * TT) - q * per
        xs = np.zeros(((NPRE + NFULL) * TT, D), f)
        xs[npad:] = x[b, 0:(q + 1) * per]
        cvc = base.copy()
        for pt in range(NPRE):
            cvc[:, CV_FLAG + pt] = 1.0 if pt * TT >= npad else 0.0
        m = dict(shared)
        m["xs"] = xs
        m["cv"] = cvc
        maps.append(m)
    return maps


def kernel(**inputs):
    NPRE, NFULL = 12, 4
    key = (NPRE, NFULL)
    if key not in _CACHE:
        _CACHE[key] = build_program(NPRE, NFULL)
    nc = _CACHE[key]
    maps = _host_inputs(inputs, NPRE, NFULL)
    res = run_bass_kernel_spmd(nc, maps, core_ids=list(range(8)))
    out = np.zeros((2, 8192, D), np.float32)
    for core in range(8):
        b, q = divmod(core, 4)
        out[b, q * 2048:(q + 1) * 2048] = res.results[core]["out"]
    return out
```

```python
import numpy as np
import concourse.bass as bass
import concourse.mybir as mybir
from concourse.bass_utils import run_bass_kernel_spmd

F32 = mybir.dt.float32
BF16 = mybir.dt.bfloat16
AF = mybir.ActivationFunctionType
ALU = mybir.AluOpType

D = 2048
DIN = 4096
DFF = 8192
TT = 512
HAL = 32
NCH = 8
EPS = 1e-6
CV_CB, CV_GAB, CV_GXB, CV_LAM, CV_DWB, CV_LNG, CV_LNB = 0, 8, 16, 24, 32, 40, 48
CV_C4W = 56
CV_FLAG = 88
CV_N = 104


class Sched:
    def __init__(self, nc):
        self.nc = nc
        self.ops = []
        self.last_w = {}
        self.readers = {}
        self.eng_objs = {"pe": nc.tensor, "act": nc.scalar, "dve": nc.vector, "pool": nc.gpsimd, "sp": nc.sync}

    def _deps(self, eng, reads, writes):
        deps = set()
        for k in reads:
            w = self.last_w.get(k)
            if w is not None:
                deps.add(w)
        for k in writes:
            w = self.last_w.get(k)
            if w is not None:
                deps.add(w)
            for r in self.readers.get(k, ()):
                deps.add(r)
        return deps

    def _add(self, eng, fn, reads, writes, kind, chan=None):
        oid = len(self.ops)
        deps = self._deps(eng, reads, writes)
        keep = set()
        raw = set(self.last_w.get(k) for k in reads if self.last_w.get(k) is not None)
        for d in deps:
            od = self.ops[d]
            if od["kind"] == "dma" or kind == "dma":
                keep.add(d)
            elif od["eng"] != eng:
                keep.add(d)
            elif d in raw and eng != "pe":
                keep.add(d)
        self.ops.append(dict(eng=eng, fn=fn, deps=keep, kind=kind, chan=chan, sig=False))
        for k in reads:
            self.readers.setdefault(k, []).append(oid)
        for k in writes:
            self.last_w[k] = oid
            self.readers[k] = []
        return oid

    def op(self, eng, fn, reads=(), writes=()):
        return self._add(eng, fn, list(reads), list(writes), "cmp")

    def dma(self, queue, fn, chan, reads=(), writes=()):
        return self._add(queue, fn, list(reads), list(writes), "dma", chan)

    def barrier(self):
        last = {}
        for i, o in enumerate(self.ops):
            if o["kind"] == "dma":
                last[("c", o["chan"])] = i
            else:
                last[("e", o["eng"])] = i
        deps = set(last.values())
        for eng in ["pe", "act", "dve", "pool", "sp"]:
            self.ops.append(dict(eng=eng, fn=lambda e: e.nop(), deps=set(deps), kind="cmp", chan=None, sig=False))

    def emit(self, block, sems, chan_sems):
        ops = self.ops
        for o in ops:
            for d in o["deps"]:
                ops[d]["sig"] = True
        cnt = {}
        for i, o in enumerate(ops):
            if o["kind"] == "dma":
                c = o["chan"]
                cnt[c] = cnt.get(c, 0) + 16
                o["ev"] = (("c", c), cnt[c])
            elif o["sig"]:
                e = o["eng"]
                cnt[e] = cnt.get(e, 0) + 1
                o["ev"] = (("e", e), cnt[e])
            else:
                o["ev"] = None
        const_total = {c: v for c, v in cnt.items() if isinstance(c, str) and c.startswith("const")}
        self.final_counts = cnt
        by_eng = {}
        for i, o in enumerate(ops):
            by_eng.setdefault(o["eng"], []).append(i)

        def semof(key):
            return chan_sems[key[1]] if key[0] == "c" else sems[key[1]]

        snap = {}
        nwaits = [0]

        def run_engine(ename, engobj):
            known = {}
            for i in by_eng.get(ename, []):
                o = ops[i]
                need = {}
                for d in o["deps"]:
                    key, val = ops[d]["ev"]
                    if key[0] == "c" and key[1] in const_total:
                        val = const_total[key[1]]
                    if need.get(key, 0) < val:
                        need[key] = val
                for key, val in sorted(need.items(), key=lambda kv: -kv[1]):
                    if known.get(key, 0) >= val:
                        continue
                    engobj.wait_ge(semof(key), val)
                    nwaits[0] += 1
                    known[key] = val
                    sn = snap.get((key, val))
                    if sn:
                        for k2, v2 in sn.items():
                            if known.get(k2, 0) < v2:
                                known[k2] = v2
                inst = o["fn"](engobj)
                if o["ev"] is not None:
                    key, val = o["ev"]
                    if o["kind"] == "dma":
                        inst.then_inc(semof(key), 16)
                    else:
                        inst.then_inc(semof(key), 1)
                        snap[(key, val)] = dict(known)
                        known[key] = max(known.get(key, 0), 0)

        @block.sync
        def _(e):
            run_engine("sp", e)

        @block.gpsimd
        def _(e):
            run_engine("pool", e)

        @block.scalar
        def _(e):
            run_engine("act", e)

        @block.vector
        def _(e):
            run_engine("dve", e)

        @block.tensor
        def _(e):
            run_engine("pe", e)


def build_program(NPRE, NFULL):
    nc = bass.Bass("TRN2", target_bir_lowering=False)
    NTILE = NPRE + NFULL
    xs_d = nc.dram_tensor("xs", [NTILE * TT, D], F32, kind="ExternalInput").ap()
    cv_d = nc.dram_tensor("cv", [128, CV_N], F32, kind="ExternalInput").ap()
    gng_d = nc.dram_tensor("gng", [128, 3, D], F32, kind="ExternalInput").ap()
    gw_d = nc.dram_tensor("gw", [128, 16, 128], F32, kind="ExternalInput").ap()
    dg_d = nc.dram_tensor("dg", [NCH, 128, 31 * 128], F32, kind="ExternalInput").ap()
    win_d = nc.dram_tensor("w_in", [D, DIN], F32, kind="ExternalInput").ap()
    wout_d = nc.dram_tensor("w_out", [D, D], F32, kind="ExternalInput").ap()
    w1_d = nc.dram_tensor("w1", [D, DFF], F32, kind="ExternalInput").ap()
    w2_d = nc.dram_tensor("w2", [DFF, D], F32, kind="ExternalInput").ap()
    out_d = nc.dram_tensor("out", [NFULL * TT, D], F32, kind="ExternalOutput").ap()
    winb_d = nc.dram_tensor("winb", [8, 128, 8192], BF16, kind="Internal").ap()
    woutb_d = nc.dram_tensor("woutb", [4, 128, 8192], BF16, kind="Internal").ap()
    w1b_d = nc.dram_tensor("w1b", [16, 128, 8192], BF16, kind="Internal").ap()
    w2b_d = nc.dram_tensor("w2b", [16, 128, 8192], BF16, kind="Internal").ap()

    S = Sched(nc)
    import contextlib
    es = contextlib.ExitStack()
    off = [16384]

    def sb(name, shape, dt, at=None):
        nbytes = int(np.prod(shape[1:])) * (4 if dt == F32 else 2)
        nbytes = (nbytes + 63) // 64 * 64
        if at is None:
            at = off[0]
            off[0] += nbytes
        return nc.alloc_sbuf_tensor_at(name, shape, dt, offset=at), at + nbytes

    def sbp(name, shape, dt):
        return sb(name, shape, dt)[0]

    def ps(name, shape, dt):
        return es.enter_context(nc.psum_tensor(name, shape, dt))

    with es:
        cv = sbp("cv", [128, CV_N], F32)
        cc = sbp("cc", [128, 16], F32)
        gng = sbp("gng", [128, 3, D], F32)
        gw = sbp("gw", [128, 16, 128], BF16)
        ident = sbp("ident", [128, 128], BF16)
        identf = sbp("identf", [128, 128], F32)
        ones = sbp("ones", [128, 128], BF16)
        hst = sbp("hst", [128, NCH], F32)
        hal = sbp("hal", [128, NCH, 4], F32)
        xnT = sbp("xnT", [128, 16, HAL + TT], BF16)
        xsb = [sbp(f"xsb{i}", [128, D], BF16) for i in range(2)]
        ssq = sbp("ssq", [128, 8], F32)
        rstd = sbp("rstd", [128, 8], F32)
        NT = 5
        tmp = [sbp(f"tmp{i}", [128, HAL + TT], F32) for i in range(NT)]
        xcbs = [sbp(f"xcb{i}", [128, TT], BF16) for i in range(2)]
        xh = sbp("xh", [128, 16, HAL], BF16)
        base = off[0]
        wlru = sbp("wlru", [128, 16, 1024], BF16)
        xst = [sbp(f"xst{i}", [128, D], F32) for i in range(2)]
        NPT = 35
        ptmp = [sbp(f"ptmp{i}", [128, HAL + TT], F32) for i in range(NPT)]
        pxcb = [sbp(f"pxcb{i}", [128, TT], BF16) for i in range(NCH)]
        assert off[0] < 229000, off[0]
        off[0] = base
        WS = 3
        wring = [sbp(f"wr{i}", [128, 16 * 512], BF16) for i in range(WS)]
        R = [sbp(f"R{i}", [128, D], F32) for i in range(4)]
        oT = sbp("oT", [128, 16, HAL + TT], BF16)
        gc = sbp("gc", [128, NCH, TT], F32)
        gstat = [sbp(f"gstat{i}", [128, TT], BF16) for i in range(2)]
        dgb = [sbp(f"dgb{i}", [128, 31, 128], BF16) for i in range(2)]
        lnt = [sbp(f"lnt{i}", [128, TT], F32) for i in range(3)]
        rl = [sbp(f"rl{i}", [128, TT], BF16) for i in range(2)]
        assert off[0] < 229000, off[0]
        NB = 4
        pb = [ps(f"pb{i}", [128, 512], F32) for i in range(NB)]
        pstat = [ps(f"pst{i}", [128, 512], F32) for i in range(2)]
        ptr = [ps(f"ptr{i}", [128, 8, 128], BF16) for i in range(2)]
        bank_ctr = [0]

        def nextbank():
            b = bank_ctr[0] % NB
            bank_ctr[0] += 1
            return b

        S.dma("sp", lambda e: e.dma_start(out=cv[:], in_=cv_d), "const_sp", writes=["cv"])
        S.dma("sp", lambda e: e.dma_start(out=gng[:], in_=gng_d), "const_sp", writes=["gng"])
        S.dma("pool", lambda e: e.dma_start(out=gw[:], in_=gw_d), "const_pool", writes=["gw"])
        win_v = win_d.rearrange("(k p) n -> p k n", p=128)
        wout_v = wout_d.rearrange("(k p) n -> p k n", p=128)
        w1_v = w1_d.rearrange("(k p) n -> p k n", p=128)
        w2_v = w2_d.rearrange("(k p) n -> p k n", p=128)
        if NPRE > 0:
            for h in range(2):
                S.dma("pool", lambda e, h=h: e.dma_start(out=wlru[:, :, h * 512:(h + 1) * 512], in_=win_v[:, :, h * 512:(h + 1) * 512]),
                      "const_pool", writes=[("wlru", h)])
        S.op("pool", lambda e: e.memset(identf[:], 0.0), writes=["identf"])
        S.op("pool", lambda e: e.affine_select(out=identf[:], in_=identf[:], pattern=[[-1, 128]], compare_op=ALU.not_equal,
                                               fill=1.0, base=0, channel_multiplier=1), reads=["identf"], writes=["identf"])
        S.op("dve", lambda e: e.tensor_copy(out=ident[:], in_=identf[:]), reads=["identf"], writes=["ident"])
        S.op("dve", lambda e: e.memset(ones[:], 1.0), writes=["ones"])
        S.op("dve", lambda e: e.memset(hst[:], 0.0), writes=[("hst", c) for c in range(NCH)])
        S.op("dve", lambda e: e.memset(hal[:], 0.0), writes=[("hal", c) for c in range(NCH)])
        S.op("act", lambda e: e.activation(out=cc[:, 0:8], in_=cv[:, CV_LAM:CV_LAM + 8], func=AF.Exp, scale=-1.0),
             reads=["cv"], writes=["cc"])
        S.op("act", lambda e: e.activation(out=cc[:, 0:8], in_=cc[:, 0:8], func=AF.Ln, bias=1.0, scale=1.0),
             reads=["cc"], writes=["cc"])
        S.op("dve", lambda e: e.tensor_scalar(out=cc[:, 8:16], in0=cc[:, 0:8], scalar1=-16.0, scalar2=None, op0=ALU.mult),
             reads=["cc"], writes=["cc2"])
        S.op("dve", lambda e: e.tensor_scalar(out=cc[:, 0:8], in0=cc[:, 0:8], scalar1=-8.0, scalar2=None, op0=ALU.mult),
             reads=["cc", "cc2"], writes=["cc"])

        cast_jobs = []
        for pi in range(8):
            col0 = (pi // 2) * 1024 + (pi % 2) * 512
            cast_jobs.append((winb_d[pi].rearrange("p (k n) -> p k n", k=16), win_v[:, :, col0:col0 + 512], ("winb", pi)))
        for j in range(4):
            cast_jobs.append((woutb_d[j].rearrange("p (k n) -> p k n", k=16), wout_v[:, :, j * 512:(j + 1) * 512], ("woutb", j)))
        for g in range(16):
            cast_jobs.append((w1b_d[g].rearrange("p (k n) -> p k n", k=16), w1_v[:, :, g * 512:(g + 1) * 512], ("w1b", g)))
            cast_jobs.append((w2b_d[g].rearrange("p (k n) -> p k n", k=4), w2_v[:, g * 4:(g + 1) * 4, :], ("w2b", g)))
        cast_ctr = [0]
        NCASTCH = 6

        def emit_casts(n):
            for _ in range(n):
                i = cast_ctr[0]
                if i >= len(cast_jobs):
                    return
                cast_ctr[0] += 1
                o_ap, i_ap, key = cast_jobs[i]
                S.dma("pool", lambda e, o_ap=o_ap, i_ap=i_ap: e.dma_start(out=o_ap, in_=i_ap), f"wcast{i % NCASTCH}",
                      writes=[key, ("wcastslot", i % NCASTCH)])

        nrm_ctr = [0]

        def norm_front(src, srck, np_, gidx):
            n = nrm_ctr[0]
            nrm_ctr[0] += 1
            slot = n % 2
            col = n % 8
            S.op("act", lambda e: e.activation(out=xsb[slot][0:np_, :], in_=src[0:np_, :], func=AF.Square, accum_out=ssq[0:np_, col:col + 1]),
                 reads=srck, writes=[("xsb", slot), ("ssq", col)])
            S.op("act", lambda e: e.activation(out=rstd[0:np_, col:col + 1], in_=ssq[0:np_, col:col + 1], func=AF.Sqrt, scale=1.0 / D, bias=EPS),
                 reads=[("ssq", col)], writes=[("rstd", col)])
            S.op("dve", lambda e: e.reciprocal(out=rstd[0:np_, col:col + 1], in_=rstd[0:np_, col:col + 1]),
                 reads=[("rstd", col)], writes=[("rstd", col)])
            S.op("dve", lambda e: e.scalar_tensor_tensor(out=xsb[slot][0:np_, :], in0=src[0:np_, :], scalar=rstd[0:np_, col:col + 1],
                                                         in1=gng[0:np_, gidx, :], op0=ALU.mult, op1=ALU.mult),
                 reads=list(srck) + [("rstd", col), "gng"], writes=[("xsb", slot)])
            return slot

        def norm_back(slot, np_, c0):
            for half in range(2):
                def tr(e, half=half):
                    for j in range(8):
                        k = half * 8 + j
                        mm = e.transpose(out=ptr[half][:, j, 0:np_], in_=xsb[slot][0:np_, k * 128:(k + 1) * 128], identity=ident[0:np_, 0:np_])
                    return mm
                S.op("pe", tr, reads=[("xsb", slot), "ident"], writes=[("ptr", half)])
                wk = [("xnT", half * 8 + j, c0) for j in range(8)]
                if half == 0:
                    S.op("act", lambda e: e.activation(out=xnT[:, 0:8, c0:c0 + np_], in_=ptr[0][:, :, 0:np_], func=AF.Copy),
                         reads=[("ptr", 0)], writes=wk)
                else:
                    S.op("dve", lambda e: e.tensor_copy(out=xnT[:, 8:16, c0:c0 + np_], in_=ptr[1][:, :, 0:np_]),
                         reads=[("ptr", 1)], writes=wk)

        def norm_tile(srcs, gidx, pre=None):
            slots = {}
            for s in range(4):
                if pre is not None:
                    pre(s)
                slots[s] = norm_front(srcs[s][0], srcs[s][1], 128, gidx)
                if s >= 1:
                    norm_back(slots[s - 1], 128, HAL + (s - 1) * 128)
            norm_back(slots[3], 128, HAL + 3 * 128)

        allx = [("xnT", k, HAL + s * 128) for k in range(16) for s in range(4)]
        allxh = allx + [("xnT", k, 0) for k in range(16)]

        wr_ctr = [0]

        def load_piece(src, nk, ncol):
            src_ap, skey = src
            slot = wr_ctr[0] % WS
            wr_ctr[0] += 1
            view = wring[slot][:, 0:nk * ncol].rearrange("p (k n) -> p k n", k=nk)
            S.dma("pool", lambda e: e.dma_start(out=wring[slot][:, 0:nk * ncol], in_=src_ap), f"wr{slot}", reads=[skey], writes=[("wr", slot)])
            return view, ("wr", slot)

        tmp_ctr = [0]

        def nexttmp():
            i = tmp_ctr[0] % NT
            tmp_ctr[0] += 1
            return i

        class Pool_:
            def __init__(self, items):
                self.free = list(items)

            def get(self, wide=False):
                for i, it in enumerate(self.free):
                    if (it[2] >= HAL + TT) == wide:
                        return self.free.pop(i)
                for i, it in enumerate(self.free):
                    if it[2] >= HAL + TT:
                        return self.free.pop(i)
                raise RuntimeError("scratch pool exhausted")

            def put(self, it):
                self.free.append(it)

        class Lru:
            def __init__(self, wsel, pool, xcb_of, aux_eng):
                self.wsel, self.pool, self.xcb_of, self.aux = wsel, pool, xcb_of, aux_eng
                self.st = {}

            def s1(self, chunks):
                pool = self.pool
                for c in chunks:
                    wview, wkey, wcol0 = self.wsel(c)
                    b = nextbank()

                    def mm(e, b=b, wview=wview, wcol0=wcol0):
                        for k in range(16):
                            m = e.matmul(pb[b][:], lhsT=wview[:, k, wcol0:wcol0 + 128], rhs=xnT[:, k, HAL:HAL + TT], start=(k == 0), stop=(k == 15))
                        return m
                    S.op("pe", mm, reads=[wkey] + allx, writes=[("pb", b)])
                    XL = pool.get(wide=True)
                    XC = pool.get()
                    xl, xlk = XL[0], XL[1]
                    xc, xck = XC[0], XC[1]
                    S.op("act", lambda e, b=b, xl=xl: e.activation(out=xl[:, HAL:HAL + TT], in_=pb[b][:], func=AF.Copy), reads=[("pb", b)], writes=[xlk])
                    S.op("dve", lambda e, xl=xl, c=c: e.tensor_copy(out=xl[:, HAL - 3:HAL], in_=hal[:, c, 0:3]), reads=[("hal", c), xlk], writes=[xlk])
                    w0 = CV_C4W + c * 4
                    S.op("dve", lambda e, xl=xl, xc=xc, w0=w0, c=c: e.tensor_scalar(out=xc[:, 0:TT], in0=xl[:, HAL - 3:HAL - 3 + TT], scalar1=cv[:, w0:w0 + 1],
                                                                                  scalar2=cv[:, CV_CB + c:CV_CB + c + 1], op0=ALU.mult, op1=ALU.add),
                         reads=[xlk, "cv"], writes=[xck])
                    for k in range(1, 4):
                        S.op("dve", lambda e, xl=xl, xc=xc, w0=w0, k=k: e.scalar_tensor_tensor(out=xc[:, 0:TT], in0=xl[:, HAL - 3 + k:HAL - 3 + k + TT],
                                                                                             scalar=cv[:, w0 + k:w0 + k + 1], in1=xc[:, 0:TT], op0=ALU.mult, op1=ALU.add),
                             reads=[xlk, xck, "cv"], writes=[xck])
                    S.op("dve", lambda e, xl=xl, c=c: e.tensor_copy(out=hal[:, c, 0:3], in_=xl[:, HAL + TT - 3:HAL + TT]), reads=[xlk], writes=[("hal", c)])
                    xb, xbk = self.xcb_of(c)
                    if self.aux == "pool":
                        S.op("pool", lambda e, xc=xc, xb=xb: e.tensor_copy(out=xb[:], in_=xc[:, 0:TT]), reads=[xck], writes=[xbk])
                    else:
                        S.op("act", lambda e, xc=xc, xb=xb: e.activation(out=xb[:], in_=xc[:, 0:TT], func=AF.Copy), reads=[xck], writes=[xbk])
                    self.st[c] = dict(XL=XL, XC=XC, xb=xb, xbk=xbk)

            def s2(self, chunks):
                for c in chunks:
                    d = self.st[c]
                    xb, xbk = d["xb"], d["xbk"]
                    ba = nextbank()
                    S.op("pe", lambda e, ba=ba, c=c, xb=xb: e.matmul(pb[ba][:], lhsT=gw[:, c, :], rhs=xb[:], start=True, stop=True), reads=["gw", xbk], writes=[("pb", ba)])
                    bx = nextbank()
                    S.op("pe", lambda e, bx=bx, c=c, xb=xb: e.matmul(pb[bx][:], lhsT=gw[:, 8 + c, :], rhs=xb[:], start=True, stop=True), reads=["gw", xbk], writes=[("pb", bx)])
                    II = self.pool.get()
                    d["II"] = II
                    r_, rk = d["XL"][0], d["XL"][1]
                    i_, ik = II[0], II[1]
                    xc, xck = d["XC"][0], d["XC"][1]
                    S.op("act", lambda e, ba=ba, r_=r_, c=c: e.activation(out=r_[:, 0:TT], in_=pb[ba][:], func=AF.Sigmoid, bias=cv[:, CV_GAB + c:CV_GAB + c + 1]),
                         reads=[("pb", ba), "cv"], writes=[rk])
                    S.op("act", lambda e, bx=bx, i_=i_, c=c: e.activation(out=i_[:, 0:TT], in_=pb[bx][:], func=AF.Sigmoid, bias=cv[:, CV_GXB + c:CV_GXB + c + 1]),
                         reads=[("pb", bx), "cv"], writes=[ik])
                    S.op(self.aux, lambda e, i_=i_, xc=xc: e.tensor_tensor(out=i_[:, 0:TT], in0=i_[:, 0:TT], in1=xc[:, 0:TT], op=ALU.mult),
                         reads=[ik, xck], writes=[ik])

            def s3(self, chunks):
                for c in chunks:
                    d = self.st[c]
                    r_, rk = d["XL"][0], d["XL"][1]
                    a_, ak = d["XC"][0], d["XC"][1]
                    S.op("act", lambda e, r_=r_, a_=a_, c=c: e.activation(out=a_[:, 0:TT], in_=r_[:, 0:TT], func=AF.Exp, scale=cc[:, c:c + 1]),
                         reads=[rk, "cc"], writes=[ak])
                    S.op("act", lambda e, r_=r_, c=c: e.activation(out=r_[:, 0:TT], in_=r_[:, 0:TT], func=AF.Exp, scale=cc[:, 8 + c:9 + c]),
                         reads=[rk, "cc2"], writes=[rk])
                for c in chunks:
                    d = self.st[c]
                    r_, rk = d["XL"][0], d["XL"][1]
                    i_, ik = d["II"][0], d["II"][1]
                    a_, ak = d["XC"][0], d["XC"][1]
                    S.op("act", lambda e, r_=r_: e.activation(out=r_[:, 0:TT], in_=r_[:, 0:TT], func=AF.Sqrt, scale=-1.0, bias=1.0), reads=[rk], writes=[rk])
                    S.op(self.aux, lambda e, i_=i_, r_=r_: e.tensor_tensor(out=i_[:, 0:TT], in0=i_[:, 0:TT], in1=r_[:, 0:TT], op=ALU.mult),
                         reads=[ik, rk], writes=[ik])
                    S.op("dve", lambda e, a_=a_, i_=i_, r_=r_, c=c: e.tensor_tensor_scan(out=r_[:, 0:TT], data0=a_[:, 0:TT], data1=i_[:, 0:TT],
                                                                                        initial=hst[:, c:c + 1], op0=ALU.mult, op1=ALU.add),
                         reads=[ak, ik, ("hst", c)], writes=[rk])
                    self.pool.put(d["XC"])
                    self.pool.put(d["II"])

            def h(self, c):
                return self.st[c]["XL"]

        def save_halo():
            S.op("dve", lambda e: e.tensor_copy(out=xh[:], in_=xnT[:, :, HAL + TT - HAL:HAL + TT]),
                 reads=[("xnT", k, HAL + 384) for k in range(16)], writes=["xh"])

        xst_ctr = [0]
        ppool = Pool_([(ptmp[i], ("ptmp", i), HAL + TT) for i in range(NPT)])

        def prefix_norm(pt):
            srcs = []
            for s in range(4):
                slot = (xst_ctr[0] + s) % 2
                srcs.append((xst[slot], [("xst", slot)]))

            def pre(s):
                slot = xst_ctr[0] % 2
                xst_ctr[0] += 1
                r0 = pt * TT + s * 128
                S.dma("sp", lambda e, slot=slot, r0=r0: e.dma_start(out=xst[slot][:], in_=xs_d[r0:r0 + 128, :]), f"xst{slot}", writes=[("xst", slot)])
            norm_tile(srcs, 0, pre)
            if pt == NPRE - 1:
                save_halo()

        def prefix_all():
            prefix_norm(0)
            for pt in range(NPRE):
                L = Lru(lambda c: (wlru, ("wlru", c // 4), c * 128), ppool, lambda c: (pxcb[c], ("pxcb", c)), "pool")
                A, B = [0, 1, 2, 3], [4, 5, 6, 7]
                emit_casts((len(cast_jobs) + NPRE - 1) // NPRE)
                L.s1(A)
                L.s1(B)
                L.s2(A)
                L.s3(A)
                if pt + 1 < NPRE:
                    prefix_norm(pt + 1)
                L.s2(B)
                L.s3(B)
                for c in range(NCH):
                    H = L.h(c)
                    S.op("dve", lambda e, h=H[0], c=c: e.tensor_scalar(out=hst[:, c:c + 1], in0=h[:, TT - 1:TT], scalar1=cv[:, CV_FLAG + pt:CV_FLAG + pt + 1],
                                                                       scalar2=None, op0=ALU.mult),
                         reads=[H[1], "cv"], writes=[("hst", c)])
                    ppool.put(H)

        dg_ctr = [0]

        def rkeys(s):
            return [("R", s, j) for j in range(4)]

        import os
        STOP = int(os.environ.get("STOP_PHASE", "99"))

        def store_R(ft):
            for s in range(4):
                S.dma("sp", lambda e, s=s: e.dma_start(out=out_d[ft * TT + s * 128: ft * TT + (s + 1) * 128, :], in_=R[s][:]),
                      f"R{s}", reads=rkeys(s), writes=[("out", ft, s)])

        def full_tile(ft):
            t0 = (NPRE + ft) * TT
            S.op("dve", lambda e: e.tensor_copy(out=xnT[:, :, 0:HAL], in_=xh[:]), reads=["xh"], writes=[("xnT", k, 0) for k in range(16)])
            for s in range(4):
                S.dma("sp", lambda e, s=s: e.dma_start(out=R[s][:], in_=xs_d[t0 + s * 128:t0 + (s + 1) * 128, :]), f"R{s}", writes=rkeys(s))
            norm_tile([(R[s], rkeys(s)) for s in range(4)], 0)
            save_halo()
            if STOP <= 1:
                return store_R(ft)
            fpool = Pool_([(tmp[i], ("tmp", i), HAL + TT) for i in range(NT)] + [(gc[:, c, :], ("gc", c), TT) for c in range(NCH)]
                          + [(lnt[i], ("lnt", i), TT) for i in range(3)])
            pieces = {}

            def get_pieces(g):
                if g not in pieces:
                    pieces[g] = (load_piece((winb_d[0 + g], ("winb", 0 + g)), 16, 512),
                                 load_piece((winb_d[2 + g], ("winb", 2 + g)), 16, 512))
                return pieces[g]
            L = Lru(lambda c: (get_pieces(c // 4)[0][0], get_pieces(c // 4)[0][1], (c % 4) * 128), fpool, lambda c: xcb4[c % 4], "dve")
            gys = {}
            xcb4 = [(xcbs[0], ("xcb", 0)), (xcbs[1], ("xcb", 1)), (gstat[0], ("gstat", 0)), (gstat[1], ("gstat", 1))]

            def front(p):
                chunks = [2 * p, 2 * p + 1]
                (wx, wxk), (wy, wyk) = get_pieces(chunks[0] // 4)
                L.s1(chunks)
                for c in chunks:
                    cl = c % 4
                    by = nextbank()

                    def mmy(e, by=by, wy=wy, cl=cl):
                        for k in range(16):
                            m = e.matmul(pb[by][:], lhsT=wy[:, k, cl * 128:(cl + 1) * 128], rhs=xnT[:, k, HAL:HAL + TT], start=(k == 0), stop=(k == 15))
                        return m
                    S.op("pe", mmy, reads=[wyk] + allx, writes=[("pb", by)])
                    GY = fpool.get()
                    gys[c] = GY
                    S.op("act", lambda e, by=by, gy=GY[0]: e.activation(out=gy[:, 0:TT], in_=pb[by][:], func=AF.Gelu_apprx_tanh),
                         reads=[("pb", by)], writes=[GY[1]])

            def back(p):
                chunks = [2 * p, 2 * p + 1]
                L.s2(chunks)
                L.s3(chunks)
                for c in chunks:
                    H = L.h(c)
                    GY = gys[c]
                    S.op("dve", lambda e, h=H[0], c=c: e.tensor_copy(out=hst[:, c:c + 1], in_=h[:, TT - 1:TT]), reads=[H[1]], writes=[("hst", c)])
                    S.op("dve", lambda e, h=H[0], gy=GY[0], c=c: e.tensor_tensor(out=oT[:, c, HAL:HAL + TT], in0=h[:, 0:TT], in1=gy[:, 0:TT], op=ALU.mult),
                         reads=[H[1], GY[1]], writes=[("oT", c)])
                    fpool.put(H)
                    fpool.put(GY)
            front(0)
            for p in range(4):
                if p + 1 < 4:
                    front(p + 1)
                back(p)
            if STOP <= 2:
                return store_R(ft)
            cpieces = {}

            def get_cp(g):
                if g not in cpieces:
                    cpieces[g] = (load_piece((winb_d[4 + g], ("winb", 4 + g)), 16, 512),
                                  load_piece((winb_d[6 + g], ("winb", 6 + g)), 16, 512))
                return cpieces[g]

            def c_front(c):
                (wv, wvk), (wg, wgk) = get_cp(c // 4)
                cl = c % 4
                gk = ("oT", 8 + c)
                ds = c % 2
                S.dma("pool", lambda e: e.dma_start(out=dgb[ds][:], in_=dg_d[c].rearrange("p (k n) -> p k n", k=31), max_dma_last_dim=4096),
                      f"dg{ds}", writes=[("dgb", ds)])
                bgm = nextbank()

                def mmg(e):
                    for k in range(16):
                        m = e.matmul(pb[bgm][:], lhsT=wg[:, k, cl * 128:(cl + 1) * 128], rhs=xnT[:, k, HAL:HAL + TT], start=(k == 0), stop=(k == 15))
                    return m
                S.op("pe", mmg, reads=[wgk] + allx, writes=[("pb", bgm)])
                bgh = nextbank()

                def mmgh(e):
                    for k in range(16):
                        m = e.matmul(pb[bgh][:, 0:HAL], lhsT=wg[:, k, cl * 128:(cl + 1) * 128], rhs=xnT[:, k, 0:HAL], start=(k == 0), stop=(k == 15))
                    for k in range(16):
                        m = e.matmul(pb[bgh][:, 64:64 + HAL], lhsT=wv[:, k, cl * 128:(cl + 1) * 128], rhs=xnT[:, k, 0:HAL],
                                     start=(k == 0), stop=(k == 15), skip_group_check=True)
                    return m
                S.op("pe", mmgh, reads=[wgk, wvk] + allxh, writes=[("pb", bgh)])
                isg = nexttmp()
                sg = tmp[isg]
                S.op("act", lambda e: e.activation(out=sg[:, HAL:HAL + TT], in_=pb[bgm][:], func=AF.Sigmoid), reads=[("pb", bgm)], writes=[("tmp", isg)])
                S.op("act", lambda e: e.activation(out=sg[:, 0:HAL], in_=pb[bgh][:, 0:HAL], func=AF.Sigmoid), reads=[("pb", bgh), ("tmp", isg)], writes=[("tmp", isg)])
                bvm = nextbank()

                def mmv(e):
                    for k in range(16):
                        m = e.matmul(pb[bvm][:], lhsT=wv[:, k, cl * 128:(cl + 1) * 128], rhs=xnT[:, k, HAL:HAL + TT], start=(k == 0), stop=(k == 15))
                    return m
                S.op("pe", mmv, reads=[wvk] + allx, writes=[("pb", bvm)])
                S.op("dve", lambda e: e.tensor_tensor(out=oT[:, 8 + c, HAL:HAL + TT], in0=pb[bvm][:], in1=sg[:, HAL:HAL + TT], op=ALU.mult),
                     reads=[("pb", bvm), ("tmp", isg)], writes=[gk])
                S.op("dve", lambda e: e.tensor_tensor(out=oT[:, 8 + c, 0:HAL], in0=pb[bgh][:, 64:64 + HAL], in1=sg[:, 0:HAL], op=ALU.mult),
                     reads=[("pb", bgh), ("tmp", isg), gk], writes=[gk])

            def c_back(c):
                gk = ("oT", 8 + c)
                ds = c % 2
                bc = nextbank()

                def mmc(e):
                    for k in range(31):
                        m = e.matmul(pb[bc][:], lhsT=dgb[ds][:, k, :], rhs=oT[:, 8 + c, HAL - 30 + k:HAL - 30 + k + TT], start=(k == 0), stop=(k == 30))
                    return m
                S.op("pe", mmc, reads=[("dgb", ds), gk], writes=[("pb", bc)])
                if c > 0:
                    stat_mm(c - 1)
                bcol = cv[:, CV_DWB + c:CV_DWB + c + 1]
                gs = 0
                S.op("act", lambda e: e.activation(out=gc[:, c, :], in_=pb[bc][:], func=AF.Identity, bias=bcol), reads=[("pb", bc), "cv"], writes=[("gc", c)])
                S.op("act", lambda e: e.activation(out=gstat[gs][:], in_=pb[bc][:], func=AF.Identity, bias=bcol), reads=[("pb", bc), "cv"], writes=[("gstat", gs)])
                S.op("act", lambda e: e.activation(out=gstat[gs + 1][:], in_=pb[bc][:], func=AF.Square, bias=bcol), reads=[("pb", bc), "cv"], writes=[("gstat", gs + 1)])

            def stat_mm(c):
                gs = 0
                S.op("pe", lambda e: e.matmul(pstat[0][:], lhsT=ones[:], rhs=gstat[gs][:], start=(c == 0), stop=(c == NCH - 1)),
                     reads=["ones", ("gstat", gs)], writes=[("pstat", 0)])
                S.op("pe", lambda e: e.matmul(pstat[1][:], lhsT=ones[:], rhs=gstat[gs + 1][:], start=(c == 0), stop=(c == NCH - 1)),
                     reads=["ones", ("gstat", gs + 1)], writes=[("pstat", 1)])
            c_front(0)
            for c in range(NCH):
                if c + 1 < NCH:
                    c_front(c + 1)
                c_back(c)
            stat_mm(NCH - 1)
            if STOP <= 3:
                return store_R(ft)
            mean, msq, rs = lnt
            nmr = msq
            S.op("act", lambda e: e.activation(out=mean[:], in_=pstat[0][:], func=AF.Copy, scale=1.0 / 1024), reads=[("pstat", 0)], writes=[("lnt", 0)])
            S.op("act", lambda e: e.activation(out=msq[:], in_=pstat[0][:], func=AF.Square, scale=1.0 / 1024), reads=[("pstat", 0)], writes=[("lnt", 1)])
            S.op("dve", lambda e: e.scalar_tensor_tensor(out=rs[:], in0=pstat[1][:], scalar=1.0 / 1024, in1=msq[:], op0=ALU.mult, op1=ALU.subtract),
                 reads=[("pstat", 1), ("lnt", 1)], writes=[("lnt", 2)])
            S.op("act", lambda e: e.activation(out=rs[:], in_=rs[:], func=AF.Sqrt, bias=EPS, scale=1.0), reads=[("lnt", 2)], writes=[("lnt", 2)])
            S.op("dve", lambda e: e.reciprocal(out=rs[:], in_=rs[:]), reads=[("lnt", 2)], writes=[("lnt", 2)])
            S.op("dve", lambda e: e.scalar_tensor_tensor(out=nmr[:], in0=mean[:], scalar=-1.0, in1=rs[:], op0=ALU.mult, op1=ALU.mult),
                 reads=[("lnt", 0), ("lnt", 2), ("lnt", 1)], writes=[("lnt", 1)])
            for c in range(NCH):
                it = nexttmp()
                t_ = tmp[it]
                S.op("dve", lambda e, c=c, t_=t_: e.tensor_tensor(out=t_[:, 0:TT], in0=gc[:, c, :], in1=rs[:], op=ALU.mult),
                     reads=[("gc", c), ("lnt", 2)], writes=[("tmp", it)])
                S.op("dve", lambda e, t_=t_: e.tensor_tensor(out=t_[:, 0:TT], in0=t_[:, 0:TT], in1=nmr[:], op=ALU.add),
                     reads=[("tmp", it), ("lnt", 1)], writes=[("tmp", it)])
                S.op("act", lambda e, c=c, t_=t_: e.activation(out=oT[:, 8 + c, HAL:HAL + TT], in_=t_[:, 0:TT], func=AF.Silu,
                                                               scale=cv[:, CV_LNG + c:CV_LNG + c + 1], bias=cv[:, CV_LNB + c:CV_LNB + c + 1]),
                     reads=[("tmp", it), "cv"], writes=[("oT", 8 + c)])
            if STOP <= 4:
                return store_R(ft)
            for j in range(4):
                wo, wok = load_piece((woutb_d[j], ("woutb", j)), 16, 512)
                for s in range(4):
                    b = nextbank()

                    def mmo(e, b=b, s=s, wo=wo):
                        for k in range(16):
                            m = e.matmul(pb[b][:], lhsT=oT[:, k, HAL + s * 128:HAL + (s + 1) * 128], rhs=wo[:, k, :], start=(k == 0), stop=(k == 15))
                        return m
                    S.op("pe", mmo, reads=[wok] + [("oT", k) for k in range(16)], writes=[("pb", b)])
                    S.op("dve", lambda e, b=b, s=s, j=j: e.tensor_tensor(out=R[s][:, j * 512:(j + 1) * 512], in0=pb[b][:],
                                                                         in1=R[s][:, j * 512:(j + 1) * 512], op=ALU.add),
                         reads=[("pb", b), ("R", s, j)], writes=[("R", s, j)])
            if STOP <= 5:
                return store_R(ft)
            norm_tile([(R[s], rkeys(s)) for s in range(4)], 1)
            def w1_phase(grp):
                a0 = (grp % 4) * 4
                w1p, w1k = load_piece((w1b_d[grp], ("w1b", grp)), 16, 512)
                for cl in range(4):
                    b = nextbank()

                    def mm1(e, b=b, w1p=w1p, cl=cl):
                        for k in range(16):
                            m = e.matmul(pb[b][:], lhsT=w1p[:, k, cl * 128:(cl + 1) * 128], rhs=xnT[:, k, HAL:HAL + TT], start=(k == 0), stop=(k == 15))
                        return m
                    S.op("pe", mm1, reads=[w1k] + allx, writes=[("pb", b)])
                    rs_ = cl % 2
                    S.op("act", lambda e, b=b, rs_=rs_: e.activation(out=rl[rs_][:], in_=pb[b][:], func=AF.Relu), reads=[("pb", b)], writes=[("rl", rs_)])
                    S.op("dve", lambda e, b=b, rs_=rs_, a0=a0, cl=cl: e.scalar_tensor_tensor(
                        out=oT[:, a0 + cl, HAL:HAL + TT], in0=pb[b][:], scalar=0.0, in1=rl[rs_][:], op0=ALU.max, op1=ALU.mult),
                        reads=[("pb", b), ("rl", rs_)], writes=[("oT", a0 + cl)])

            def w2_phase(grp):
                a0 = (grp % 4) * 4
                w2p, w2k = load_piece((w2b_d[grp], ("w2b", grp)), 4, D)
                for s in range(4):
                    for j in range(4):
                        b = nextbank()

                        def mm2(e, b=b, s=s, j=j, a0=a0, w2p=w2p):
                            for fl in range(4):
                                m = e.matmul(pb[b][:], lhsT=oT[:, a0 + fl, HAL + s * 128:HAL + (s + 1) * 128], rhs=w2p[:, fl, j * 512:(j + 1) * 512],
                                             start=(fl == 0), stop=(fl == 3))
                            return m
                        S.op("pe", mm2, reads=[w2k] + [("oT", a0 + fl) for fl in range(4)], writes=[("pb", b)])
                        S.op("dve", lambda e, b=b, s=s, j=j: e.tensor_tensor(out=R[s][:, j * 512:(j + 1) * 512], in0=pb[b][:],
                                                                             in1=R[s][:, j * 512:(j + 1) * 512], op=ALU.add),
                             reads=[("pb", b), ("R", s, j)], writes=[("R", s, j)])
            NG = 16
            w1_phase(0)
            for grp in range(NG):
                if grp + 1 < NG:
                    w1_phase(grp + 1)
                w2_phase(grp)
            if STOP <= 6:
                return store_R(ft)
            for s in range(4):
                n = nrm_ctr[0]
                nrm_ctr[0] += 1
                slot = n % 2
                col = n % 8
                S.op("act", lambda e, s=s, slot=slot, col=col: e.activation(out=xsb[slot][:], in_=R[s][:], func=AF.Square, accum_out=ssq[:, col:col + 1]),
                     reads=rkeys(s), writes=[("xsb", slot), ("ssq", col)])
                S.op("act", lambda e, col=col: e.activation(out=rstd[:, col:col + 1], in_=ssq[:, col:col + 1], func=AF.Sqrt, scale=1.0 / D, bias=EPS),
                     reads=[("ssq", col)], writes=[("rstd", col)])
                S.op("dve", lambda e, col=col: e.reciprocal(out=rstd[:, col:col + 1], in_=rstd[:, col:col + 1]), reads=[("rstd", col)], writes=[("rstd", col)])
                S.op("dve", lambda e, s=s, col=col: e.scalar_tensor_tensor(out=R[s][:], in0=R[s][:], scalar=rstd[:, col:col + 1], in1=gng[:, 2, :],
                                                                           op0=ALU.mult, op1=ALU.mult),
                     reads=rkeys(s) + [("rstd", col), "gng"], writes=rkeys(s))
                S.dma("sp", lambda e, s=s: e.dma_start(out=out_d[ft * TT + s * 128: ft * TT + (s + 1) * 128, :], in_=R[s][:]),
                      f"R{s}", reads=rkeys(s), writes=[("out", ft, s)])

        if NPRE > 0:
            prefix_all()
            S.barrier()
        for ft in range(NFULL):
            full_tile(ft)
        outkeys = [("out", ft, s) for ft in range(NFULL) for s in range(4)]
        S.op("sp", lambda e: e.nop(), reads=outkeys)

        chans = sorted(set(o["chan"] for o in S.ops if o["kind"] == "dma"))
        sems = {e: es.enter_context(nc.semaphore(f"s_{e}")) for e in ["pe", "act", "dve", "pool", "sp"]}
        chan_sems = {c: es.enter_context(nc.semaphore(f"c_{c}")) for c in chans}
        block = es.enter_context(nc.Block())
        S.emit(block, sems, chan_sems)
        print("ops", len(S.ops), "sbuf bytes", off[0], "counts", {k: v for k, v in S.final_counts.items()})
    return nc


_CACHE = {}


def _host_inputs(inp, NPRE=12, NFULL=4, seq=8192, batch=2, chunks=4):
    f = np.float32
    x = np.asarray(inp["x"], f)
    cvs = []
    per = NFULL * TT

    def pc(v):
        return np.ascontiguousarray(np.asarray(v, f).reshape(NCH, 128).T)
    base = np.zeros((128, CV_N), f)
    base[:, CV_CB:CV_CB + 8] = pc(inp["lru_conv_b"][0])
    base[:, CV_GAB:CV_GAB + 8] = pc(inp["lru_gate_a_b"][0])
    base[:, CV_GXB:CV_GXB + 8] = pc(inp["lru_gate_x_b"][0])
    base[:, CV_LAM:CV_LAM + 8] = pc(inp["lru_lambda"][0])
    base[:, CV_DWB:CV_DWB + 8] = pc(inp["conf_dw_b"][0])
    base[:, CV_LNG:CV_LNG + 8] = pc(inp["conf_ln_g"][0])
    base[:, CV_LNB:CV_LNB + 8] = pc(inp["conf_ln_b"][0])
    c4 = np.asarray(inp["lru_conv_w"][0], f)
    base[:, CV_C4W:CV_C4W + 32] = c4.reshape(4, NCH, 128).transpose(2, 1, 0).reshape(128, 32)
    gng = np.stack([np.broadcast_to(np.asarray(inp["mix_norm_g"][0], f), (128, D)),
                    np.broadcast_to(np.asarray(inp["mlp_norm_g"][0], f), (128, D)),
                    np.broadcast_to(np.asarray(inp["final_norm_g"], f), (128, D))], axis=1)
    gng = np.ascontiguousarray(gng)
    gw = np.zeros((128, 16, 128), f)
    for gi, name in enumerate(["lru_gate_a_w", "lru_gate_x_w"]):
        w = np.asarray(inp[name][0], f)
        for c in range(NCH):
            for hh in range(2):
                gw[hh * 64:(hh + 1) * 64, gi * 8 + c, hh * 64:(hh + 1) * 64] = w[c * 2 + hh]
    dw = np.asarray(inp["conf_dw_w"][0], f)
    dg = np.zeros((NCH, 128, 31, 128), f)
    ar = np.arange(128)
    for c in range(NCH):
        dg[c, ar, :, ar] = dw[:, c * 128:(c + 1) * 128].T
    dg = dg.reshape(NCH, 128, 31 * 128)
    shared = dict(gng=gng, gw=gw, dg=dg,
                  w_in=np.ascontiguousarray(np.asarray(inp["w_in"][0], f)),
                  w_out=np.ascontiguousarray(np.asarray(inp["w_out"][0], f)),
                  w1=np.ascontiguousarray(np.asarray(inp["mlp_w1"][0], f)),
                  w2=np.ascontiguousarray(np.asarray(inp["mlp_w2"][0], f)))
    maps = []
    for core in range(batch * chunks):
        b, q = divmod(core, chunks)
        npad = (NPRE * TT) - q * per
        xs = np.zeros(((NPRE + NFULL) * TT, D), f)
        xs[npad:] = x[b, 0:(q + 1) * per]
        cvc = base.copy()
        for pt in range(NPRE):
            cvc[:, CV_FLAG + pt] = 1.0 if pt * TT >= npad else 0.0
        m = dict(shared)
        m["xs"] = xs
        m["cv"] = cvc
        maps.append(m)
    return maps


def kernel(**inputs):
    NPRE, NFULL = 12, 4
    key = (NPRE, NFULL)
    if key not in _CACHE:
        _CACHE[key] = build_program(NPRE, NFULL)
    nc = _CACHE[key]
    maps = _host_inputs(inputs, NPRE, NFULL)
    res = run_bass_kernel_spmd(nc, maps, core_ids=list(range(8)))
    out = np.zeros((2, 8192, D), np.float32)
    for core in range(8):
        b, q = divmod(core, 4)
        out[b, q * 2048:(q + 1) * 2048] = res.results[core]["out"]
    return out
```

```python
import os
import numpy as np
import concourse.bass as bass
import concourse.mybir as mybir
from concourse.bass_utils import run_bass_kernel_spmd

F32 = mybir.dt.float32
BF16 = mybir.dt.bfloat16
AF = mybir.ActivationFunctionType
ALU = mybir.AluOpType

D = 2048
DIN = 4096
DFF = 8192
TT = 512
HAL = 32
NCH = 8
EPS = 1e-6
CV_CB, CV_GAB, CV_GXB, CV_LAM, CV_DWB, CV_LNG, CV_LNB = 0, 8, 16, 24, 32, 40, 48
CV_C4W = 56
CV_FLAG = 88
CV_N = 104


class Sched:
    def __init__(self, nc):
        self.nc = nc
        self.ops = []
        self.last_w = {}
        self.readers = {}
        self.eng_objs = {"pe": nc.tensor, "act": nc.scalar, "dve": nc.vector, "pool": nc.gpsimd, "sp": nc.sync}

    def _deps(self, eng, reads, writes):
        deps = set()
        for k in reads:
            w = self.last_w.get(k)
            if w is not None:
                deps.add(w)
        for k in writes:
            w = self.last_w.get(k)
            if w is not None:
                deps.add(w)
            for r in self.readers.get(k, ()):
                deps.add(r)
        return deps

    def _add(self, eng, fn, reads, writes, kind, chan=None):
        oid = len(self.ops)
        deps = self._deps(eng, reads, writes)
        keep = set()
        raw = set(self.last_w.get(k) for k in reads if self.last_w.get(k) is not None)
        for d in deps:
            od = self.ops[d]
            if od["kind"] == "dma" or kind == "dma":
                keep.add(d)
            elif od["eng"] != eng:
                keep.add(d)
            elif d in raw and eng != "pe":
                keep.add(d)
        self.ops.append(dict(eng=eng, fn=fn, deps=keep, kind=kind, chan=chan, sig=False))
        for k in reads:
            self.readers.setdefault(k, []).append(oid)
        for k in writes:
            self.last_w[k] = oid
            self.readers[k] = []
        return oid

    def op(self, eng, fn, reads=(), writes=()):
        return self._add(eng, fn, list(reads), list(writes), "cmp")

    def dma(self, queue, fn, chan, reads=(), writes=()):
        return self._add(queue, fn, list(reads), list(writes), "dma", chan)

    def barrier(self):
        last = {}
        for i, o in enumerate(self.ops):
            if o["kind"] == "dma":
                last[("c", o["chan"])] = i
            else:
                last[("e", o["eng"])] = i
        deps = set(last.values())
        for eng in ["pe", "act", "dve", "pool", "sp"]:
            self.ops.append(dict(eng=eng, fn=lambda e: e.nop(), deps=set(deps), kind="cmp", chan=None, sig=False))

    def emit(self, block, sems, chan_sems):
        ops = self.ops
        for o in ops:
            for d in o["deps"]:
                ops[d]["sig"] = True
        cnt = {}
        for i, o in enumerate(ops):
            if o["kind"] == "dma":
                c = o["chan"]
                cnt[c] = cnt.get(c, 0) + 16
                o["ev"] = (("c", c), cnt[c])
            elif o["sig"]:
                e = o["eng"]
                cnt[e] = cnt.get(e, 0) + 1
                o["ev"] = (("e", e), cnt[e])
            else:
                o["ev"] = None
        const_total = {c: v for c, v in cnt.items() if isinstance(c, str) and c.startswith("const")}
        self.final_counts = cnt
        by_eng = {}
        for i, o in enumerate(ops):
            by_eng.setdefault(o["eng"], []).append(i)

        def semof(key):
            return chan_sems[key[1]] if key[0] == "c" else sems[key[1]]

        snap = {}
        nwaits = [0]

        def run_engine(ename, engobj):
            known = {}
            for i in by_eng.get(ename, []):
                o = ops[i]
                need = {}
                for d in o["deps"]:
                    key, val = ops[d]["ev"]
                    if key[0] == "c" and key[1] in const_total:
                        val = const_total[key[1]]
                    if need.get(key, 0) < val:
                        need[key] = val
                for key, val in sorted(need.items(), key=lambda kv: -kv[1]):
                    if known.get(key, 0) >= val:
                        continue
                    engobj.wait_ge(semof(key), val)
                    nwaits[0] += 1
                    known[key] = val
                    sn = snap.get((key, val))
                    if sn:
                        for k2, v2 in sn.items():
                            if known.get(k2, 0) < v2:
                                known[k2] = v2
                inst = o["fn"](engobj)
                if o["ev"] is not None:
                    key, val = o["ev"]
                    if o["kind"] == "dma":
                        inst.then_inc(semof(key), 16)
                    else:
                        inst.then_inc(semof(key), 1)
                        snap[(key, val)] = dict(known)
                        known[key] = max(known.get(key, 0), 0)

        @block.sync
        def _(e):
            run_engine("sp", e)

        @block.gpsimd
        def _(e):
            run_engine("pool", e)

        @block.scalar
        def _(e):
            run_engine("act", e)

        @block.vector
        def _(e):
            run_engine("dve", e)

        @block.tensor
        def _(e):
            run_engine("pe", e)


def build_program(NPRE, NFULL):
    nc = bass.Bass("TRN2", target_bir_lowering=False)
    NTILE = NPRE + NFULL
    xs_d = nc.dram_tensor("xs", [NTILE * TT, D], F32, kind="ExternalInput").ap()
    cv_d = nc.dram_tensor("cv", [128, CV_N], F32, kind="ExternalInput").ap()
    gng_d = nc.dram_tensor("gng", [128, 3, D], F32, kind="ExternalInput").ap()
    gw_d = nc.dram_tensor("gw", [128, 16, 128], F32, kind="ExternalInput").ap()
    dg_d = nc.dram_tensor("dg", [NCH, 128, 31 * 128], F32, kind="ExternalInput").ap()
    dg4_d = nc.dram_tensor("dg4", [128, 32, 128], F32, kind="ExternalInput").ap()
    win_d = nc.dram_tensor("w_in", [D, DIN], F32, kind="ExternalInput").ap()
    wout_d = nc.dram_tensor("w_out", [D, D], F32, kind="ExternalInput").ap()
    w1_d = nc.dram_tensor("w1", [D, DFF], F32, kind="ExternalInput").ap()
    w2_d = nc.dram_tensor("w2", [DFF, D], F32, kind="ExternalInput").ap()
    out_d = nc.dram_tensor("out", [NFULL * TT, D], F32, kind="ExternalOutput").ap()
    w1b_d = nc.dram_tensor("w1b", [16, 128, 8192], BF16, kind="Internal").ap()
    w2b_d = nc.dram_tensor("w2b", [16, 128, 8192], BF16, kind="Internal").ap()

    S = Sched(nc)
    import contextlib
    es = contextlib.ExitStack()
    off = [16384]

    def sb(name, shape, dt, at=None):
        nbytes = int(np.prod(shape[1:])) * (4 if dt == F32 else 2)
        nbytes = (nbytes + 63) // 64 * 64
        if at is None:
            at = off[0]
            off[0] += nbytes
        return nc.alloc_sbuf_tensor_at(name, shape, dt, offset=at), at + nbytes

    def sbp(name, shape, dt):
        return sb(name, shape, dt)[0]

    def ps(name, shape, dt):
        return es.enter_context(nc.psum_tensor(name, shape, dt))

    with es:
        cv = sbp("cv", [128, CV_N], F32)
        cc = sbp("cc", [128, 16], F32)
        gng = sbp("gng", [128, 3, D], F32)
        gw = sbp("gw", [128, 16, 128], BF16)
        ident = sbp("ident", [128, 128], BF16)
        identf = sbp("identf", [128, 128], F32)
        ones = sbp("ones", [128, 128], BF16)
        hst = sbp("hst", [128, NCH], F32)
        hal = sbp("hal", [128, NCH, 4], F32)
        xnT = sbp("xnT", [128, 16, HAL + TT], BF16)
        xsb = [sbp(f"xsb{i}", [128, D], BF16) for i in range(2)]
        ssq = sbp("ssq", [128, 8], F32)
        rstd = sbp("rstd", [128, 8], F32)
        NT = 5
        tmp = [sbp(f"tmp{i}", [128, HAL + TT], F32) for i in range(NT)]
        xcbs = [sbp(f"xcb{i}", [128, TT], BF16) for i in range(2)]
        xh = sbp("xh", [128, 16, HAL], BF16)
        base = off[0]
        wlru = sbp("wlru", [128, 16, 1024], BF16)
        xst = [sbp(f"xst{i}", [128, D], F32) for i in range(2)]
        NPT = 27
        ptmp = [sbp(f"ptmp{i}", [128, HAL + TT], F32) for i in range(NPT)]
        dg4 = sbp("dg4", [128, 32, 128], BF16)
        xlb = [sbp(f"xlb{i}", [128, HAL + TT], BF16) for i in range(NCH)]
        halb = sbp("halb", [128, NCH, 4], BF16)
        pxcb = [sbp(f"pxcb{i}", [128, TT], BF16) for i in range(NCH)]
        assert off[0] < 229000, off[0]
        off[0] = base
        WS = 3
        wring = [sbp(f"wr{i}", [128, 16 * 512], BF16) for i in range(WS)]
        R = [sbp(f"R{i}", [128, D], F32) for i in range(4)]
        oT = sbp("oT", [128, 16, HAL + TT], BF16)
        gc = sbp("gc", [128, NCH, TT], F32)
        gstat = [sbp(f"gstat{i}", [128, TT], BF16) for i in range(2)]
        dgb = [sbp(f"dgb{i}", [128, 31, 128], BF16) for i in range(2)]
        lnt = [sbp(f"lnt{i}", [128, TT], F32) for i in range(3)]
        rl = [sbp(f"rl{i}", [128, TT], BF16) for i in range(2)]
        assert off[0] < 229000, off[0]
        NB = 4
        pb = [ps(f"pb{i}", [128, 512], F32) for i in range(NB)]
        pstat = [ps(f"pst{i}", [128, 512], F32) for i in range(2)]
        ptr = [ps(f"ptr{i}", [128, 8, 128], BF16) for i in range(2)]
        bank_ctr = [0]

        def nextbank():
            b = bank_ctr[0] % NB
            bank_ctr[0] += 1
            return b

        S.dma("sp", lambda e: e.dma_start(out=cv[:], in_=cv_d), "const_sp", writes=["cv"])
        S.dma("sp", lambda e: e.dma_start(out=gng[:], in_=gng_d), "const_sp", writes=["gng"])
        S.dma("pool", lambda e: e.dma_start(out=gw[:], in_=gw_d), "const_pool", writes=["gw"])
        win_v = win_d.rearrange("(k p) n -> p k n", p=128)
        wout_v = wout_d.rearrange("(k p) n -> p k n", p=128)
        w1_v = w1_d.rearrange("(k p) n -> p k n", p=128)
        w2_v = w2_d.rearrange("(k p) n -> p k n", p=128)
        if NPRE > 0:
            for h in range(2):
                S.dma("pool", lambda e, h=h: e.dma_start(out=wlru[:, :, h * 512:(h + 1) * 512], in_=win_v[:, :, h * 512:(h + 1) * 512]),
                      "const_pool", writes=[("wlru", h)])
        if NPRE > 0:
            S.dma("pool", lambda e: e.dma_start(out=dg4[:], in_=dg4_d), "const_pool", writes=["dg4"])
            S.op("dve", lambda e: e.memset(halb[:], 0.0), writes=[("halb", c) for c in range(NCH)])
        S.op("pool", lambda e: e.memset(identf[:], 0.0), writes=["identf"])
        S.op("pool", lambda e: e.affine_select(out=identf[:], in_=identf[:], pattern=[[-1, 128]], compare_op=ALU.not_equal,
                                               fill=1.0, base=0, channel_multiplier=1), reads=["identf"], writes=["identf"])
        S.op("dve", lambda e: e.tensor_copy(out=ident[:], in_=identf[:]), reads=["identf"], writes=["ident"])
        S.op("dve", lambda e: e.memset(ones[:], 1.0), writes=["ones"])
        S.op("dve", lambda e: e.memset(hst[:], 0.0), writes=[("hst", c) for c in range(NCH)])
        S.op("dve", lambda e: e.memset(hal[:], 0.0), writes=[("hal", c) for c in range(NCH)])
        S.op("act", lambda e: e.activation(out=cc[:, 0:8], in_=cv[:, CV_LAM:CV_LAM + 8], func=AF.Exp, scale=-1.0),
             reads=["cv"], writes=["cc"])
        S.op("act", lambda e: e.activation(out=cc[:, 0:8], in_=cc[:, 0:8], func=AF.Ln, bias=1.0, scale=1.0),
             reads=["cc"], writes=["cc"])
        S.op("dve", lambda e: e.tensor_scalar(out=cc[:, 8:16], in0=cc[:, 0:8], scalar1=-16.0, scalar2=None, op0=ALU.mult),
             reads=["cc"], writes=["cc2"])
        S.op("dve", lambda e: e.tensor_scalar(out=cc[:, 0:8], in0=cc[:, 0:8], scalar1=-8.0, scalar2=None, op0=ALU.mult),
             reads=["cc", "cc2"], writes=["cc"])

        cast_jobs = []
        PRECAST = lambda g: (g % 3 == 0)
        for g in range(16):
            if PRECAST(g):
                cast_jobs.append((w1b_d[g].rearrange("p (k n) -> p k n", k=16), w1_v[:, :, g * 512:(g + 1) * 512], ("w1b", g)))
                cast_jobs.append((w2b_d[g].rearrange("p (k n) -> p k n", k=4), w2_v[:, g * 4:(g + 1) * 4, :], ("w2b", g)))
        cast_ctr = [0]
        NCASTCH = 6

        def emit_casts(n):
            for _ in range(n):
                i = cast_ctr[0]
                if i >= len(cast_jobs):
                    return
                cast_ctr[0] += 1
                o_ap, i_ap, key = cast_jobs[i]
                S.dma("pool", lambda e, o_ap=o_ap, i_ap=i_ap: e.dma_start(out=o_ap, in_=i_ap), f"wcast{i % NCASTCH}",
                      writes=[key, ("wcastslot", i % NCASTCH)])

        nrm_ctr = [0]

        def norm_front(src, srck, np_, gidx):
            n = nrm_ctr[0]
            nrm_ctr[0] += 1
            slot = n % 2
            col = n % 8
            S.op("act", lambda e: e.activation(out=xsb[slot][0:np_, :], in_=src[0:np_, :], func=AF.Square, accum_out=ssq[0:np_, col:col + 1]),
                 reads=srck, writes=[("xsb", slot), ("ssq", col)])
            S.op("act", lambda e: e.activation(out=rstd[0:np_, col:col + 1], in_=ssq[0:np_, col:col + 1], func=AF.Sqrt, scale=1.0 / D, bias=EPS),
                 reads=[("ssq", col)], writes=[("rstd", col)])
            S.op("dve", lambda e: e.reciprocal(out=rstd[0:np_, col:col + 1], in_=rstd[0:np_, col:col + 1]),
                 reads=[("rstd", col)], writes=[("rstd", col)])
            S.op("dve", lambda e: e.scalar_tensor_tensor(out=xsb[slot][0:np_, :], in0=src[0:np_, :], scalar=rstd[0:np_, col:col + 1],
                                                         in1=gng[0:np_, gidx, :], op0=ALU.mult, op1=ALU.mult),
                 reads=list(srck) + [("rstd", col), "gng"], writes=[("xsb", slot)])
            return slot

        def norm_back(slot, np_, c0):
            for half in range(2):
                def tr(e, half=half):
                    for j in range(8):
                        k = half * 8 + j
                        mm = e.transpose(out=ptr[half][:, j, 0:np_], in_=xsb[slot][0:np_, k * 128:(k + 1) * 128], identity=ident[0:np_, 0:np_])
                    return mm
                S.op("pe", tr, reads=[("xsb", slot), "ident"], writes=[("ptr", half)])
                wk = [("xnT", half * 8 + j, c0) for j in range(8)]
                if half == 0:
                    S.op("act", lambda e: e.activation(out=xnT[:, 0:8, c0:c0 + np_], in_=ptr[0][:, :, 0:np_], func=AF.Copy),
                         reads=[("ptr", 0)], writes=wk)
                else:
                    S.op("dve", lambda e: e.tensor_copy(out=xnT[:, 8:16, c0:c0 + np_], in_=ptr[1][:, :, 0:np_]),
                         reads=[("ptr", 1)], writes=wk)

        def norm_tile(srcs, gidx, pre=None):
            slots = {}
            for s in range(4):
                if pre is not None:
                    pre(s)
                slots[s] = norm_front(srcs[s][0], srcs[s][1], 128, gidx)
                if s >= 1:
                    norm_back(slots[s - 1], 128, HAL + (s - 1) * 128)
            norm_back(slots[3], 128, HAL + 3 * 128)

        allx = [("xnT", k, HAL + s * 128) for k in range(16) for s in range(4)]
        allxh = allx + [("xnT", k, 0) for k in range(16)]

        wr_ctr = [0]

        def load_piece(src, nk, ncol):
            src_ap, skey = src
            slot = wr_ctr[0] % WS
            wr_ctr[0] += 1
            view = wring[slot][:, 0:nk * ncol].rearrange("p (k n) -> p k n", k=nk)
            if skey is None:
                S.dma("pool", lambda e: e.dma_start(out=view, in_=src_ap), f"wr{slot}", writes=[("wr", slot)])
            else:
                S.dma("pool", lambda e: e.dma_start(out=wring[slot][:, 0:nk * ncol], in_=src_ap), f"wr{slot}", reads=[skey], writes=[("wr", slot)])
            return view, ("wr", slot)

        tmp_ctr = [0]

        def nexttmp():
            i = tmp_ctr[0] % NT
            tmp_ctr[0] += 1
            return i

        class Pool_:
            def __init__(self, items):
                self.free = list(items)

            def get(self, wide=False):
                for i, it in enumerate(self.free):
                    if (it[2] >= HAL + TT) == wide:
                        return self.free.pop(i)
                for i, it in enumerate(self.free):
                    if it[2] >= HAL + TT:
                        return self.free.pop(i)
                raise RuntimeError("scratch pool exhausted")

            def put(self, it):
                self.free.append(it)

        class Lru:
            def __init__(self, wsel, pool, xcb_of, aux_eng, conv_pe=False, save_hal=False):
                self.wsel, self.pool, self.xcb_of, self.aux = wsel, pool, xcb_of, aux_eng
                self.conv_pe, self.save_hal = conv_pe, save_hal
                self.st = {}

            def s1_pe(self, chunks):
                pool = self.pool

                def head(c):
                    wview, wkey, wcol0 = self.wsel(c)
                    b = nextbank()

                    def mm(e, b=b, wview=wview, wcol0=wcol0):
                        for k in range(16):
                            m = e.matmul(pb[b][:], lhsT=wview[:, k, wcol0:wcol0 + 128], rhs=xnT[:, k, HAL:HAL + TT], start=(k == 0), stop=(k == 15))
                        return m
                    S.op("pe", mm, reads=[wkey] + allx, writes=[("pb", b)])
                    xl = xlb[c]
                    xlk = ("xlb", c)
                    S.op("act", lambda e, b=b, xl=xl: e.activation(out=xl[:, HAL:HAL + TT], in_=pb[b][:], func=AF.Copy), reads=[("pb", b)], writes=[xlk])
                    S.op("dve", lambda e, xl=xl, c=c: e.tensor_copy(out=xl[:, HAL - 3:HAL], in_=halb[:, c, 0:3]), reads=[("halb", c), xlk], writes=[xlk])
                    if self.save_hal:
                        S.op("dve", lambda e, b=b, c=c: e.tensor_copy(out=hal[:, c, 0:3], in_=pb[b][:, TT - 3:TT]), reads=[("pb", b)], writes=[("hal", c)])

                def tail(c):
                    xl = xlb[c]
                    xlk = ("xlb", c)
                    b2 = nextbank()

                    def mc(e, b2=b2, xl=xl, c=c):
                        for k in range(4):
                            m = e.matmul(pb[b2][:], lhsT=dg4[:, c * 4 + k, :], rhs=xl[:, HAL - 3 + k:HAL - 3 + k + TT], start=(k == 0), stop=(k == 3))
                        return m
                    S.op("pe", mc, reads=["dg4", xlk], writes=[("pb", b2)])
                    S.op("dve", lambda e, xl=xl, c=c: e.tensor_copy(out=halb[:, c, 0:3], in_=xl[:, HAL + TT - 3:HAL + TT]), reads=[xlk], writes=[("halb", c)])
                    RB = pool.get(wide=True)
                    XC = pool.get()
                    xc, xck = XC[0], XC[1]
                    S.op("dve", lambda e, b2=b2, xc=xc, c=c: e.tensor_scalar(out=xc[:, 0:TT], in0=pb[b2][:], scalar1=cv[:, CV_CB + c:CV_CB + c + 1],
                                                                           scalar2=None, op0=ALU.add),
                         reads=[("pb", b2), "cv"], writes=[xck])
                    xb, xbk = self.xcb_of(c)
                    S.op("dve", lambda e, xc=xc, xb=xb: e.tensor_copy(out=xb[:], in_=xc[:, 0:TT]), reads=[xck], writes=[xbk])
                    self.st[c] = dict(XL=RB, XC=XC, xb=xb, xbk=xbk)
                for i, c in enumerate(chunks):
                    head(c)
                    if i > 0:
                        tail(chunks[i - 1])
                tail(chunks[-1])

            def s1(self, chunks):
                if self.conv_pe:
                    return self.s1_pe(chunks)
                pool = self.pool
                for c in chunks:
                    wview, wkey, wcol0 = self.wsel(c)
                    b = nextbank()

                    def mm(e, b=b, wview=wview, wcol0=wcol0):
                        for k in range(16):
                            m = e.matmul(pb[b][:], lhsT=wview[:, k, wcol0:wcol0 + 128], rhs=xnT[:, k, HAL:HAL + TT], start=(k == 0), stop=(k == 15))
                        return m
                    S.op("pe", mm, reads=[wkey] + allx, writes=[("pb", b)])
                    XL = pool.get(wide=True)
                    XC = pool.get()
                    xl, xlk = XL[0], XL[1]
                    xc, xck = XC[0], XC[1]
                    S.op("act", lambda e, b=b, xl=xl: e.activation(out=xl[:, HAL:HAL + TT], in_=pb[b][:], func=AF.Copy), reads=[("pb", b)], writes=[xlk])
                    S.op("dve", lambda e, xl=xl, c=c: e.tensor_copy(out=xl[:, HAL - 3:HAL], in_=hal[:, c, 0:3]), reads=[("hal", c), xlk], writes=[xlk])
                    w0 = CV_C4W + c * 4
                    S.op("dve", lambda e, xl=xl, xc=xc, w0=w0, c=c: e.tensor_scalar(out=xc[:, 0:TT], in0=xl[:, HAL - 3:HAL - 3 + TT], scalar1=cv[:, w0:w0 + 1],
                                                                                  scalar2=cv[:, CV_CB + c:CV_CB + c + 1], op0=ALU.mult, op1=ALU.add),
                         reads=[xlk, "cv"], writes=[xck])
                    for k in range(1, 4):
                        S.op("dve", lambda e, xl=xl, xc=xc, w0=w0, k=k: e.scalar_tensor_tensor(out=xc[:, 0:TT], in0=xl[:, HAL - 3 + k:HAL - 3 + k + TT],
                                                                                             scalar=cv[:, w0 + k:w0 + k + 1], in1=xc[:, 0:TT], op0=ALU.mult, op1=ALU.add),
                             reads=[xlk, xck, "cv"], writes=[xck])
                    S.op("dve", lambda e, xl=xl, c=c: e.tensor_copy(out=hal[:, c, 0:3], in_=xl[:, HAL + TT - 3:HAL + TT]), reads=[xlk], writes=[("hal", c)])
                    xb, xbk = self.xcb_of(c)
                    S.op("dve", lambda e, xc=xc, xb=xb: e.tensor_copy(out=xb[:], in_=xc[:, 0:TT]), reads=[xck], writes=[xbk])
                    self.st[c] = dict(XL=XL, XC=XC, xb=xb, xbk=xbk)

            def s2(self, chunks):
                for c in chunks:
                    d = self.st[c]
                    xb, xbk = d["xb"], d["xbk"]
                    ba = nextbank()
                    S.op("pe", lambda e, ba=ba, c=c, xb=xb: e.matmul(pb[ba][:], lhsT=gw[:, c, :], rhs=xb[:], start=True, stop=True), reads=["gw", xbk], writes=[("pb", ba)])
                    bx = nextbank()
                    S.op("pe", lambda e, bx=bx, c=c, xb=xb: e.matmul(pb[bx][:], lhsT=gw[:, 8 + c, :], rhs=xb[:], start=True, stop=True), reads=["gw", xbk], writes=[("pb", bx)])
                    II = self.pool.get()
                    d["II"] = II
                    r_, rk = d["XL"][0], d["XL"][1]
                    i_, ik = II[0], II[1]
                    xc, xck = d["XC"][0], d["XC"][1]
                    S.op("act", lambda e, ba=ba, r_=r_, c=c: e.activation(out=r_[:, 0:TT], in_=pb[ba][:], func=AF.Sigmoid, bias=cv[:, CV_GAB + c:CV_GAB + c + 1]),
                         reads=[("pb", ba), "cv"], writes=[rk])
                    S.op("act", lambda e, bx=bx, i_=i_, c=c: e.activation(out=i_[:, 0:TT], in_=pb[bx][:], func=AF.Sigmoid, bias=cv[:, CV_GXB + c:CV_GXB + c + 1]),
                         reads=[("pb", bx), "cv"], writes=[ik])
                    S.op(self.aux, lambda e, i_=i_, xc=xc: e.tensor_tensor(out=i_[:, 0:TT], in0=i_[:, 0:TT], in1=xc[:, 0:TT], op=ALU.mult),
                         reads=[ik, xck], writes=[ik])

            def s3(self, chunks):
                for c in chunks:
                    d = self.st[c]
                    r_, rk = d["XL"][0], d["XL"][1]
                    a_, ak = d["XC"][0], d["XC"][1]
                    S.op("act", lambda e, r_=r_, a_=a_, c=c: e.activation(out=a_[:, 0:TT], in_=r_[:, 0:TT], func=AF.Exp, scale=cc[:, c:c + 1]),
                         reads=[rk, "cc"], writes=[ak])
                    S.op("act", lambda e, r_=r_, c=c: e.activation(out=r_[:, 0:TT], in_=r_[:, 0:TT], func=AF.Exp, scale=cc[:, 8 + c:9 + c]),
                         reads=[rk, "cc2"], writes=[rk])
                for c in chunks:
                    d = self.st[c]
                    r_, rk = d["XL"][0], d["XL"][1]
                    i_, ik = d["II"][0], d["II"][1]
                    a_, ak = d["XC"][0], d["XC"][1]
                    S.op("act", lambda e, r_=r_: e.activation(out=r_[:, 0:TT], in_=r_[:, 0:TT], func=AF.Sqrt, scale=-1.0, bias=1.0), reads=[rk], writes=[rk])
                    S.op(self.aux, lambda e, i_=i_, r_=r_: e.tensor_tensor(out=i_[:, 0:TT], in0=i_[:, 0:TT], in1=r_[:, 0:TT], op=ALU.mult),
                         reads=[ik, rk], writes=[ik])
                    S.op("dve", lambda e, a_=a_, i_=i_, r_=r_, c=c: e.tensor_tensor_scan(out=r_[:, 0:TT], data0=a_[:, 0:TT], data1=i_[:, 0:TT],
                                                                                        initial=hst[:, c:c + 1], op0=ALU.mult, op1=ALU.add),
                         reads=[ak, ik, ("hst", c)], writes=[rk])
                    self.pool.put(d["XC"])
                    self.pool.put(d["II"])

            def h(self, c):
                return self.st[c]["XL"]

        def save_halo():
            S.op("dve", lambda e: e.tensor_copy(out=xh[:], in_=xnT[:, :, HAL + TT - HAL:HAL + TT]),
                 reads=[("xnT", k, HAL + 384) for k in range(16)], writes=["xh"])

        xst_ctr = [0]
        ppool = Pool_([(ptmp[i], ("ptmp", i), HAL + TT) for i in range(NPT)])

        def prefix_norm(pt):
            srcs = []
            for s in range(4):
                slot = (xst_ctr[0] + s) % 2
                srcs.append((xst[slot], [("xst", slot)]))

            def pre(s):
                slot = xst_ctr[0] % 2
                xst_ctr[0] += 1
                r0 = pt * TT + s * 128
                S.dma("sp", lambda e, slot=slot, r0=r0: e.dma_start(out=xst[slot][:], in_=xs_d[r0:r0 + 128, :]), f"xst{slot}", writes=[("xst", slot)])
            norm_tile(srcs, 0, pre)
            if pt == NPRE - 1:
                save_halo()

        def prefix_all():
            prefix_norm(0)
            for pt in range(NPRE):
                L = Lru(lambda c: (wlru, ("wlru", c // 4), c * 128), ppool, lambda c: (pxcb[c], ("pxcb", c)), "pool",
                        conv_pe=True, save_hal=(pt == NPRE - 1))
                A, B = [0, 1, 2, 3], [4, 5, 6, 7]
                emit_casts(int(os.environ.get("CASTS_PER_TILE", (len(cast_jobs) + NPRE - 1) // NPRE)))
                L.s1(A + B)
                if pt + 1 < NPRE:
                    prefix_norm(pt + 1)
                L.s2(A)
                L.s3(A)
                L.s2(B)
                L.s3(B)
                for c in range(NCH):
                    H = L.h(c)
                    S.op("dve", lambda e, h=H[0], c=c: e.tensor_scalar(out=hst[:, c:c + 1], in0=h[:, TT - 1:TT], scalar1=cv[:, CV_FLAG + pt:CV_FLAG + pt + 1],
                                                                       scalar2=None, op0=ALU.mult),
                         reads=[H[1], "cv"], writes=[("hst", c)])
                    ppool.put(H)

        dg_ctr = [0]

        def rkeys(s):
            return [("R", s, j) for j in range(4)]

        import os
        STOP = int(os.environ.get("STOP_PHASE", "99"))

        def store_R(ft):
            for s in range(4):
                S.dma("sp", lambda e, s=s: e.dma_start(out=out_d[ft * TT + s * 128: ft * TT + (s + 1) * 128, :], in_=R[s][:]),
                      f"R{s}", reads=rkeys(s), writes=[("out", ft, s)])

        def full_tile(ft):
            t0 = (NPRE + ft) * TT
            S.op("dve", lambda e: e.tensor_copy(out=xnT[:, :, 0:HAL], in_=xh[:]), reads=["xh"], writes=[("xnT", k, 0) for k in range(16)])
            for s in range(4):
                S.dma("sp", lambda e, s=s: e.dma_start(out=R[s][:], in_=xs_d[t0 + s * 128:t0 + (s + 1) * 128, :]), f"R{s}", writes=rkeys(s))
            norm_tile([(R[s], rkeys(s)) for s in range(4)], 0)
            save_halo()
            if STOP <= 1:
                return store_R(ft)
            fpool = Pool_([(tmp[i], ("tmp", i), HAL + TT) for i in range(NT)] + [(gc[:, c, :], ("gc", c), TT) for c in range(NCH)]
                          + [(lnt[i], ("lnt", i), TT) for i in range(3)])
            pieces = {}

            def get_pieces(g):
                if g not in pieces:
                    pieces[g] = (load_piece((win_v[:, :, g * 512:(g + 1) * 512], None), 16, 512),
                                 load_piece((win_v[:, :, 1024 + g * 512:1024 + (g + 1) * 512], None), 16, 512))
                return pieces[g]
            L = Lru(lambda c: (get_pieces(c // 4)[0][0], get_pieces(c // 4)[0][1], (c % 4) * 128), fpool, lambda c: xcb4[c % 4], "dve")
            gys = {}
            xcb4 = [(xcbs[0], ("xcb", 0)), (xcbs[1], ("xcb", 1)), (gstat[0], ("gstat", 0)), (gstat[1], ("gstat", 1))]

            def front(p):
                chunks = [2 * p, 2 * p + 1]
                (wx, wxk), (wy, wyk) = get_pieces(chunks[0] // 4)
                L.s1(chunks)
                for c in chunks:
                    cl = c % 4
                    by = nextbank()

                    def mmy(e, by=by, wy=wy, cl=cl):
                        for k in range(16):
                            m = e.matmul(pb[by][:], lhsT=wy[:, k, cl * 128:(cl + 1) * 128], rhs=xnT[:, k, HAL:HAL + TT], start=(k == 0), stop=(k == 15))
                        return m
                    S.op("pe", mmy, reads=[wyk] + allx, writes=[("pb", by)])
                    GY = fpool.get()
                    gys[c] = GY
                    S.op("act", lambda e, by=by, gy=GY[0]: e.activation(out=gy[:, 0:TT], in_=pb[by][:], func=AF.Gelu_apprx_tanh),
                         reads=[("pb", by)], writes=[GY[1]])

            def back(p):
                chunks = [2 * p, 2 * p + 1]
                L.s2(chunks)
                L.s3(chunks)
                for c in chunks:
                    H = L.h(c)
                    GY = gys[c]
                    S.op("dve", lambda e, h=H[0], c=c: e.tensor_copy(out=hst[:, c:c + 1], in_=h[:, TT - 1:TT]), reads=[H[1]], writes=[("hst", c)])
                    S.op("dve", lambda e, h=H[0], gy=GY[0], c=c: e.tensor_tensor(out=oT[:, c, HAL:HAL + TT], in0=h[:, 0:TT], in1=gy[:, 0:TT], op=ALU.mult),
                         reads=[H[1], GY[1]], writes=[("oT", c)])
                    fpool.put(H)
                    fpool.put(GY)
            front(0)
            for p in range(4):
                if p + 1 < 4:
                    front(p + 1)
                back(p)
            if STOP <= 2:
                return store_R(ft)
            cpieces = {}

            def get_cp(g):
                if g not in cpieces:
                    cpieces[g] = (load_piece((win_v[:, :, 2048 + g * 512:2048 + (g + 1) * 512], None), 16, 512),
                                  load_piece((win_v[:, :, 3072 + g * 512:3072 + (g + 1) * 512], None), 16, 512))
                return cpieces[g]

            def c_front(c):
                (wv, wvk), (wg, wgk) = get_cp(c // 4)
                cl = c % 4
                gk = ("oT", 8 + c)
                ds = c % 2
                S.dma("pool", lambda e: e.dma_start(out=dgb[ds][:], in_=dg_d[c].rearrange("p (k n) -> p k n", k=31), max_dma_last_dim=4096),
                      f"dg{ds}", writes=[("dgb", ds)])
                bgm = nextbank()

                def mmg(e):
                    for k in range(16):
                        m = e.matmul(pb[bgm][:], lhsT=wg[:, k, cl * 128:(cl + 1) * 128], rhs=xnT[:, k, HAL:HAL + TT], start=(k == 0), stop=(k == 15))
                    return m
                S.op("pe", mmg, reads=[wgk] + allx, writes=[("pb", bgm)])
                bgh = nextbank()

                def mmgh(e):
                    for k in range(16):
                        m = e.matmul(pb[bgh][:, 0:HAL], lhsT=wg[:, k, cl * 128:(cl + 1) * 128], rhs=xnT[:, k, 0:HAL], start=(k == 0), stop=(k == 15))
                    for k in range(16):
                        m = e.matmul(pb[bgh][:, 64:64 + HAL], lhsT=wv[:, k, cl * 128:(cl + 1) * 128], rhs=xnT[:, k, 0:HAL],
                                     start=(k == 0), stop=(k == 15), skip_group_check=True)
                    return m
                S.op("pe", mmgh, reads=[wgk, wvk] + allxh, writes=[("pb", bgh)])
                isg = nexttmp()
                sg = tmp[isg]
                S.op("act", lambda e: e.activation(out=sg[:, HAL:HAL + TT], in_=pb[bgm][:], func=AF.Sigmoid), reads=[("pb", bgm)], writes=[("tmp", isg)])
                S.op("act", lambda e: e.activation(out=sg[:, 0:HAL], in_=pb[bgh][:, 0:HAL], func=AF.Sigmoid), reads=[("pb", bgh), ("tmp", isg)], writes=[("tmp", isg)])
                bvm = nextbank()

                def mmv(e):
                    for k in range(16):
                        m = e.matmul(pb[bvm][:], lhsT=wv[:, k, cl * 128:(cl + 1) * 128], rhs=xnT[:, k, HAL:HAL + TT], start=(k == 0), stop=(k == 15))
                    return m
                S.op("pe", mmv, reads=[wvk] + allx, writes=[("pb", bvm)])
                S.op("dve", lambda e: e.tensor_tensor(out=oT[:, 8 + c, HAL:HAL + TT], in0=pb[bvm][:], in1=sg[:, HAL:HAL + TT], op=ALU.mult),
                     reads=[("pb", bvm), ("tmp", isg)], writes=[gk])
                S.op("dve", lambda e: e.tensor_tensor(out=oT[:, 8 + c, 0:HAL], in0=pb[bgh][:, 64:64 + HAL], in1=sg[:, 0:HAL], op=ALU.mult),
                     reads=[("pb", bgh), ("tmp", isg), gk], writes=[gk])

            def c_back(c):
                gk = ("oT", 8 + c)
                ds = c % 2
                bc = nextbank()

                def mmc(e):
                    for k in range(31):
                        m = e.matmul(pb[bc][:], lhsT=dgb[ds][:, k, :], rhs=oT[:, 8 + c, HAL - 30 + k:HAL - 30 + k + TT], start=(k == 0), stop=(k == 30))
                    return m
                S.op("pe", mmc, reads=[("dgb", ds), gk], writes=[("pb", bc)])
                if c > 0:
                    stat_mm(c - 1)
                bcol = cv[:, CV_DWB + c:CV_DWB + c + 1]
                gs = 0
                S.op("act", lambda e: e.activation(out=gc[:, c, :], in_=pb[bc][:], func=AF.Identity, bias=bcol), reads=[("pb", bc), "cv"], writes=[("gc", c)])
                S.op("act", lambda e: e.activation(out=gstat[gs][:], in_=pb[bc][:], func=AF.Identity, bias=bcol), reads=[("pb", bc), "cv"], writes=[("gstat", gs)])
                S.op("act", lambda e: e.activation(out=gstat[gs + 1][:], in_=pb[bc][:], func=AF.Square, bias=bcol), reads=[("pb", bc), "cv"], writes=[("gstat", gs + 1)])

            def stat_mm(c):
                gs = 0
                S.op("pe", lambda e: e.matmul(pstat[0][:], lhsT=ones[:], rhs=gstat[gs][:], start=(c == 0), stop=(c == NCH - 1)),
                     reads=["ones", ("gstat", gs)], writes=[("pstat", 0)])
                S.op("pe", lambda e: e.matmul(pstat[1][:], lhsT=ones[:], rhs=gstat[gs + 1][:], start=(c == 0), stop=(c == NCH - 1)),
                     reads=["ones", ("gstat", gs + 1)], writes=[("pstat", 1)])
            c_front(0)
            for c in range(NCH):
                if c + 1 < NCH:
                    c_front(c + 1)
                c_back(c)
            stat_mm(NCH - 1)
            if STOP <= 3:
                return store_R(ft)
            mean, msq, rs = lnt
            nmr = msq
            S.op("act", lambda e: e.activation(out=mean[:], in_=pstat[0][:], func=AF.Copy, scale=1.0 / 1024), reads=[("pstat", 0)], writes=[("lnt", 0)])
            S.op("act", lambda e: e.activation(out=msq[:], in_=pstat[0][:], func=AF.Square, scale=1.0 / 1024), reads=[("pstat", 0)], writes=[("lnt", 1)])
            S.op("dve", lambda e: e.scalar_tensor_tensor(out=rs[:], in0=pstat[1][:], scalar=1.0 / 1024, in1=msq[:], op0=ALU.mult, op1=ALU.subtract),
                 reads=[("pstat", 1), ("lnt", 1)], writes=[("lnt", 2)])
            S.op("act", lambda e: e.activation(out=rs[:], in_=rs[:], func=AF.Sqrt, bias=EPS, scale=1.0), reads=[("lnt", 2)], writes=[("lnt", 2)])
            S.op("dve", lambda e: e.reciprocal(out=rs[:], in_=rs[:]), reads=[("lnt", 2)], writes=[("lnt", 2)])
            S.op("dve", lambda e: e.scalar_tensor_tensor(out=nmr[:], in0=mean[:], scalar=-1.0, in1=rs[:], op0=ALU.mult, op1=ALU.mult),
                 reads=[("lnt", 0), ("lnt", 2), ("lnt", 1)], writes=[("lnt", 1)])
            for c in range(NCH):
                it = nexttmp()
                t_ = tmp[it]
                S.op("dve", lambda e, c=c, t_=t_: e.tensor_tensor(out=t_[:, 0:TT], in0=gc[:, c, :], in1=rs[:], op=ALU.mult),
                     reads=[("gc", c), ("lnt", 2)], writes=[("tmp", it)])
                S.op("dve", lambda e, t_=t_: e.tensor_tensor(out=t_[:, 0:TT], in0=t_[:, 0:TT], in1=nmr[:], op=ALU.add),
                     reads=[("tmp", it), ("lnt", 1)], writes=[("tmp", it)])
                S.op("act", lambda e, c=c, t_=t_: e.activation(out=oT[:, 8 + c, HAL:HAL + TT], in_=t_[:, 0:TT], func=AF.Silu,
                                                               scale=cv[:, CV_LNG + c:CV_LNG + c + 1], bias=cv[:, CV_LNB + c:CV_LNB + c + 1]),
                     reads=[("tmp", it), "cv"], writes=[("oT", 8 + c)])
            if STOP <= 4:
                return store_R(ft)
            for j in range(4):
                wo, wok = load_piece((wout_v[:, :, j * 512:(j + 1) * 512], None), 16, 512)
                for s in range(4):
                    b = nextbank()

                    def mmo(e, b=b, s=s, wo=wo):
                        for k in range(16):
                            m = e.matmul(pb[b][:], lhsT=oT[:, k, HAL + s * 128:HAL + (s + 1) * 128], rhs=wo[:, k, :], start=(k == 0), stop=(k == 15))
                        return m
                    S.op("pe", mmo, reads=[wok] + [("oT", k) for k in range(16)], writes=[("pb", b)])
                    S.op("dve", lambda e, b=b, s=s, j=j: e.tensor_tensor(out=R[s][:, j * 512:(j + 1) * 512], in0=pb[b][:],
                                                                         in1=R[s][:, j * 512:(j + 1) * 512], op=ALU.add),
                         reads=[("pb", b), ("R", s, j)], writes=[("R", s, j)])
            if STOP <= 5:
                return store_R(ft)
            norm_tile([(R[s], rkeys(s)) for s in range(4)], 1)
            def w1_phase(grp):
                a0 = (grp % 4) * 4
                w1p, w1k = load_piece((w1b_d[grp], ("w1b", grp)) if PRECAST(grp) else (w1_v[:, :, grp * 512:(grp + 1) * 512], None), 16, 512)
                for cl in range(4):
                    b = nextbank()

                    def mm1(e, b=b, w1p=w1p, cl=cl):
                        for k in range(16):
                            m = e.matmul(pb[b][:], lhsT=w1p[:, k, cl * 128:(cl + 1) * 128], rhs=xnT[:, k, HAL:HAL + TT], start=(k == 0), stop=(k == 15))
                        return m
                    S.op("pe", mm1, reads=[w1k] + allx, writes=[("pb", b)])
                    rs_ = cl % 2
                    S.op("act", lambda e, b=b, rs_=rs_: e.activation(out=rl[rs_][:], in_=pb[b][:], func=AF.Relu), reads=[("pb", b)], writes=[("rl", rs_)])
                    S.op("dve", lambda e, b=b, rs_=rs_, a0=a0, cl=cl: e.scalar_tensor_tensor(
                        out=oT[:, a0 + cl, HAL:HAL + TT], in0=pb[b][:], scalar=0.0, in1=rl[rs_][:], op0=ALU.max, op1=ALU.mult),
                        reads=[("pb", b), ("rl", rs_)], writes=[("oT", a0 + cl)])

            def w2_phase(grp):
                a0 = (grp % 4) * 4
                w2p, w2k = load_piece((w2b_d[grp], ("w2b", grp)) if PRECAST(grp) else (w2_v[:, grp * 4:(grp + 1) * 4, :], None), 4, D)
                for s in range(4):
                    for j in range(4):
                        b = nextbank()

                        def mm2(e, b=b, s=s, j=j, a0=a0, w2p=w2p):
                            for fl in range(4):
                                m = e.matmul(pb[b][:], lhsT=oT[:, a0 + fl, HAL + s * 128:HAL + (s + 1) * 128], rhs=w2p[:, fl, j * 512:(j + 1) * 512],
                                             start=(fl == 0), stop=(fl == 3))
                            return m
                        S.op("pe", mm2, reads=[w2k] + [("oT", a0 + fl) for fl in range(4)], writes=[("pb", b)])
                        S.op("dve", lambda e, b=b, s=s, j=j: e.tensor_tensor(out=R[s][:, j * 512:(j + 1) * 512], in0=pb[b][:],
                                                                             in1=R[s][:, j * 512:(j + 1) * 512], op=ALU.add),
                             reads=[("pb", b), ("R", s, j)], writes=[("R", s, j)])
            NG = 16
            w1_phase(0)
            for grp in range(NG):
                if grp + 1 < NG:
                    w1_phase(grp + 1)
                w2_phase(grp)
            if STOP <= 6:
                return store_R(ft)
            for s in range(4):
                n = nrm_ctr[0]
                nrm_ctr[0] += 1
                slot = n % 2
                col = n % 8
                S.op("act", lambda e, s=s, slot=slot, col=col: e.activation(out=xsb[slot][:], in_=R[s][:], func=AF.Square, accum_out=ssq[:, col:col + 1]),
                     reads=rkeys(s), writes=[("xsb", slot), ("ssq", col)])
                S.op("act", lambda e, col=col: e.activation(out=rstd[:, col:col + 1], in_=ssq[:, col:col + 1], func=AF.Sqrt, scale=1.0 / D, bias=EPS),
                     reads=[("ssq", col)], writes=[("rstd", col)])
                S.op("dve", lambda e, col=col: e.reciprocal(out=rstd[:, col:col + 1], in_=rstd[:, col:col + 1]), reads=[("rstd", col)], writes=[("rstd", col)])
                S.op("dve", lambda e, s=s, col=col: e.scalar_tensor_tensor(out=R[s][:], in0=R[s][:], scalar=rstd[:, col:col + 1], in1=gng[:, 2, :],
                                                                           op0=ALU.mult, op1=ALU.mult),
                     reads=rkeys(s) + [("rstd", col), "gng"], writes=rkeys(s))
                S.dma("sp", lambda e, s=s: e.dma_start(out=out_d[ft * TT + s * 128: ft * TT + (s + 1) * 128, :], in_=R[s][:]),
                      f"R{s}", reads=rkeys(s), writes=[("out", ft, s)])

        if NPRE > 0:
            prefix_all()
            emit_casts(len(cast_jobs))
            S.barrier()
        for ft in range(NFULL):
            full_tile(ft)
        outkeys = [("out", ft, s) for ft in range(NFULL) for s in range(4)]
        S.op("sp", lambda e: e.nop(), reads=outkeys)

        chans = sorted(set(o["chan"] for o in S.ops if o["kind"] == "dma"))
        sems = {e: es.enter_context(nc.semaphore(f"s_{e}")) for e in ["pe", "act", "dve", "pool", "sp"]}
        chan_sems = {c: es.enter_context(nc.semaphore(f"c_{c}")) for c in chans}
        block = es.enter_context(nc.Block())
        S.emit(block, sems, chan_sems)
        print("ops", len(S.ops), "sbuf bytes", off[0], "counts", {k: v for k, v in S.final_counts.items()})
    return nc


_CACHE = {}


def _host_inputs(inp, NPRE=12, NFULL=4, seq=8192, batch=2, chunks=4):
    f = np.float32
    x = np.asarray(inp["x"], f)
    cvs = []
    per = NFULL * TT

    def pc(v):
        return np.ascontiguousarray(np.asarray(v, f).reshape(NCH, 128).T)
    base = np.zeros((128, CV_N), f)
    base[:, CV_CB:CV_CB + 8] = pc(inp["lru_conv_b"][0])
    base[:, CV_GAB:CV_GAB + 8] = pc(inp["lru_gate_a_b"][0])
    base[:, CV_GXB:CV_GXB + 8] = pc(inp["lru_gate_x_b"][0])
    base[:, CV_LAM:CV_LAM + 8] = pc(inp["lru_lambda"][0])
    base[:, CV_DWB:CV_DWB + 8] = pc(inp["conf_dw_b"][0])
    base[:, CV_LNG:CV_LNG + 8] = pc(inp["conf_ln_g"][0])
    base[:, CV_LNB:CV_LNB + 8] = pc(inp["conf_ln_b"][0])
    c4 = np.asarray(inp["lru_conv_w"][0], f)
    base[:, CV_C4W:CV_C4W + 32] = c4.reshape(4, NCH, 128).transpose(2, 1, 0).reshape(128, 32)
    gng = np.stack([np.broadcast_to(np.asarray(inp["mix_norm_g"][0], f), (128, D)),
                    np.broadcast_to(np.asarray(inp["mlp_norm_g"][0], f), (128, D)),
                    np.broadcast_to(np.asarray(inp["final_norm_g"], f), (128, D))], axis=1)
    gng = np.ascontiguousarray(gng)
    gw = np.zeros((128, 16, 128), f)
    for gi, name in enumerate(["lru_gate_a_w", "lru_gate_x_w"]):
        w = np.asarray(inp[name][0], f)
        for c in range(NCH):
            for hh in range(2):
                gw[hh * 64:(hh + 1) * 64, gi * 8 + c, hh * 64:(hh + 1) * 64] = w[c * 2 + hh]
    dw = np.asarray(inp["conf_dw_w"][0], f)
    dg = np.zeros((NCH, 128, 31, 128), f)
    ar = np.arange(128)
    for c in range(NCH):
        dg[c, ar, :, ar] = dw[:, c * 128:(c + 1) * 128].T
    dg = dg.reshape(NCH, 128, 31 * 128)
    dg4 = np.zeros((128, 32, 128), f)
    for c in range(NCH):
        for k in range(4):
            dg4[ar, c * 4 + k, ar] = c4[k, c * 128:(c + 1) * 128]
    shared = dict(gng=gng, gw=gw, dg=dg, dg4=dg4,
                  w_in=np.ascontiguousarray(np.asarray(inp["w_in"][0], f)),
                  w_out=np.ascontiguousarray(np.asarray(inp["w_out"][0], f)),
                  w1=np.ascontiguousarray(np.asarray(inp["mlp_w1"][0], f)),
                  w2=np.ascontiguousarray(np.asarray(inp["mlp_w2"][0], f)))
    maps = []
    for core in range(batch * chunks):
        b, q = divmod(core, chunks)
        npad = (NPRE * TT) - q * per
        xs = np.zeros(((NPRE + NFULL) * TT, D), f)
        xs[npad:] = x[b, 0:(q + 1) * per]
        cvc = base.copy()
        for pt in range(NPRE):
            cvc[:, CV_FLAG + pt] = 1.0 if pt * TT >= npad else 0.0
        m = dict(shared)
        m["xs"] = xs
        m["cv"] = cvc
        maps.append(m)
    return maps


def kernel(**inputs):
    NPRE, NFULL = 12, 4
    key = (NPRE, NFULL)
    if key not in _CACHE:
        _CACHE[key] = build_program(NPRE, NFULL)
    nc = _CACHE[key]
    maps = _host_inputs(inputs, NPRE, NFULL)
    res = run_bass_kernel_spmd(nc, maps, core_ids=list(range(8)))
    out = np.zeros((2, 8192, D), np.float32)
    for core in range(8):
        b, q = divmod(core, 4)
        out[b, q * 2048:(q + 1) * 2048] = res.results[core]["out"]
    return out
```

```python
import os
import numpy as np
import concourse.bass as bass
import concourse.mybir as mybir
from concourse.bass_utils import run_bass_kernel_spmd

F32 = mybir.dt.float32
BF16 = mybir.dt.bfloat16
AF = mybir.ActivationFunctionType
ALU = mybir.AluOpType

D = 2048
DIN = 4096
DFF = 8192
TT = 512
HAL = 32
NCH = 8
EPS = 1e-6
CV_CB, CV_GAB, CV_GXB, CV_LAM, CV_DWB, CV_LNG, CV_LNB = 0, 8, 16, 24, 32, 40, 48
CV_C4W = 56
CV_FLAG = 88
CV_N = 104


class Sched:
    def __init__(self, nc):
        self.nc = nc
        self.ops = []
        self.last_w = {}
        self.readers = {}
        self.eng_objs = {"pe": nc.tensor, "act": nc.scalar, "dve": nc.vector, "pool": nc.gpsimd, "sp": nc.sync}

    def _deps(self, eng, reads, writes):
        deps = set()
        for k in reads:
            w = self.last_w.get(k)
            if w is not None:
                deps.add(w)
        for k in writes:
            w = self.last_w.get(k)
            if w is not None:
                deps.add(w)
            for r in self.readers.get(k, ()):
                deps.add(r)
        return deps

    def _add(self, eng, fn, reads, writes, kind, chan=None):
        oid = len(self.ops)
        deps = self._deps(eng, reads, writes)
        keep = set()
        raw = set(self.last_w.get(k) for k in reads if self.last_w.get(k) is not None)
        for d in deps:
            od = self.ops[d]
            if od["kind"] == "dma" or kind == "dma":
                keep.add(d)
            elif od["eng"] != eng:
                keep.add(d)
            elif d in raw and eng != "pe":
                keep.add(d)
        self.ops.append(dict(eng=eng, fn=fn, deps=keep, kind=kind, chan=chan, sig=False))
        for k in reads:
            self.readers.setdefault(k, []).append(oid)
        for k in writes:
            self.last_w[k] = oid
            self.readers[k] = []
        return oid

    def op(self, eng, fn, reads=(), writes=()):
        return self._add(eng, fn, list(reads), list(writes), "cmp")

    def dma(self, queue, fn, chan, reads=(), writes=()):
        return self._add(queue, fn, list(reads), list(writes), "dma", chan)

    def barrier(self):
        last = {}
        for i, o in enumerate(self.ops):
            if o["kind"] == "dma":
                last[("c", o["chan"])] = i
            else:
                last[("e", o["eng"])] = i
        deps = set(last.values())
        for eng in ["pe", "act", "dve", "pool", "sp"]:
            self.ops.append(dict(eng=eng, fn=lambda e: e.nop(), deps=set(deps), kind="cmp", chan=None, sig=False))

    def emit(self, block, sems, chan_sems):
        ops = self.ops
        for o in ops:
            for d in o["deps"]:
                ops[d]["sig"] = True
        cnt = {}
        for i, o in enumerate(ops):
            if o["kind"] == "dma":
                c = o["chan"]
                cnt[c] = cnt.get(c, 0) + 16
                o["ev"] = (("c", c), cnt[c])
            elif o["sig"]:
                e = o["eng"]
                cnt[e] = cnt.get(e, 0) + 1
                o["ev"] = (("e", e), cnt[e])
            else:
                o["ev"] = None
        const_total = {c: v for c, v in cnt.items() if isinstance(c, str) and c.startswith("const")}
        self.final_counts = cnt
        by_eng = {}
        for i, o in enumerate(ops):
            by_eng.setdefault(o["eng"], []).append(i)

        def semof(key):
            return chan_sems[key[1]] if key[0] == "c" else sems[key[1]]

        snap = {}
        nwaits = [0]

        def run_engine(ename, engobj):
            known = {}
            for i in by_eng.get(ename, []):
                o = ops[i]
                need = {}
                for d in o["deps"]:
                    key, val = ops[d]["ev"]
                    if key[0] == "c" and key[1] in const_total:
                        val = const_total[key[1]]
                    if need.get(key, 0) < val:
                        need[key] = val
                for key, val in sorted(need.items(), key=lambda kv: -kv[1]):
                    if known.get(key, 0) >= val:
                        continue
                    engobj.wait_ge(semof(key), val)
                    nwaits[0] += 1
                    known[key] = val
                    sn = snap.get((key, val))
                    if sn:
                        for k2, v2 in sn.items():
                            if known.get(k2, 0) < v2:
                                known[k2] = v2
                inst = o["fn"](engobj)
                if o["ev"] is not None:
                    key, val = o["ev"]
                    if o["kind"] == "dma":
                        inst.then_inc(semof(key), 16)
                    else:
                        inst.then_inc(semof(key), 1)
                        snap[(key, val)] = dict(known)
                        known[key] = max(known.get(key, 0), 0)

        @block.sync
        def _(e):
            run_engine("sp", e)

        @block.gpsimd
        def _(e):
            run_engine("pool", e)

        @block.scalar
        def _(e):
            run_engine("act", e)

        @block.vector
        def _(e):
            run_engine("dve", e)

        @block.tensor
        def _(e):
            run_engine("pe", e)


def build_program(NPRE, NFULL):
    nc = bass.Bass("TRN2", target_bir_lowering=False)
    NTILE = NPRE + NFULL
    xs_d = nc.dram_tensor("xs", [NTILE * TT, D], F32, kind="ExternalInput").ap()
    cv_d = nc.dram_tensor("cv", [128, CV_N], F32, kind="ExternalInput").ap()
    gng_d = nc.dram_tensor("gng", [128, 3, D], F32, kind="ExternalInput").ap()
    gw_d = nc.dram_tensor("gw", [128, 16, 128], F32, kind="ExternalInput").ap()
    dg_d = nc.dram_tensor("dg", [NCH, 128, 31 * 128], F32, kind="ExternalInput").ap()
    dg4_d = nc.dram_tensor("dg4", [128, 32, 128], F32, kind="ExternalInput").ap()
    win_d = nc.dram_tensor("w_in", [D, DIN], F32, kind="ExternalInput").ap()
    wout_d = nc.dram_tensor("w_out", [D, D], F32, kind="ExternalInput").ap()
    w1_d = nc.dram_tensor("w1", [D, DFF], F32, kind="ExternalInput").ap()
    w2_d = nc.dram_tensor("w2", [DFF, D], F32, kind="ExternalInput").ap()
    out_d = nc.dram_tensor("out", [NFULL * TT, D], F32, kind="ExternalOutput").ap()
    w1b_d = nc.dram_tensor("w1b", [16, 128, 8192], BF16, kind="Internal").ap()
    w2b_d = nc.dram_tensor("w2b", [16, 128, 8192], BF16, kind="Internal").ap()

    S = Sched(nc)
    import contextlib
    es = contextlib.ExitStack()
    off = [16384]

    def sb(name, shape, dt, at=None):
        nbytes = int(np.prod(shape[1:])) * (4 if dt == F32 else 2)
        nbytes = (nbytes + 63) // 64 * 64
        if at is None:
            at = off[0]
            off[0] += nbytes
        return nc.alloc_sbuf_tensor_at(name, shape, dt, offset=at), at + nbytes

    def sbp(name, shape, dt):
        return sb(name, shape, dt)[0]

    def ps(name, shape, dt):
        return es.enter_context(nc.psum_tensor(name, shape, dt))

    with es:
        cv = sbp("cv", [128, CV_N], F32)
        cc = sbp("cc", [128, 16], F32)
        gng = sbp("gng", [128, 3, D], F32)
        gw = sbp("gw", [128, 16, 128], BF16)
        ident = sbp("ident", [128, 128], BF16)
        identf = sbp("identf", [128, 128], F32)
        ones = sbp("ones", [128, 128], BF16)
        hst = sbp("hst", [128, NCH], F32)
        hal = sbp("hal", [128, NCH, 4], F32)
        xnT = sbp("xnT", [128, 16, HAL + TT], BF16)
        xsb = [sbp(f"xsb{i}", [128, D], BF16) for i in range(2)]
        mhalf = sbp("mhalf", [128, 8], F32)
        ssq = sbp("ssq", [128, 8], F32)
        rstd = sbp("rstd", [128, 8], F32)
        NT = 5
        tmp = [sbp(f"tmp{i}", [128, HAL + TT], F32) for i in range(NT)]
        xcbs = [sbp(f"xcb{i}", [128, TT], BF16) for i in range(2)]
        xh = sbp("xh", [128, 16, HAL], BF16)
        base = off[0]
        wlru = sbp("wlru", [128, 16, 1024], BF16)
        xst = [sbp(f"xst{i}", [128, D], F32) for i in range(2)]
        NPT = 31
        ptmp = [sbp(f"ptmp{i}", [128, HAL + TT], F32) for i in range(NPT)]
        dg4 = sbp("dg4", [128, 32, 128], BF16)
        xlb = [sbp(f"xlb{i}", [128, HAL + TT], BF16) for i in range(NCH)]
        halb = sbp("halb", [128, NCH, 4], BF16)
        pxcb = [sbp(f"pxcb{i}", [128, TT], BF16) for i in range(NCH)]
        assert off[0] < 229000, off[0]
        off[0] = base
        WS = 3
        wring = [sbp(f"wr{i}", [128, 16 * 512], BF16) for i in range(WS)]
        R = [sbp(f"R{i}", [128, D], F32) for i in range(4)]
        oT = sbp("oT", [128, 16, HAL + TT], BF16)
        gc = sbp("gc", [128, NCH, TT], F32)
        gstat = [sbp(f"gstat{i}", [128, TT], BF16) for i in range(2)]
        dgb = [sbp(f"dgb{i}", [128, 31, 128], BF16) for i in range(2)]
        lnt = [sbp(f"lnt{i}", [128, TT], F32) for i in range(3)]
        rl = [sbp(f"rl{i}", [128, TT], BF16) for i in range(2)]
        assert off[0] < 229000, off[0]
        NB = 4
        pb = [ps(f"pb{i}", [128, 512], F32) for i in range(NB)]
        pstat = [ps(f"pst{i}", [128, 512], F32) for i in range(2)]
        ptr = [ps(f"ptr{i}", [128, 8, 128], BF16) for i in range(2)]
        bank_ctr = [0]

        def nextbank():
            b = bank_ctr[0] % NB
            bank_ctr[0] += 1
            return b

        S.dma("sp", lambda e: e.dma_start(out=cv[:], in_=cv_d), "const_sp", writes=["cv"])
        S.dma("sp", lambda e: e.dma_start(out=gng[:], in_=gng_d), "const_sp", writes=["gng"])
        S.dma("pool", lambda e: e.dma_start(out=gw[:], in_=gw_d), "const_pool", writes=["gw"])
        win_v = win_d.rearrange("(k p) n -> p k n", p=128)
        wout_v = wout_d.rearrange("(k p) n -> p k n", p=128)
        w1_v = w1_d.rearrange("(k p) n -> p k n", p=128)
        w2_v = w2_d.rearrange("(k p) n -> p k n", p=128)
        if NPRE > 0:
            for h in range(2):
                S.dma("pool", lambda e, h=h: e.dma_start(out=wlru[:, :, h * 512:(h + 1) * 512], in_=win_v[:, :, h * 512:(h + 1) * 512]),
                      "const_pool", writes=[("wlru", h)])
        if NPRE > 0:
            S.dma("pool", lambda e: e.dma_start(out=dg4[:], in_=dg4_d), "const_pool", writes=["dg4"])
            S.op("dve", lambda e: e.memset(halb[:], 0.0), writes=[("halb", c) for c in range(NCH)])
        S.op("pool", lambda e: e.memset(identf[:], 0.0), writes=["identf"])
        S.op("pool", lambda e: e.affine_select(out=identf[:], in_=identf[:], pattern=[[-1, 128]], compare_op=ALU.not_equal,
                                               fill=1.0, base=0, channel_multiplier=1), reads=["identf"], writes=["identf"])
        S.op("dve", lambda e: e.tensor_copy(out=ident[:], in_=identf[:]), reads=["identf"], writes=["ident"])
        S.op("dve", lambda e: e.memset(ones[:], 1.0), writes=["ones"])
        S.op("dve", lambda e: e.memset(mhalf[:], -0.5), writes=["mhalf"])
        S.op("dve", lambda e: e.memset(hst[:], 0.0), writes=[("hst", c) for c in range(NCH)])
        S.op("dve", lambda e: e.memset(hal[:], 0.0), writes=[("hal", c) for c in range(NCH)])
        S.op("act", lambda e: e.activation(out=cc[:, 0:8], in_=cv[:, CV_LAM:CV_LAM + 8], func=AF.Exp, scale=-1.0),
             reads=["cv"], writes=["cc"])
        S.op("act", lambda e: e.activation(out=cc[:, 0:8], in_=cc[:, 0:8], func=AF.Ln, bias=1.0, scale=1.0),
             reads=["cc"], writes=["cc"])
        S.op("dve", lambda e: e.tensor_scalar(out=cc[:, 8:16], in0=cc[:, 0:8], scalar1=-16.0, scalar2=None, op0=ALU.mult),
             reads=["cc"], writes=["cc2"])
        S.op("dve", lambda e: e.tensor_scalar(out=cc[:, 0:8], in0=cc[:, 0:8], scalar1=-8.0, scalar2=None, op0=ALU.mult),
             reads=["cc", "cc2"], writes=["cc"])

        cast_jobs = []
        PRECAST = lambda g: (g % 3 == 0)
        for g in range(16):
            if PRECAST(g):
                cast_jobs.append((w1b_d[g].rearrange("p (k n) -> p k n", k=16), w1_v[:, :, g * 512:(g + 1) * 512], ("w1b", g)))
                cast_jobs.append((w2b_d[g].rearrange("p (k n) -> p k n", k=4), w2_v[:, g * 4:(g + 1) * 4, :], ("w2b", g)))
        cast_ctr = [0]
        NCASTCH = 6

        def emit_casts(n):
            for _ in range(n):
                i = cast_ctr[0]
                if i >= len(cast_jobs):
                    return
                cast_ctr[0] += 1
                o_ap, i_ap, key = cast_jobs[i]
                S.dma("pool", lambda e, o_ap=o_ap, i_ap=i_ap: e.dma_start(out=o_ap, in_=i_ap), f"wcast{i % NCASTCH}",
                      writes=[key, ("wcastslot", i % NCASTCH)])

        nrm_ctr = [0]

        def norm_front(src, srck, np_, gidx):
            n = nrm_ctr[0]
            nrm_ctr[0] += 1
            slot = n % 2
            col = n % 8
            S.op("act", lambda e: e.activation(out=xsb[slot][0:np_, :], in_=src[0:np_, :], func=AF.Square, accum_out=ssq[0:np_, col:col + 1]),
                 reads=srck, writes=[("xsb", slot), ("ssq", col)])
            S.op("act", lambda e: e.activation(out=rstd[0:np_, col:col + 1], in_=ssq[0:np_, col:col + 1], func=AF.Sqrt, scale=1.0 / D, bias=EPS),
                 reads=[("ssq", col)], writes=[("rstd", col)])
            S.op("dve", lambda e: e.reciprocal(out=rstd[0:np_, col:col + 1], in_=rstd[0:np_, col:col + 1]),
                 reads=[("rstd", col)], writes=[("rstd", col)])
            S.op("dve", lambda e: e.scalar_tensor_tensor(out=xsb[slot][0:np_, :], in0=src[0:np_, :], scalar=rstd[0:np_, col:col + 1],
                                                         in1=gng[0:np_, gidx, :], op0=ALU.mult, op1=ALU.mult),
                 reads=list(srck) + [("rstd", col), "gng"], writes=[("xsb", slot)])
            return slot

        def norm_back(slot, np_, c0):
            for half in range(2):
                def tr(e, half=half):
                    for j in range(8):
                        k = half * 8 + j
                        mm = e.transpose(out=ptr[half][:, j, 0:np_], in_=xsb[slot][0:np_, k * 128:(k + 1) * 128], identity=ident[0:np_, 0:np_])
                    return mm
                S.op("pe", tr, reads=[("xsb", slot), "ident"], writes=[("ptr", half)])
                wk = [("xnT", half * 8 + j, c0) for j in range(8)]
                if half == 0:
                    S.op("act", lambda e: e.activation(out=xnT[:, 0:8, c0:c0 + np_], in_=ptr[0][:, :, 0:np_], func=AF.Copy),
                         reads=[("ptr", 0)], writes=wk)
                else:
                    S.op("dve", lambda e: e.tensor_copy(out=xnT[:, 8:16, c0:c0 + np_], in_=ptr[1][:, :, 0:np_]),
                         reads=[("ptr", 1)], writes=wk)

        def norm_tile(srcs, gidx, pre=None):
            slots = {}
            for s in range(4):
                if pre is not None:
                    pre(s)
                slots[s] = norm_front(srcs[s][0], srcs[s][1], 128, gidx)
                if s >= 1:
                    norm_back(slots[s - 1], 128, HAL + (s - 1) * 128)
            norm_back(slots[3], 128, HAL + 3 * 128)

        allx = [("xnT", k, HAL + s * 128) for k in range(16) for s in range(4)]
        allxh = allx + [("xnT", k, 0) for k in range(16)]

        wr_ctr = [0]

        def load_piece(src, nk, ncol):
            src_ap, skey = src
            slot = wr_ctr[0] % WS
            wr_ctr[0] += 1
            view = wring[slot][:, 0:nk * ncol].rearrange("p (k n) -> p k n", k=nk)
            if skey is None:
                S.dma("pool", lambda e: e.dma_start(out=view, in_=src_ap), f"wr{slot}", writes=[("wr", slot)])
            else:
                S.dma("pool", lambda e: e.dma_start(out=wring[slot][:, 0:nk * ncol], in_=src_ap), f"wr{slot}", reads=[skey], writes=[("wr", slot)])
            return view, ("wr", slot)

        tmp_ctr = [0]

        def nexttmp():
            i = tmp_ctr[0] % NT
            tmp_ctr[0] += 1
            return i

        class Pool_:
            def __init__(self, items):
                self.free = list(items)

            def get(self, wide=False):
                for i, it in enumerate(self.free):
                    if (it[2] >= HAL + TT) == wide:
                        return self.free.pop(i)
                for i, it in enumerate(self.free):
                    if it[2] >= HAL + TT:
                        return self.free.pop(i)
                raise RuntimeError("scratch pool exhausted")

            def put(self, it):
                self.free.append(it)

        class Lru:
            def __init__(self, wsel, pool, xcb_of, aux_eng, conv_pe=False, save_hal=False):
                self.wsel, self.pool, self.xcb_of, self.aux = wsel, pool, xcb_of, aux_eng
                self.conv_pe, self.save_hal = conv_pe, save_hal
                self.st = {}

            def _head(self, c):
                wview, wkey, wcol0 = self.wsel(c)
                b = nextbank()

                def mm(e, b=b, wview=wview, wcol0=wcol0):
                    for k in range(16):
                        m = e.matmul(pb[b][:], lhsT=wview[:, k, wcol0:wcol0 + 128], rhs=xnT[:, k, HAL:HAL + TT], start=(k == 0), stop=(k == 15))
                    return m
                S.op("pe", mm, reads=[wkey] + allx, writes=[("pb", b)])
                xl = xlb[c]
                xlk = ("xlb", c)
                S.op("act", lambda e, b=b, xl=xl: e.activation(out=xl[:, HAL:HAL + TT], in_=pb[b][:], func=AF.Copy), reads=[("pb", b)], writes=[xlk])
                S.op("dve", lambda e, xl=xl, c=c: e.tensor_copy(out=xl[:, HAL - 3:HAL], in_=halb[:, c, 0:3]), reads=[("halb", c), xlk], writes=[xlk])
                if self.save_hal:
                    S.op("dve", lambda e, b=b, c=c: e.tensor_copy(out=hal[:, c, 0:3], in_=pb[b][:, TT - 3:TT]), reads=[("pb", b)], writes=[("hal", c)])

            def _tail(self, c):
                pool = self.pool
                xl = xlb[c]
                xlk = ("xlb", c)
                b2 = nextbank()

                def mc(e, b2=b2, xl=xl, c=c):
                    for k in range(4):
                        m = e.matmul(pb[b2][:], lhsT=dg4[:, c * 4 + k, :], rhs=xl[:, HAL - 3 + k:HAL - 3 + k + TT], start=(k == 0), stop=(k == 3))
                    return m
                S.op("pe", mc, reads=["dg4", xlk], writes=[("pb", b2)])
                S.op("dve", lambda e, xl=xl, c=c: e.tensor_copy(out=halb[:, c, 0:3], in_=xl[:, HAL + TT - 3:HAL + TT]), reads=[xlk], writes=[("halb", c)])
                RB = pool.get(wide=True)
                XC = pool.get()
                xc, xck = XC[0], XC[1]
                S.op("dve", lambda e, b2=b2, xc=xc, c=c: e.tensor_scalar(out=xc[:, 0:TT], in0=pb[b2][:], scalar1=cv[:, CV_CB + c:CV_CB + c + 1],
                                                                       scalar2=None, op0=ALU.add),
                     reads=[("pb", b2), "cv"], writes=[xck])
                xb, xbk = self.xcb_of(c)
                S.op("dve", lambda e, xc=xc, xb=xb: e.tensor_copy(out=xb[:], in_=xc[:, 0:TT]), reads=[xck], writes=[xbk])
                self.st[c] = dict(XL=RB, XC=XC, xb=xb, xbk=xbk)

            def s1_pe(self, chunks):
                for _ in self.g_s1(chunks):
                    pass

            def s1(self, chunks):
                if self.conv_pe:
                    return self.s1_pe(chunks)
                pool = self.pool
                for c in chunks:
                    wview, wkey, wcol0 = self.wsel(c)
                    b = nextbank()

                    def mm(e, b=b, wview=wview, wcol0=wcol0):
                        for k in range(16):
                            m = e.matmul(pb[b][:], lhsT=wview[:, k, wcol0:wcol0 + 128], rhs=xnT[:, k, HAL:HAL + TT], start=(k == 0), stop=(k == 15))
                        return m
                    S.op("pe", mm, reads=[wkey] + allx, writes=[("pb", b)])
                    XL = pool.get(wide=True)
                    XC = pool.get()
                    xl, xlk = XL[0], XL[1]
                    xc, xck = XC[0], XC[1]
                    S.op("act", lambda e, b=b, xl=xl: e.activation(out=xl[:, HAL:HAL + TT], in_=pb[b][:], func=AF.Copy), reads=[("pb", b)], writes=[xlk])
                    S.op("dve", lambda e, xl=xl, c=c: e.tensor_copy(out=xl[:, HAL - 3:HAL], in_=hal[:, c, 0:3]), reads=[("hal", c), xlk], writes=[xlk])
                    w0 = CV_C4W + c * 4
                    S.op("dve", lambda e, xl=xl, xc=xc, w0=w0, c=c: e.tensor_scalar(out=xc[:, 0:TT], in0=xl[:, HAL - 3:HAL - 3 + TT], scalar1=cv[:, w0:w0 + 1],
                                                                                  scalar2=cv[:, CV_CB + c:CV_CB + c + 1], op0=ALU.mult, op1=ALU.add),
                         reads=[xlk, "cv"], writes=[xck])
                    for k in range(1, 4):
                        S.op("dve", lambda e, xl=xl, xc=xc, w0=w0, k=k: e.scalar_tensor_tensor(out=xc[:, 0:TT], in0=xl[:, HAL - 3 + k:HAL - 3 + k + TT],
                                                                                             scalar=cv[:, w0 + k:w0 + k + 1], in1=xc[:, 0:TT], op0=ALU.mult, op1=ALU.add),
                             reads=[xlk, xck, "cv"], writes=[xck])
                    S.op("dve", lambda e, xl=xl, c=c: e.tensor_copy(out=hal[:, c, 0:3], in_=xl[:, HAL + TT - 3:HAL + TT]), reads=[xlk], writes=[("hal", c)])
                    xb, xbk = self.xcb_of(c)
                    S.op("dve", lambda e, xc=xc, xb=xb: e.tensor_copy(out=xb[:], in_=xc[:, 0:TT]), reads=[xck], writes=[xbk])
                    self.st[c] = dict(XL=XL, XC=XC, xb=xb, xbk=xbk)

            def s2(self, chunks):
                for c in chunks:
                    d = self.st[c]
                    xb, xbk = d["xb"], d["xbk"]
                    ba = nextbank()
                    S.op("pe", lambda e, ba=ba, c=c, xb=xb: e.matmul(pb[ba][:], lhsT=gw[:, c, :], rhs=xb[:], start=True, stop=True), reads=["gw", xbk], writes=[("pb", ba)])
                    bx = nextbank()
                    S.op("pe", lambda e, bx=bx, c=c, xb=xb: e.matmul(pb[bx][:], lhsT=gw[:, 8 + c, :], rhs=xb[:], start=True, stop=True), reads=["gw", xbk], writes=[("pb", bx)])
                    II = self.pool.get()
                    d["II"] = II
                    r_, rk = d["XL"][0], d["XL"][1]
                    i_, ik = II[0], II[1]
                    xc, xck = d["XC"][0], d["XC"][1]
                    S.op("act", lambda e, ba=ba, r_=r_, c=c: e.activation(out=r_[:, 0:TT], in_=pb[ba][:], func=AF.Sigmoid, bias=cv[:, CV_GAB + c:CV_GAB + c + 1]),
                         reads=[("pb", ba), "cv"], writes=[rk])
                    S.op("act", lambda e, bx=bx, i_=i_, c=c: e.activation(out=i_[:, 0:TT], in_=pb[bx][:], func=AF.Sigmoid, bias=cv[:, CV_GXB + c:CV_GXB + c + 1]),
                         reads=[("pb", bx), "cv"], writes=[ik])
                    S.op(self.aux, lambda e, i_=i_, xc=xc: e.tensor_tensor(out=i_[:, 0:TT], in0=i_[:, 0:TT], in1=xc[:, 0:TT], op=ALU.mult),
                         reads=[ik, xck], writes=[ik])

            def s3(self, chunks):
                for c in chunks:
                    d = self.st[c]
                    r_, rk = d["XL"][0], d["XL"][1]
                    a_, ak = d["XC"][0], d["XC"][1]
                    S.op("act", lambda e, r_=r_, a_=a_, c=c: e.activation(out=a_[:, 0:TT], in_=r_[:, 0:TT], func=AF.Exp, scale=cc[:, c:c + 1]),
                         reads=[rk, "cc"], writes=[ak])
                    S.op("act", lambda e, r_=r_, c=c: e.activation(out=r_[:, 0:TT], in_=r_[:, 0:TT], func=AF.Exp, scale=cc[:, 8 + c:9 + c]),
                         reads=[rk, "cc2"], writes=[rk])
                for c in chunks:
                    d = self.st[c]
                    r_, rk = d["XL"][0], d["XL"][1]
                    i_, ik = d["II"][0], d["II"][1]
                    a_, ak = d["XC"][0], d["XC"][1]
                    S.op("act", lambda e, r_=r_: e.activation(out=r_[:, 0:TT], in_=r_[:, 0:TT], func=AF.Sqrt, scale=-1.0, bias=1.0), reads=[rk], writes=[rk])
                    S.op(self.aux, lambda e, i_=i_, r_=r_: e.tensor_tensor(out=i_[:, 0:TT], in0=i_[:, 0:TT], in1=r_[:, 0:TT], op=ALU.mult),
                         reads=[ik, rk], writes=[ik])
                    S.op("dve", lambda e, a_=a_, i_=i_, r_=r_, c=c: e.tensor_tensor_scan(out=r_[:, 0:TT], data0=a_[:, 0:TT], data1=i_[:, 0:TT],
                                                                                        initial=hst[:, c:c + 1], op0=ALU.mult, op1=ALU.add),
                         reads=[ak, ik, ("hst", c)], writes=[rk])
                    self.pool.put(d["XC"])
                    self.pool.put(d["II"])

            def g_s1(self, chunks):
                for i, c in enumerate(chunks):
                    self._head(c)
                    if i > 0:
                        self._tail(chunks[i - 1])
                    yield
                self._tail(chunks[-1])
                yield

            def g_s2(self, chunks):
                for c in chunks:
                    self.s2([c])
                    yield

            def g_s3a(self, chunks):
                for c in chunks:
                    d = self.st[c]
                    r_, rk = d["XL"][0], d["XL"][1]
                    a_, ak = d["XC"][0], d["XC"][1]
                    S.op("act", lambda e, r_=r_, a_=a_, c=c: e.activation(out=a_[:, 0:TT], in_=r_[:, 0:TT], func=AF.Exp, scale=cc[:, c:c + 1]),
                         reads=[rk, "cc"], writes=[ak])
                    S.op("act", lambda e, r_=r_, c=c: e.activation(out=r_[:, 0:TT], in_=r_[:, 0:TT], func=AF.Exp, scale=cc[:, 8 + c:9 + c]),
                         reads=[rk, "cc2"], writes=[rk])
                    yield

            def g_s3b(self, chunks, after=None):
                for c in chunks:
                    d = self.st[c]
                    r_, rk = d["XL"][0], d["XL"][1]
                    i_, ik = d["II"][0], d["II"][1]
                    a_, ak = d["XC"][0], d["XC"][1]
                    S.op("act", lambda e, r_=r_: e.activation(out=r_[:, 0:TT], in_=r_[:, 0:TT], func=AF.Sqrt, scale=-1.0, bias=1.0), reads=[rk], writes=[rk])
                    S.op(self.aux, lambda e, i_=i_, r_=r_: e.tensor_tensor(out=i_[:, 0:TT], in0=i_[:, 0:TT], in1=r_[:, 0:TT], op=ALU.mult),
                         reads=[ik, rk], writes=[ik])
                    S.op("dve", lambda e, a_=a_, i_=i_, r_=r_, c=c: e.tensor_tensor_scan(out=r_[:, 0:TT], data0=a_[:, 0:TT], data1=i_[:, 0:TT],
                                                                                        initial=hst[:, c:c + 1], op0=ALU.mult, op1=ALU.add),
                         reads=[ak, ik, ("hst", c)], writes=[rk])
                    self.pool.put(d["XC"])
                    self.pool.put(d["II"])
                    if after is not None:
                        after(c)
                    yield

            def h(self, c):
                return self.st[c]["XL"]

        def save_halo():
            S.op("dve", lambda e: e.tensor_copy(out=xh[:], in_=xnT[:, :, HAL + TT - HAL:HAL + TT]),
                 reads=[("xnT", k, HAL + 384) for k in range(16)], writes=["xh"])

        xst_ctr = [0]
        ppool = Pool_([(ptmp[i], ("ptmp", i), HAL + TT) for i in range(NPT)])

        def prefix_norm(pt):
            srcs = []
            for s in range(4):
                slot = (xst_ctr[0] + s) % 2
                srcs.append((xst[slot], [("xst", slot)]))

            def pre(s):
                slot = xst_ctr[0] % 2
                xst_ctr[0] += 1
                r0 = pt * TT + s * 128
                S.dma("sp", lambda e, slot=slot, r0=r0: e.dma_start(out=xst[slot][:], in_=xs_d[r0:r0 + 128, :]), f"xst{slot}", writes=[("xst", slot)])
            norm_tile(srcs, 0, pre)
            if pt == NPRE - 1:
                save_halo()

        def run(*gens):
            gens = [g for g in gens if g is not None]
            while gens:
                for g in list(gens):
                    try:
                        next(g)
                    except StopIteration:
                        gens.remove(g)

        def chain(*gens):
            for g in gens:
                yield from g

        def g_norm(pt):
            srcs = []
            for s in range(4):
                slot = (xst_ctr[0] + s) % 2
                srcs.append((xst[slot], [("xst", slot)]))
            slots = {}
            for s in range(4):
                slot = xst_ctr[0] % 2
                xst_ctr[0] += 1
                r0 = pt * TT + s * 128
                S.dma("sp", lambda e, slot=slot, r0=r0: e.dma_start(out=xst[slot][:], in_=xs_d[r0:r0 + 128, :]), f"xst{slot}", writes=[("xst", slot)])
                slots[s] = norm_front(srcs[s][0], srcs[s][1], 128, 0)
                if s >= 1:
                    norm_back(slots[s - 1], 128, HAL + (s - 1) * 128)
                yield
            norm_back(slots[3], 128, HAL + 3 * 128)
            if pt == NPRE - 1:
                save_halo()
            yield

        def prefix_all():
            A, B = [0, 1, 2, 3], [4, 5, 6, 7]
            run(g_norm(0))
            prevL = None
            prev_pt = None

            def hst_upd(L, pt):
                def f(c):
                    H = L.h(c)
                    S.op("dve", lambda e, h=H[0], c=c: e.tensor_scalar(out=hst[:, c:c + 1], in0=h[:, TT - 1:TT], scalar1=cv[:, CV_FLAG + pt:CV_FLAG + pt + 1],
                                                                       scalar2=None, op0=ALU.mult),
                         reads=[H[1], "cv"], writes=[("hst", c)])
                    ppool.put(H)
                return f
            for pt in range(NPRE):
                L = Lru(lambda c: (wlru, ("wlru", c // 4), c * 128), ppool, lambda c: (pxcb[c], ("pxcb", c)), "pool",
                        conv_pe=True, save_hal=(pt == NPRE - 1))
                emit_casts(int(os.environ.get("CASTS_PER_TILE", (len(cast_jobs) + NPRE - 1) // NPRE)))
                tail_prev = chain(prevL.g_s3a(B), prevL.g_s3b(B, hst_upd(prevL, prev_pt))) if prevL is not None else None
                run(L.g_s1(A + B), tail_prev)
                run(g_norm(pt + 1) if pt + 1 < NPRE else None, L.g_s2(A))
                run(L.g_s3a(A))
                run(L.g_s3b(A, hst_upd(L, pt)))
                run(L.g_s2(B))
                prevL, prev_pt = L, pt
            run(chain(prevL.g_s3a(B), prevL.g_s3b(B, hst_upd(prevL, prev_pt))))

        dg_ctr = [0]

        def rkeys(s):
            return [("R", s, j) for j in range(4)]

        import os
        STOP = int(os.environ.get("STOP_PHASE", "99"))

        def store_R(ft):
            for s in range(4):
                S.dma("sp", lambda e, s=s: e.dma_start(out=out_d[ft * TT + s * 128: ft * TT + (s + 1) * 128, :], in_=R[s][:]),
                      f"R{s}", reads=rkeys(s), writes=[("out", ft, s)])

        def full_tile(ft):
            t0 = (NPRE + ft) * TT
            S.op("dve", lambda e: e.tensor_copy(out=xnT[:, :, 0:HAL], in_=xh[:]), reads=["xh"], writes=[("xnT", k, 0) for k in range(16)])
            for s in range(4):
                S.dma("sp", lambda e, s=s: e.dma_start(out=R[s][:], in_=xs_d[t0 + s * 128:t0 + (s + 1) * 128, :]), f"R{s}", writes=rkeys(s))
            norm_tile([(R[s], rkeys(s)) for s in range(4)], 0)
            save_halo()
            if STOP <= 1:
                return store_R(ft)
            fpool = Pool_([(tmp[i], ("tmp", i), HAL + TT) for i in range(NT)] + [(gc[:, c, :], ("gc", c), TT) for c in range(NCH)]
                          + [(lnt[i], ("lnt", i), TT) for i in range(3)])
            pieces = {}

            def get_pieces(g):
                if g not in pieces:
                    pieces[g] = (load_piece((win_v[:, :, g * 512:(g + 1) * 512], None), 16, 512),
                                 load_piece((win_v[:, :, 1024 + g * 512:1024 + (g + 1) * 512], None), 16, 512))
                return pieces[g]
            L = Lru(lambda c: (get_pieces(c // 4)[0][0], get_pieces(c // 4)[0][1], (c % 4) * 128), fpool, lambda c: xcb4[c % 4], "dve")
            gys = {}
            xcb4 = [(xcbs[0], ("xcb", 0)), (xcbs[1], ("xcb", 1)), (gstat[0], ("gstat", 0)), (gstat[1], ("gstat", 1))]

            def front(p):
                chunks = [2 * p, 2 * p + 1]
                (wx, wxk), (wy, wyk) = get_pieces(chunks[0] // 4)
                L.s1(chunks)
                for c in chunks:
                    cl = c % 4
                    by = nextbank()

                    def mmy(e, by=by, wy=wy, cl=cl):
                        for k in range(16):
                            m = e.matmul(pb[by][:], lhsT=wy[:, k, cl * 128:(cl + 1) * 128], rhs=xnT[:, k, HAL:HAL + TT], start=(k == 0), stop=(k == 15))
                        return m
                    S.op("pe", mmy, reads=[wyk] + allx, writes=[("pb", by)])
                    GY = fpool.get()
                    gys[c] = GY
                    S.op("act", lambda e, by=by, gy=GY[0]: e.activation(out=gy[:, 0:TT], in_=pb[by][:], func=AF.Gelu_apprx_tanh),
                         reads=[("pb", by)], writes=[GY[1]])

            def back(p):
                chunks = [2 * p, 2 * p + 1]
                L.s2(chunks)
                L.s3(chunks)
                for c in chunks:
                    H = L.h(c)
                    GY = gys[c]
                    S.op("dve", lambda e, h=H[0], c=c: e.tensor_copy(out=hst[:, c:c + 1], in_=h[:, TT - 1:TT]), reads=[H[1]], writes=[("hst", c)])
                    S.op("dve", lambda e, h=H[0], gy=GY[0], c=c: e.tensor_tensor(out=oT[:, c, HAL:HAL + TT], in0=h[:, 0:TT], in1=gy[:, 0:TT], op=ALU.mult),
                         reads=[H[1], GY[1]], writes=[("oT", c)])
                    fpool.put(H)
                    fpool.put(GY)
            front(0)
            for p in range(4):
                if p + 1 < 4:
                    front(p + 1)
                back(p)
            if STOP <= 2:
                return store_R(ft)
            cpieces = {}

            def get_cp(g):
                if g not in cpieces:
                    cpieces[g] = (load_piece((win_v[:, :, 2048 + g * 512:2048 + (g + 1) * 512], None), 16, 512),
                                  load_piece((win_v[:, :, 3072 + g * 512:3072 + (g + 1) * 512], None), 16, 512))
                return cpieces[g]

            def c_front(c):
                (wv, wvk), (wg, wgk) = get_cp(c // 4)
                cl = c % 4
                gk = ("oT", 8 + c)
                ds = c % 2
                S.dma("pool", lambda e: e.dma_start(out=dgb[ds][:], in_=dg_d[c].rearrange("p (k n) -> p k n", k=31), max_dma_last_dim=4096),
                      f"dg{ds}", writes=[("dgb", ds)])
                bgm = nextbank()

                def mmg(e):
                    for k in range(16):
                        m = e.matmul(pb[bgm][:], lhsT=wg[:, k, cl * 128:(cl + 1) * 128], rhs=xnT[:, k, HAL:HAL + TT], start=(k == 0), stop=(k == 15))
                    return m
                S.op("pe", mmg, reads=[wgk] + allx, writes=[("pb", bgm)])
                bgh = nextbank()

                def mmgh(e):
                    for k in range(16):
                        m = e.matmul(pb[bgh][:, 0:HAL], lhsT=wg[:, k, cl * 128:(cl + 1) * 128], rhs=xnT[:, k, 0:HAL], start=(k == 0), stop=(k == 15))
                    for k in range(16):
                        m = e.matmul(pb[bgh][:, 64:64 + HAL], lhsT=wv[:, k, cl * 128:(cl + 1) * 128], rhs=xnT[:, k, 0:HAL],
                                     start=(k == 0), stop=(k == 15), skip_group_check=True)
                    return m
                S.op("pe", mmgh, reads=[wgk, wvk] + allxh, writes=[("pb", bgh)])
                isg = nexttmp()
                sg = tmp[isg]
                S.op("act", lambda e: e.activation(out=sg[:, HAL:HAL + TT], in_=pb[bgm][:], func=AF.Sigmoid), reads=[("pb", bgm)], writes=[("tmp", isg)])
                S.op("act", lambda e: e.activation(out=sg[:, 0:HAL], in_=pb[bgh][:, 0:HAL], func=AF.Sigmoid), reads=[("pb", bgh), ("tmp", isg)], writes=[("tmp", isg)])
                bvm = nextbank()

                def mmv(e):
                    for k in range(16):
                        m = e.matmul(pb[bvm][:], lhsT=wv[:, k, cl * 128:(cl + 1) * 128], rhs=xnT[:, k, HAL:HAL + TT], start=(k == 0), stop=(k == 15))
                    return m
                S.op("pe", mmv, reads=[wvk] + allx, writes=[("pb", bvm)])
                S.op("dve", lambda e: e.tensor_tensor(out=oT[:, 8 + c, HAL:HAL + TT], in0=pb[bvm][:], in1=sg[:, HAL:HAL + TT], op=ALU.mult),
                     reads=[("pb", bvm), ("tmp", isg)], writes=[gk])
                S.op("dve", lambda e: e.tensor_tensor(out=oT[:, 8 + c, 0:HAL], in0=pb[bgh][:, 64:64 + HAL], in1=sg[:, 0:HAL], op=ALU.mult),
                     reads=[("pb", bgh), ("tmp", isg), gk], writes=[gk])

            def c_back(c):
                gk = ("oT", 8 + c)
                ds = c % 2
                bc = nextbank()

                def mmc(e):
                    for k in range(31):
                        m = e.matmul(pb[bc][:], lhsT=dgb[ds][:, k, :], rhs=oT[:, 8 + c, HAL - 30 + k:HAL - 30 + k + TT], start=(k == 0), stop=(k == 30))
                    return m
                S.op("pe", mmc, reads=[("dgb", ds), gk], writes=[("pb", bc)])
                if c > 0:
                    stat_mm(c - 1)
                bcol = cv[:, CV_DWB + c:CV_DWB + c + 1]
                gs = 0
                S.op("act", lambda e: e.activation(out=gc[:, c, :], in_=pb[bc][:], func=AF.Identity, bias=bcol), reads=[("pb", bc), "cv"], writes=[("gc", c)])
                S.op("act", lambda e: e.activation(out=gstat[gs][:], in_=pb[bc][:], func=AF.Identity, bias=bcol), reads=[("pb", bc), "cv"], writes=[("gstat", gs)])
                S.op("act", lambda e: e.activation(out=gstat[gs + 1][:], in_=pb[bc][:], func=AF.Square, bias=bcol), reads=[("pb", bc), "cv"], writes=[("gstat", gs + 1)])

            def stat_mm(c):
                gs = 0
                S.op("pe", lambda e: e.matmul(pstat[0][:], lhsT=ones[:], rhs=gstat[gs][:], start=(c == 0), stop=(c == NCH - 1)),
                     reads=["ones", ("gstat", gs)], writes=[("pstat", 0)])
                S.op("pe", lambda e: e.matmul(pstat[1][:], lhsT=ones[:], rhs=gstat[gs + 1][:], start=(c == 0), stop=(c == NCH - 1)),
                     reads=["ones", ("gstat", gs + 1)], writes=[("pstat", 1)])
            c_front(0)
            for c in range(NCH):
                if c + 1 < NCH:
                    c_front(c + 1)
                c_back(c)
            stat_mm(NCH - 1)
            if STOP <= 3:
                return store_R(ft)
            mean, msq, rs = lnt
            nmr = msq
            S.op("act", lambda e: e.activation(out=mean[:], in_=pstat[0][:], func=AF.Copy, scale=1.0 / 1024), reads=[("pstat", 0)], writes=[("lnt", 0)])
            S.op("act", lambda e: e.activation(out=msq[:], in_=pstat[0][:], func=AF.Square, scale=1.0 / 1024), reads=[("pstat", 0)], writes=[("lnt", 1)])
            S.op("dve", lambda e: e.scalar_tensor_tensor(out=rs[:], in0=pstat[1][:], scalar=1.0 / 1024, in1=msq[:], op0=ALU.mult, op1=ALU.subtract),
                 reads=[("pstat", 1), ("lnt", 1)], writes=[("lnt", 2)])
            S.op("act", lambda e: e.activation(out=rs[:], in_=rs[:], func=AF.Sqrt, bias=EPS, scale=1.0), reads=[("lnt", 2)], writes=[("lnt", 2)])
            S.op("dve", lambda e: e.reciprocal(out=rs[:], in_=rs[:]), reads=[("lnt", 2)], writes=[("lnt", 2)])
            S.op("dve", lambda e: e.scalar_tensor_tensor(out=nmr[:], in0=mean[:], scalar=-1.0, in1=rs[:], op0=ALU.mult, op1=ALU.mult),
                 reads=[("lnt", 0), ("lnt", 2), ("lnt", 1)], writes=[("lnt", 1)])
            for c in range(NCH):
                it = nexttmp()
                t_ = tmp[it]
                S.op("dve", lambda e, c=c, t_=t_: e.tensor_tensor(out=t_[:, 0:TT], in0=gc[:, c, :], in1=rs[:], op=ALU.mult),
                     reads=[("gc", c), ("lnt", 2)], writes=[("tmp", it)])
                S.op("dve", lambda e, t_=t_: e.tensor_tensor(out=t_[:, 0:TT], in0=t_[:, 0:TT], in1=nmr[:], op=ALU.add),
                     reads=[("tmp", it), ("lnt", 1)], writes=[("tmp", it)])
                S.op("act", lambda e, c=c, t_=t_: e.activation(out=oT[:, 8 + c, HAL:HAL + TT], in_=t_[:, 0:TT], func=AF.Silu,
                                                               scale=cv[:, CV_LNG + c:CV_LNG + c + 1], bias=cv[:, CV_LNB + c:CV_LNB + c + 1]),
                     reads=[("tmp", it), "cv"], writes=[("oT", 8 + c)])
            if STOP <= 4:
                return store_R(ft)
            for j in range(4):
                wo, wok = load_piece((wout_v[:, :, j * 512:(j + 1) * 512], None), 16, 512)
                for s in range(4):
                    b = nextbank()

                    def mmo(e, b=b, s=s, wo=wo):
                        for k in range(16):
                            m = e.matmul(pb[b][:], lhsT=oT[:, k, HAL + s * 128:HAL + (s + 1) * 128], rhs=wo[:, k, :], start=(k == 0), stop=(k == 15))
                        return m
                    S.op("pe", mmo, reads=[wok] + [("oT", k) for k in range(16)], writes=[("pb", b)])
                    S.op("dve", lambda e, b=b, s=s, j=j: e.tensor_tensor(out=R[s][:, j * 512:(j + 1) * 512], in0=pb[b][:],
                                                                         in1=R[s][:, j * 512:(j + 1) * 512], op=ALU.add),
                         reads=[("pb", b), ("R", s, j)], writes=[("R", s, j)])
            if STOP <= 5:
                return store_R(ft)
            norm_tile([(R[s], rkeys(s)) for s in range(4)], 1)
            def w1_phase(grp):
                a0 = (grp % 4) * 4
                w1p, w1k = load_piece((w1b_d[grp], ("w1b", grp)) if PRECAST(grp) else (w1_v[:, :, grp * 512:(grp + 1) * 512], None), 16, 512)
                for cl in range(4):
                    b = nextbank()

                    def mm1(e, b=b, w1p=w1p, cl=cl):
                        for k in range(16):
                            m = e.matmul(pb[b][:], lhsT=w1p[:, k, cl * 128:(cl + 1) * 128], rhs=xnT[:, k, HAL:HAL + TT], start=(k == 0), stop=(k == 15))
                        return m
                    S.op("pe", mm1, reads=[w1k] + allx, writes=[("pb", b)])
                    rs_ = cl % 2
                    S.op("act", lambda e, b=b, rs_=rs_: e.activation(out=rl[rs_][:], in_=pb[b][:], func=AF.Relu), reads=[("pb", b)], writes=[("rl", rs_)])
                    S.op("dve", lambda e, b=b, rs_=rs_, a0=a0, cl=cl: e.scalar_tensor_tensor(
                        out=oT[:, a0 + cl, HAL:HAL + TT], in0=pb[b][:], scalar=0.0, in1=rl[rs_][:], op0=ALU.max, op1=ALU.mult),
                        reads=[("pb", b), ("rl", rs_)], writes=[("oT", a0 + cl)])

            def w2_phase(grp):
                a0 = (grp % 4) * 4
                w2p, w2k = load_piece((w2b_d[grp], ("w2b", grp)) if PRECAST(grp) else (w2_v[:, grp * 4:(grp + 1) * 4, :], None), 4, D)
                for s in range(4):
                    for j in range(4):
                        b = nextbank()

                        def mm2(e, b=b, s=s, j=j, a0=a0, w2p=w2p):
                            for fl in range(4):
                                m = e.matmul(pb[b][:], lhsT=oT[:, a0 + fl, HAL + s * 128:HAL + (s + 1) * 128], rhs=w2p[:, fl, j * 512:(j + 1) * 512],
                                             start=(fl == 0), stop=(fl == 3))
                            return m
                        S.op("pe", mm2, reads=[w2k] + [("oT", a0 + fl) for fl in range(4)], writes=[("pb", b)])
                        S.op("dve", lambda e, b=b, s=s, j=j: e.tensor_tensor(out=R[s][:, j * 512:(j + 1) * 512], in0=pb[b][:],
                                                                             in1=R[s][:, j * 512:(j + 1) * 512], op=ALU.add),
                             reads=[("pb", b), ("R", s, j)], writes=[("R", s, j)])
            NG = 16
            w1_phase(0)
            for grp in range(NG):
                if grp + 1 < NG:
                    w1_phase(grp + 1)
                w2_phase(grp)
            if STOP <= 6:
                return store_R(ft)
            for s in range(4):
                n = nrm_ctr[0]
                nrm_ctr[0] += 1
                slot = n % 2
                col = n % 8
                S.op("act", lambda e, s=s, slot=slot, col=col: e.activation(out=xsb[slot][:], in_=R[s][:], func=AF.Square, accum_out=ssq[:, col:col + 1]),
                     reads=rkeys(s), writes=[("xsb", slot), ("ssq", col)])
                S.op("act", lambda e, col=col: e.activation(out=rstd[:, col:col + 1], in_=ssq[:, col:col + 1], func=AF.Sqrt, scale=1.0 / D, bias=EPS),
                     reads=[("ssq", col)], writes=[("rstd", col)])
                S.op("dve", lambda e, col=col: e.reciprocal(out=rstd[:, col:col + 1], in_=rstd[:, col:col + 1]), reads=[("rstd", col)], writes=[("rstd", col)])
                S.op("dve", lambda e, s=s, col=col: e.scalar_tensor_tensor(out=R[s][:], in0=R[s][:], scalar=rstd[:, col:col + 1], in1=gng[:, 2, :],
                                                                           op0=ALU.mult, op1=ALU.mult),
                     reads=rkeys(s) + [("rstd", col), "gng"], writes=rkeys(s))
                S.dma("sp", lambda e, s=s: e.dma_start(out=out_d[ft * TT + s * 128: ft * TT + (s + 1) * 128, :], in_=R[s][:]),
                      f"R{s}", reads=rkeys(s), writes=[("out", ft, s)])

        if NPRE > 0:
            prefix_all()
            emit_casts(len(cast_jobs))
            S.barrier()
        for ft in range(NFULL):
            full_tile(ft)
        outkeys = [("out", ft, s) for ft in range(NFULL) for s in range(4)]
        S.op("sp", lambda e: e.nop(), reads=outkeys)

        chans = sorted(set(o["chan"] for o in S.ops if o["kind"] == "dma"))
        sems = {e: es.enter_context(nc.semaphore(f"s_{e}")) for e in ["pe", "act", "dve", "pool", "sp"]}
        chan_sems = {c: es.enter_context(nc.semaphore(f"c_{c}")) for c in chans}
        block = es.enter_context(nc.Block())
        S.emit(block, sems, chan_sems)
        print("ops", len(S.ops), "sbuf bytes", off[0], "counts", {k: v for k, v in S.final_counts.items()})
    return nc


_CACHE = {}


def _host_inputs(inp, NPRE=12, NFULL=4, seq=8192, batch=2, chunks=4):
    f = np.float32
    x = np.asarray(inp["x"], f)
    cvs = []
    per = NFULL * TT

    def pc(v):
        return np.ascontiguousarray(np.asarray(v, f).reshape(NCH, 128).T)
    base = np.zeros((128, CV_N), f)
    base[:, CV_CB:CV_CB + 8] = pc(inp["lru_conv_b"][0])
    base[:, CV_GAB:CV_GAB + 8] = pc(inp["lru_gate_a_b"][0])
    base[:, CV_GXB:CV_GXB + 8] = pc(inp["lru_gate_x_b"][0])
    base[:, CV_LAM:CV_LAM + 8] = pc(inp["lru_lambda"][0])
    base[:, CV_DWB:CV_DWB + 8] = pc(inp["conf_dw_b"][0])
    base[:, CV_LNG:CV_LNG + 8] = pc(inp["conf_ln_g"][0])
    base[:, CV_LNB:CV_LNB + 8] = pc(inp["conf_ln_b"][0])
    c4 = np.asarray(inp["lru_conv_w"][0], f)
    base[:, CV_C4W:CV_C4W + 32] = c4.reshape(4, NCH, 128).transpose(2, 1, 0).reshape(128, 32)
    gng = np.stack([np.broadcast_to(np.asarray(inp["mix_norm_g"][0], f), (128, D)),
                    np.broadcast_to(np.asarray(inp["mlp_norm_g"][0], f), (128, D)),
                    np.broadcast_to(np.asarray(inp["final_norm_g"], f), (128, D))], axis=1)
    gng = np.ascontiguousarray(gng)
    gw = np.zeros((128, 16, 128), f)
    for gi, name in enumerate(["lru_gate_a_w", "lru_gate_x_w"]):
        w = np.asarray(inp[name][0], f)
        for c in range(NCH):
            for hh in range(2):
                gw[hh * 64:(hh + 1) * 64, gi * 8 + c, hh * 64:(hh + 1) * 64] = w[c * 2 + hh]
    dw = np.asarray(inp["conf_dw_w"][0], f)
    dg = np.zeros((NCH, 128, 31, 128), f)
    ar = np.arange(128)
    for c in range(NCH):
        dg[c, ar, :, ar] = dw[:, c * 128:(c + 1) * 128].T
    dg = dg.reshape(NCH, 128, 31 * 128)
    dg4 = np.zeros((128, 32, 128), f)
    for c in range(NCH):
        for k in range(4):
            dg4[ar, c * 4 + k, ar] = c4[k, c * 128:(c + 1) * 128]
    shared = dict(gng=gng, gw=gw, dg=dg, dg4=dg4,
                  w_in=np.ascontiguousarray(np.asarray(inp["w_in"][0], f)),
                  w_out=np.ascontiguousarray(np.asarray(inp["w_out"][0], f)),
                  w1=np.ascontiguousarray(np.asarray(inp["mlp_w1"][0], f)),
                  w2=np.ascontiguousarray(np.asarray(inp["mlp_w2"][0], f)))
    maps = []
    for core in range(batch * chunks):
        b, q = divmod(core, chunks)
        npad = (NPRE * TT) - q * per
        xs = np.zeros(((NPRE + NFULL) * TT, D), f)
        xs[npad:] = x[b, 0:(q + 1) * per]
        cvc = base.copy()
        for pt in range(NPRE):
            cvc[:, CV_FLAG + pt] = 1.0 if pt * TT >= npad else 0.0
        m = dict(shared)
        m["xs"] = xs
        m["cv"] = cvc
        maps.append(m)
    return maps


def kernel(**inputs):
    NPRE, NFULL = 12, 4
    key = (NPRE, NFULL)
    if key not in _CACHE:
        _CACHE[key] = build_program(NPRE, NFULL)
    nc = _CACHE[key]
    maps = _host_inputs(inputs, NPRE, NFULL)
    res = run_bass_kernel_spmd(nc, maps, core_ids=list(range(8)))
    out = np.zeros((2, 8192, D), np.float32)
    for core in range(8):
        b, q = divmod(core, 4)
        out[b, q * 2048:(q + 1) * 2048] = res.results[core]["out"]
    return out
```
